# Optimizing a Trainium2 kernel written in Bass

```python
import math
import jax, jax.numpy as jnp
from jax import lax
import numpy as np

D_MODEL = 1024
BATCH = 8
SEQ = 4096
DEPTH = 1

N_META = 16
CONV_WIDTH = 1024
CONV_GROUPS = 16
CONV_K = 3
N_DIFF_HEADS = 8
DIFF_HEAD_DIM = 64
ATTN_QK_WIDTH = N_DIFF_HEADS * 2 * DIFF_HEAD_DIM
ATTN_V_WIDTH = N_DIFF_HEADS * 2 * DIFF_HEAD_DIM
IN_PROJ_WIDTH = 3 * CONV_WIDTH + 2 * ATTN_QK_WIDTH + ATTN_V_WIDTH + 2 * D_MODEL
D_FF = 2816
FFN_CONV_K = 3
Q_BLOCK = 128
RMS_EPS = 1e-6
NEG_INF = -1e30

kernel_name = "hybrid_shortconv_diffattn_gated_merge"


def rmsnorm(x, g):
    xf = x.astype(jnp.float32)
    y = xf * lax.rsqrt(jnp.mean(xf * xf, axis=-1, keepdims=True) + RMS_EPS)
    return (y * g.astype(jnp.float32)).astype(x.dtype)


def causal_dwconv(x, w, b=None):
    K = w.shape[0]
    L = x.shape[1]
    xp = jnp.pad(x, ((0, 0), (K - 1, 0), (0, 0)))
    y = xp[:, 0:L] * w[0]
    for k in range(1, K):
        y = y + xp[:, k:k + L] * w[k]
    if b is not None:
        y = y + b
    return y


def alibi_slopes(n_heads):
    return 2.0 ** (-8.0 * jnp.arange(1, n_heads + 1, dtype=jnp.float32) / n_heads)


def diff_attention(q, k, v, lam, lam_init, subln_g):
    B, L, H, _, dh = q.shape
    Lp = -(-L // Q_BLOCK) * Q_BLOCK
    q = jnp.pad(q, ((0, 0), (0, Lp - L), (0, 0), (0, 0), (0, 0)))
    k = jnp.pad(k, ((0, 0), (0, Lp - L), (0, 0), (0, 0), (0, 0)))
    v = jnp.pad(v, ((0, 0), (0, Lp - L), (0, 0), (0, 0)))
    nb = Lp // Q_BLOCK
    qb = q.reshape(B, nb, Q_BLOCK, H, 2, dh).transpose(1, 0, 2, 3, 4, 5)
    slopes = alibi_slopes(H)
    k_pos = jnp.arange(Lp)
    scale = dh ** -0.5

    def block(args):
        qi, i = args
        q_pos = i * Q_BLOCK + jnp.arange(Q_BLOCK)
        dist = (q_pos[:, None] - k_pos[None, :]).astype(jnp.float32)
        bias = -slopes[:, None, None] * dist
        causal = k_pos[None, :] <= q_pos[:, None]
        s = jnp.einsum('bqhmd,bkhmd->bhmqk', qi, k).astype(jnp.float32) * scale
        s = jnp.where(causal, s + bias[None, :, None], NEG_INF)
        p = jax.nn.softmax(s, axis=-1)
        w = p[:, :, 0] - lam * p[:, :, 1]
        return jnp.einsum('bhqk,bkhe->bqhe', w.astype(v.dtype), v)

    o = lax.map(block, (qb, jnp.arange(nb)))
    o = o.transpose(1, 0, 2, 3, 4).reshape(B, Lp, H, 2 * dh)[:, :L]
    o = rmsnorm(o, subln_g) * (1.0 - lam_init)
    return o.reshape(B, L, H * 2 * dh)


def setup_inputs(seed: int = 0) -> dict:
    key = jax.random.key(seed)
    ks = jax.random.split(key, 24)
    f32 = jnp.float32

    def nrm(k, shape, scale):
        return jax.random.normal(k, shape, f32) * scale

    def gain(k, shape):
        return 1.0 + 0.05 * jax.random.normal(k, shape, f32)

    L = DEPTH
    return {
        "x": nrm(ks[0], (BATCH, SEQ, D_MODEL), 1.0),
        "meta_tokens": nrm(ks[1], (N_META, D_MODEL), 1.0),
        "w_in": nrm(ks[2], (L, D_MODEL, IN_PROJ_WIDTH), D_MODEL ** -0.5),
        "conv_w": nrm(ks[3], (L, CONV_K, CONV_WIDTH), CONV_K ** -0.5),
        "w_conv_out": nrm(ks[4], (L, CONV_WIDTH, D_MODEL), CONV_WIDTH ** -0.5),
        "lambda_q1": nrm(ks[5], (L, DIFF_HEAD_DIM), 0.1),
        "lambda_k1": nrm(ks[6], (L, DIFF_HEAD_DIM), 0.1),
        "lambda_q2": nrm(ks[7], (L, DIFF_HEAD_DIM), 0.1),
        "lambda_k2": nrm(ks[8], (L, DIFF_HEAD_DIM), 0.1),
        "subln_g": gain(ks[9], (L, 2 * DIFF_HEAD_DIM)),
        "w_attn_out": nrm(ks[10], (L, ATTN_V_WIDTH, D_MODEL), ATTN_V_WIDTH ** -0.5),
        "w_mix_out": nrm(ks[11], (L, D_MODEL, D_MODEL), D_MODEL ** -0.5),
        "norm_mix_pre": gain(ks[12], (L, D_MODEL)),
        "norm_mix_post": gain(ks[13], (L, D_MODEL)),
        "w_ffn_up": nrm(ks[14], (L, D_MODEL, 2 * D_FF), D_MODEL ** -0.5),
        "ffn_conv_w": nrm(ks[15], (L, FFN_CONV_K, 2 * D_FF), FFN_CONV_K ** -0.5),
        "ffn_conv_b": nrm(ks[16], (L, 2 * D_FF), 0.01),
        "w_ffn_down": nrm(ks[17], (L, D_FF, D_MODEL), D_FF ** -0.5),
        "norm_ffn_pre": gain(ks[18], (L, D_MODEL)),
        "norm_ffn_post": gain(ks[19], (L, D_MODEL)),
    }


def reference(x, meta_tokens, w_in, conv_w, w_conv_out, lambda_q1, lambda_k1, lambda_q2, lambda_k2,
              subln_g, w_attn_out, w_mix_out, norm_mix_pre, norm_mix_post, w_ffn_up, ffn_conv_w,
              ffn_conv_b, w_ffn_down, norm_ffn_pre, norm_ffn_post):
    B = x.shape[0]
    h = jnp.concatenate(
        [jnp.broadcast_to(meta_tokens[None].astype(x.dtype), (B, N_META, D_MODEL)), x], axis=1)
    Ltot = h.shape[1]
    sizes = [CONV_WIDTH] * 3 + [ATTN_QK_WIDTH] * 2 + [ATTN_V_WIDTH] + [D_MODEL] * 2
    split_idx = np.cumsum(sizes)[:-1].tolist()
    for l in range(DEPTH):
        lam_init = 0.8 - 0.6 * math.exp(-0.3 * l)
        a = rmsnorm(h, norm_mix_pre[l])
        proj = a @ w_in[l]
        cb, cc, cx, q, k, v, ga, gb = jnp.split(proj, split_idx, axis=-1)
        ya = (cb * causal_dwconv(cc * cx, conv_w[l])) @ w_conv_out[l]
        lam = (jnp.exp(jnp.sum(lambda_q1[l].astype(jnp.float32) * lambda_k1[l].astype(jnp.float32)))
               - jnp.exp(jnp.sum(lambda_q2[l].astype(jnp.float32) * lambda_k2[l].astype(jnp.float32)))
               + lam_init)
        q = q.reshape(B, Ltot, N_DIFF_HEADS, 2, DIFF_HEAD_DIM)
        k = k.reshape(B, Ltot, N_DIFF_HEADS, 2, DIFF_HEAD_DIM)
        v = v.reshape(B, Ltot, N_DIFF_HEADS, 2 * DIFF_HEAD_DIM)
        yb = diff_attention(q, k, v, lam, lam_init, subln_g[l]) @ w_attn_out[l]
        mix = (jax.nn.sigmoid(ga) * ya + jax.nn.sigmoid(gb) * yb) @ w_mix_out[l]
        h = h + rmsnorm(mix, norm_mix_post[l])
        f = rmsnorm(h, norm_ffn_pre[l])
        u = causal_dwconv(f @ w_ffn_up[l], ffn_conv_w[l], ffn_conv_b[l])
        g, u = jnp.split(u, 2, axis=-1)
        y = (jax.nn.silu(g) * u) @ w_ffn_down[l]
        h = h + rmsnorm(y, norm_ffn_post[l])
    return h[:, N_META:]
```

```python
import numpy as np
import ml_dtypes
import concourse.bass as bass
import concourse.mybir as mybir
from concourse.bass_utils import run_bass_kernel_spmd
from contextlib import ExitStack

F32 = mybir.dt.float32
BF16 = mybir.dt.bfloat16
AF = mybir.ActivationFunctionType
ALU = mybir.AluOpType
AX = mybir.AxisListType

NT = 4112
NJ = 9
NKB = 33
EPS = 1e-6
LAM_INIT = 0.8 - 0.6 * 1.0
DEBUG = False
FAST_RECIP = False
TASK_INTERLEAVE = True
PSWAP = 0
STOP_AFTER = 99
SUB = 99
EXTRA = 0

C_G1, C_GPOST, C_GFPRE, C_GFPOST = 0, 8, 16, 24
C_CW = 32
C_FCW = 56
C_FCB = 188
C_SUBG = 232
C_LQ1, C_LK1, C_LQ2, C_LK2 = 233, 297, 361, 425
NSMALL = 489


def trng(j):
    t0 = 512 * j
    return t0, min(512, NT - t0)


def RECIP(e, out, in_):
    if FAST_RECIP:
        return e.reciprocal_approx_fast(out=out, in_=in_)
    return e.reciprocal(out=out, in_=in_)


class Buf:
    __slots__ = ("w", "r")

    def __init__(self, after=None):
        self.w = None
        self.r = dict(after) if after else {}


def merged(bufs):
    d = {}
    for b in bufs:
        if b.w is not None:
            s, v = b.w
            if v > d.get(s, 0):
                d[s] = v
        for s, v in b.r.items():
            if v > d.get(s, 0):
                d[s] = v
    return d


class Queue:
    def __init__(self, name, sem):
        self.name = name
        self.sem = sem
        self.n = 0
        self.ops = []
        self.waited = {}


class DSem:
    def __init__(self, key):
        self.key = key
        self.count = 0


class Prog:
    def __init__(self, nc, st):
        self.nc = nc
        self.sems = {}
        self.q = {}
        for name in ("pe", "act", "dve", "pool", "sp"):
            key = "s_" + name
            self.sems[key] = st.enter_context(nc.semaphore(key))
            self.q[name] = Queue(name, key)
        self.dring = {}
        self.dri = {}
        for qn, n in (("sp", 24), ("pool", 16)):
            ring = []
            for i in range(n):
                key = f"d_{qn}{i}"
                self.sems[key] = st.enter_context(nc.semaphore(key))
                ring.append(DSem(key))
            self.dring[qn] = ring
            self.dri[qn] = 0

    def _collect(self, qn, reads, writes):
        q = self.q[qn]
        waits = {}

        def need(s, v, raw):
            if s == q.sem and (qn == "pe" or not raw):
                return
            if v > waits.get(s, 0):
                waits[s] = v

        for b in reads:
            if b.w is not None:
                need(b.w[0], b.w[1], True)
        for b in writes:
            if b.w is not None:
                need(b.w[0], b.w[1], False)
            for s, v in b.r.items():
                need(s, v, False)
        out = []
        for s, v in waits.items():
            if v > q.waited.get(s, 0):
                q.waited[s] = v
                out.append((s, v))
        return out

    def _commit(self, tok, reads, writes):
        s, v = tok
        for b in reads:
            if v > b.r.get(s, 0):
                b.r[s] = v
        for b in writes:
            b.w = tok
            b.r = {}

    def op(self, qn, fn, reads=(), writes=()):
        q = self.q[qn]
        waits = self._collect(qn, reads, writes)
        q.n += 1
        tok = (q.sem, q.n)
        q.ops.append((waits, fn, tok, 1))
        self._commit(tok, reads, writes)
        return tok

    def mm(self, out_ap, pairs, reads, writes):
        q = self.q["pe"]
        waits = self._collect("pe", reads, writes)
        n = len(pairs)
        tok = None
        for i, (l, r) in enumerate(pairs):
            def fn(e, l=l, r=r, i=i):
                return e.matmul(out_ap, lhsT=l, rhs=r, start=(i == 0), stop=(i == n - 1))
            if i == n - 1:
                q.n += 1
                tok = (q.sem, q.n)
            q.ops.append((waits if i == 0 else [], fn, tok if i == n - 1 else None, 1))
        self._commit(tok, reads, writes)
        return tok

    def dma(self, qn, out, in_, reads=(), writes=()):
        q = self.q[qn]
        ring = self.dring[qn]
        ent = ring[self.dri[qn] % len(ring)]
        self.dri[qn] += 1
        waits = self._collect(qn, reads, writes)
        if ent.count > 0 and 16 * ent.count > q.waited.get(ent.key, 0):
            q.waited[ent.key] = 16 * ent.count
            waits.append((ent.key, 16 * ent.count))
        ent.count += 1
        tok = (ent.key, 16 * ent.count)
        q.ops.append((waits, lambda e: e.dma_start(out=out, in_=in_), tok, 16))
        self._commit(tok, reads, writes)
        return tok

    def final_wait(self, qn, toks):
        q = self.q[qn]
        waits = []
        for s, v in toks.items():
            if v > q.waited.get(s, 0):
                q.waited[s] = v
                waits.append((s, v))
        q.ops.append((waits, None, None, 0))

    def final_all(self):
        toks = {}
        for q in self.q.values():
            if q.n > 0:
                toks[q.sem] = q.n
        for ring in self.dring.values():
            for ent in ring:
                if ent.count > 0:
                    toks[ent.key] = 16 * ent.count
        self.final_wait("sp", toks)

    def replay(self, qn, e):
        for waits, fn, tok, inc in self.q[qn].ops:
            for s, v in waits:
                e.wait_ge(self.sems[s], v)
            if fn is None:
                continue
            ins = fn(e)
            if tok is not None:
                ins.then_inc(self.sems[tok[0]], inc)


def build_nc():
    nc = bass.Bass("TRN2", target_bir_lowering=False)

    def din(name, shape, dt=F32):
        return nc.dram_tensor(name, list(shape), dt, kind="ExternalInput").ap()

    hT0 = din("hT0", [1024, NT])
    w_in = din("w_in", [1024, 8192])
    w_co = din("w_conv_out", [1024, 1024])
    w_ao = din("w_attn_out", [1024, 1024])
    w_mix = din("w_mix_out", [1024, 1024])
    w_up = din("w_ffn_up", [1024, 5632])
    w_dn = din("w_ffn_down", [2816, 1024])
    small_d = din("small", [128, NSMALL])
    ident_d = din("ident", [128, 128])
    augq_d = din("augq", [8, 4, NT], BF16)
    augk_d = din("augk", [8, 4, NT], BF16)
    masks_d = din("masks", [128, 4 * 512])
    outT = nc.dram_tensor("outT", [1024, 4096], F32, kind="ExternalOutput").ap()
    skind = "ExternalOutput" if DEBUG else "Internal"
    m1_scr = nc.dram_tensor("m1_scr", [8, 128, NT], F32, kind=skind).ap()
    sgb_scr = nc.dram_tensor("sgb_scr", [8, 128, NT], F32, kind=skind).ap()
    o_scr = nc.dram_tensor("o_scr", [8, 128, NT], BF16, kind=skind).ap()
    h1_scr = nc.dram_tensor("h1_scr", [NJ, 128, 8 * 512], F32, kind=skind).ap()
    hid_scr = nc.dram_tensor("hid_scr", [NJ, 128, 22 * 512], BF16, kind=skind).ap()
    if DEBUG:
        dbgA = nc.dram_tensor("dbgA", [128, 8 * NT], BF16, kind="ExternalOutput").ap()
        dbgB = nc.dram_tensor("dbgB", [128, 8 * NT], BF16, kind="ExternalOutput").ap()
        dbgC = nc.dram_tensor("dbgC", [128, 8 * NT], BF16, kind="ExternalOutput").ap()

    hT0_v = hT0.rearrange("(c p) t -> p c t", p=128)
    outT_v = outT.rearrange("(c p) t -> p c t", p=128)

    def wsrc(w, kc0, kc, col0):
        return w.rearrange("(k p) f -> p k f", p=128)[:, kc0:kc0 + kc, col0:col0 + 128]

    off = [0]

    def A(n):
        o = off[0]
        off[0] += (n + 31) // 32 * 32
        return o

    SMALL = A(NSMALL * 4)
    IDENT = A(128 * 4)
    ONES = A(128 * 2)
    MISC = A(64 * 4)
    IDENTB = A(128 * 2)
    BIGA = A(8 * NT * 2)
    BIGB = A(8 * NT * 2)
    WST = A(4 * 4096)
    WBF = A(8 * 2048)
    TB = A(44 * 1024)
    TOTAL = off[0]

    st = ExitStack()
    with st:
        pool_t = st.enter_context(nc.sbuf_tensor("pool", [128, TOTAL // 2], BF16))
        psb = [st.enter_context(nc.psum_tensor(f"ps{i}", [128, 512], F32)) for i in range(8)]
        P = Prog(nc, st)
        block = st.enter_context(nc.Block())

        def V(offb, dt, n, p0=0, p1=128):
            e0 = offb // 2
            if dt == BF16:
                return pool_t[p0:p1, e0:e0 + n]
            return pool_t[p0:p1, e0:e0 + 2 * n].bitcast(F32)

        pbuf = [Buf() for _ in range(8)]

        def PS(i, w=512, p0=0, p1=128, c0=0):
            return psb[i][p0:p1, c0:c0 + w]

        small = V(SMALL, F32, NSMALL)
        small_b = Buf()
        ident = V(IDENT, F32, 128)
        ident_b = Buf()
        ones = V(ONES, BF16, 128)
        ones_b = Buf()
        misc = V(MISC, F32, 64)
        misc_b = Buf()

        def sc(col, p0=0, p1=128):
            return small[p0:p1, col:col + 1]

        def mcol(col):
            return misc[:, col:col + 1]

        class StopBuild(Exception):
            pass

        def check_sub(k):
            if STOP_AFTER == 5 and k > SUB:
                raise StopBuild()

        def check_stop(n):
            if n > STOP_AFTER:
                raise StopBuild()

        def body():
            P.dma("sp", small, small_d, writes=[small_b])
            P.dma("sp", ident, ident_d, writes=[ident_b])
            P.op("pool", lambda e: e.memset(ones, 1.0), writes=[ones_b])
            identb = V(IDENTB, BF16, 128)
            identb_b = Buf()
            P.op("pool", lambda e: e.tensor_copy(out=identb, in_=ident), reads=[ident_b], writes=[identb_b])
            lamtmp = V(TB, F32, 64)
            lamtmp_b = Buf()
            P.op("dve", lambda e: e.tensor_tensor(out=lamtmp, in0=small[:, C_LQ1:C_LQ1 + 64], in1=small[:, C_LK1:C_LK1 + 64], op=ALU.mult),
                 reads=[small_b], writes=[lamtmp_b])
            P.op("dve", lambda e: e.reduce_sum(out=mcol(0), in_=lamtmp, axis=AX.X), reads=[lamtmp_b], writes=[misc_b])
            P.op("dve", lambda e: e.tensor_tensor(out=lamtmp, in0=small[:, C_LQ2:C_LQ2 + 64], in1=small[:, C_LK2:C_LK2 + 64], op=ALU.mult),
                 reads=[small_b, misc_b], writes=[lamtmp_b])
            P.op("dve", lambda e: e.reduce_sum(out=mcol(1), in_=lamtmp, axis=AX.X), reads=[lamtmp_b], writes=[misc_b])
            P.op("act", lambda e: e.activation(out=misc[:, 0:2], in_=misc[:, 0:2], func=AF.Exp), reads=[misc_b], writes=[misc_b])
            P.op("dve", lambda e: e.tensor_tensor(out=mcol(2), in0=mcol(0), in1=mcol(1), op=ALU.subtract), reads=[misc_b], writes=[misc_b])
            P.op("dve", lambda e: e.tensor_scalar(out=mcol(3), in0=mcol(2), scalar1=LAM_INIT, scalar2=-1.0, op0=ALU.add, op1=ALU.mult),
                 reads=[misc_b], writes=[misc_b])
            P.op("dve", lambda e: e.tensor_scalar(out=mcol(4), in0=sc(C_SUBG), scalar1=1.0 - LAM_INIT, scalar2=None, op0=ALU.mult),
                 reads=[misc_b, small_b], writes=[misc_b])
            neglam = mcol(3)
            gsub = mcol(4)

            def bigv(base, c, t0, w, p0=0, p1=128):
                return V(base + (c * NT + t0) * 2, BF16, w, p0, p1)

            wst_b = [Buf() for _ in range(4)]
            wbf_b = [Buf() for _ in range(8)]
            wctr = [0, 0]

            def load_w(src, kc, dst, dst_b, eng="pool"):
                s = wctr[0] % 4
                wctr[0] += 1
                stg = V(WST + s * 4096, F32, kc * 128)
                stg3 = stg.rearrange("p (k f) -> p k f", k=kc)
                P.dma("sp", stg3, src, writes=[wst_b[s]])
                if eng == "pool":
                    P.op("pool", lambda e: e.tensor_copy(out=dst, in_=stg), reads=[wst_b[s]], writes=[dst_b])
                else:
                    P.op("act", lambda e: e.activation(out=dst, in_=stg, func=AF.Copy), reads=[wst_b[s]], writes=[dst_b])

            def ring_slot():
                s = wctr[1] % 8
                wctr[1] += 1
                return V(WBF + s * 2048, BF16, 1024), wbf_b[s]

            def wk(wv, k):
                return wv[:, k * 128:(k + 1) * 128]

            wplan = []
            wnext = [0]

            def plan_ring(srcs):
                unit = []
                views = []
                for src in srcs:
                    unit.append([src, 8, None, None])
                    views.append(None)
                wplan.append(unit)
                return len(wplan) - 1

            wviews = {}

            def prefetch(upto):
                while wnext[0] < min(upto, len(wplan)):
                    u = wnext[0]
                    vs = []
                    for ent in wplan[u]:
                        src, kc, dst, dstb = ent
                        if dst is None:
                            dst, dstb = ring_slot()
                        load_w(src, kc, dst, dstb)
                        vs.append((dst, dstb))
                    wviews[u] = vs
                    wnext[0] += 1

            U_P1 = [plan_ring([wsrc(w_in, 0, 8, c * 128), wsrc(w_in, 0, 8, 1024 + c * 128), wsrc(w_in, 0, 8, 2048 + c * 128)]) for c in range(8)]
            U_P2 = [plan_ring([wsrc(w_co, 0, 8, f * 128), wsrc(w_in, 0, 8, 6144 + f * 128), wsrc(w_in, 0, 8, 7168 + f * 128)]) for f in range(8)]
            U_P3 = [plan_ring([wsrc(w_in, 0, 8, 3072 + h * 128), wsrc(w_in, 0, 8, 4096 + h * 128), wsrc(w_in, 0, 8, 5120 + h * 128)]) for h in range(8)]
            U_P4 = [plan_ring([wsrc(w_ao, 0, 8, f * 128)]) for f in range(8)]
            U_P5 = plan_ring([wsrc(w_mix, 0, 8, f * 128) for f in range(8)])
            U_P6 = [plan_ring([wsrc(w_up, 0, 8, i * 128), wsrc(w_up, 0, 8, 2816 + i * 128)]) for i in range(22)]
            wdn_b = [Buf() for _ in range(8)]
            unit = []
            for fo in range(8):
                for (k0, kc) in ((0, 8), (8, 8), (16, 6)):
                    dst = V(BIGB + (fo * 22 + k0) * 128 * 2, BF16, kc * 128)
                    unit.append([wsrc(w_dn, k0, kc, fo * 128), kc, dst, wdn_b[fo]])
            wplan.append(unit)
            U_P7 = len(wplan) - 1

            def wdn(fo, k):
                return V(BIGB + (fo * 22 + k) * 128 * 2, BF16, 128)

            aT_b = [[Buf() for _ in range(NJ)] for _ in range(8)]
            zT_b = [[Buf() for _ in range(NJ)] for _ in range(8)]
            Tbufs = [lamtmp_b]

            def new_T(n):
                nonlocal Tbufs
                after = merged(Tbufs)
                bs = [Buf(after) for _ in range(n)]
                Tbufs = bs
                return bs

            check_stop(0)
            prefetch(2)
            tb = new_T(6)
            hin_b, sq_b, sqv_b, rstd_b = tb[0:2], tb[2], tb[3], tb[4]
            HIN = [TB, TB + 16384]
            SQ = TB + 32768
            SQV = TB + 40960
            RSTD = TB + 43008

            def norm_stats(sq_view_fn, nchunks, w, bank, sq_buf, scale, sqv, sqv_b_, rstd, rstd_b_):
                P.mm(PS(bank, w), [(ones, sq_view_fn(c)) for c in range(nchunks)], reads=[ones_b, sq_buf], writes=[pbuf[bank]])
                P.op("act", lambda e: e.activation(out=sqv, in_=PS(bank, w), func=AF.Ln, scale=scale, bias=epsb),
                     reads=[pbuf[bank], misc_b], writes=[sqv_b_])
                P.op("act", lambda e: e.activation(out=rstd, in_=sqv, func=AF.Exp, scale=-0.5), reads=[sqv_b_], writes=[rstd_b_])

            P.op("dve", lambda e: e.memset(mcol(5), EPS), writes=[misc_b])
            epsb = mcol(5)

            for j in range(NJ):
                t0, w = trng(j)
                hb = hin_b[j % 2]
                hin = V(HIN[j % 2], F32, 8 * w)
                P.dma("sp", hin.rearrange("p (c t) -> p c t", c=8), hT0_v[:, :, t0:t0 + w], writes=[hb])
                sq = V(SQ, BF16, 8 * w)
                P.op("act", lambda e, sq=sq, hin=hin: e.activation(out=sq, in_=hin, func=AF.Square), reads=[hb], writes=[sq_b])
                sqv = V(SQV, F32, w)
                rstd = V(RSTD, F32, w)
                norm_stats(lambda c, sq=sq, w=w: sq[:, c * w:(c + 1) * w], 8, w, j % 2, sq_b, 1.0 / 1024, sqv, sqv_b, rstd, rstd_b)
                for c in range(8):
                    P.op("dve", lambda e, c=c, hin=hin, w=w, t0=t0, rstd=rstd: e.scalar_tensor_tensor(
                        out=bigv(BIGA, c, t0, w), in0=hin[:, c * w:(c + 1) * w], scalar=sc(C_G1 + c), in1=rstd, op0=ALU.mult, op1=ALU.mult),
                        reads=[hb, rstd_b, small_b], writes=[aT_b[c][j]])

            check_stop(1)
            tb = new_T(2 + 2 + 1 + 2 * NJ)
            cc_b, yb1_b = tb[0:2], tb[2:4]
            ubz_b = tb[4]
            ub_b = [tb[5:5 + NJ], tb[5 + NJ:5 + 2 * NJ]]
            CC = [TB, TB + 2048]
            YB1 = [TB + 4096, TB + 6144]
            UB = [TB + 16384, TB + 16384 + 8256]

            def dgv(base, i):
                return V(base + i * 256, BF16, 128)

            for i in range(2):
                P.op("pool", lambda e, i=i: e.memset(V(UB[i], BF16, 2), 0.0), writes=[ubz_b])

            items = [(c, j) for c in range(8) for j in range(NJ)]
            CBK = [0, 3, 6]

            def p1_A(idx):
                c, j = items[idx]
                t0, w = trng(j)
                if j == 0:
                    prefetch(U_P1[c] + 2)
                (wcb, wcb_b), (wcc, wcc_b), (wcx, wcx_b) = wviews[U_P1[c]]
                bcb = CBK[idx % 3]
                bcc = 1 + 3 * (idx % 2)
                bcx = 2 + 3 * (idx % 2)
                rd = [aT_b[k][j] for k in range(8)]
                P.mm(PS(bcb, w), [(wk(wcb, k), bigv(BIGA, k, t0, w)) for k in range(8)], reads=rd + [wcb_b], writes=[pbuf[bcb]])
                P.mm(PS(bcc, w), [(wk(wcc, k), bigv(BIGA, k, t0, w)) for k in range(8)], reads=rd + [wcc_b], writes=[pbuf[bcc]])
                P.mm(PS(bcx, w), [(wk(wcx, k), bigv(BIGA, k, t0, w)) for k in range(8)], reads=rd + [wcx_b], writes=[pbuf[bcx]])
                ccv = V(CC[idx % 2], F32, w)
                P.op("act", lambda e: e.activation(out=ccv, in_=PS(bcc, w), func=AF.Copy), reads=[pbuf[bcc]], writes=[cc_b[idx % 2]])
                ubv = V(UB[c % 2] + (2 + t0) * 2, BF16, w)
                P.op("dve", lambda e: e.tensor_tensor(out=ubv, in0=PS(bcx, w), in1=ccv, op=ALU.mult),
                     reads=[pbuf[bcx], cc_b[idx % 2]], writes=[ub_b[c % 2][j]])

            def p1_B(idx):
                c, j = items[idx]
                t0, w = trng(j)
                bcb = CBK[idx % 3]
                rd = [ub_b[c % 2][j], ubz_b] + ([ub_b[c % 2][j - 1]] if j > 0 else [])
                yb = V(YB1[idx % 2], F32, w)
                ybb = yb1_b[idx % 2]
                P.op("act", lambda e: e.activation(out=yb, in_=V(UB[c % 2] + t0 * 2, BF16, w), func=AF.Identity, scale=sc(C_CW + c), bias=0.0),
                     reads=rd + [small_b], writes=[ybb])
                P.op("dve", lambda e: e.scalar_tensor_tensor(out=yb, in0=V(UB[c % 2] + (t0 + 1) * 2, BF16, w), scalar=sc(C_CW + 8 + c), in1=yb, op0=ALU.mult, op1=ALU.add),
                     reads=rd + [ybb, small_b], writes=[ybb])
                P.op("dve", lambda e: e.scalar_tensor_tensor(out=yb, in0=V(UB[c % 2] + (t0 + 2) * 2, BF16, w), scalar=sc(C_CW + 16 + c), in1=yb, op0=ALU.mult, op1=ALU.add),
                     reads=rd + [ybb, small_b], writes=[ybb])
                P.op("dve", lambda e: e.tensor_tensor(out=bigv(BIGB, c, t0, w), in0=PS(bcb, w), in1=yb, op=ALU.mult),
                     reads=[pbuf[bcb], ybb], writes=[zT_b[c][j]])

            for idx in range(len(items) + 1):
                if idx < len(items):
                    p1_A(idx)
                if idx >= 1:
                    p1_B(idx - 1)

            if DEBUG:
                P.dma("sp", dbgA, V(BIGA, BF16, 8 * NT), reads=[b for r in aT_b for b in r])
                P.dma("sp", dbgB, V(BIGB, BF16, 8 * NT), reads=[b for r in zT_b for b in r])

            check_stop(2)
            tb = new_T(6)
            sga_b, m1o_b, sgbo_b = tb[0:2], tb[2:4], tb[4:6]
            SGA = [TB, TB + 2048]
            M1O = [TB + 4096, TB + 6144]
            SGBO = [TB + 8192, TB + 10240]
            m1s_b = [[Buf() for _ in range(NJ)] for _ in range(8)]
            sgbs_b = [[Buf() for _ in range(NJ)] for _ in range(8)]
            idx = 0
            for f in range(8):
                prefetch(U_P2[f] + 2)
                (wco, wco_b), (wga, wga_b), (wgb, wgb_b) = wviews[U_P2[f]]
                for j in range(NJ):
                    t0, w = trng(j)
                    bk = 3 * (idx % 2)
                    i2 = idx % 2
                    rda = [aT_b[k][j] for k in range(8)]
                    rdz = [zT_b[k][j] for k in range(8)]
                    P.mm(PS(bk, w), [(wk(wco, k), bigv(BIGB, k, t0, w)) for k in range(8)], reads=rdz + [wco_b], writes=[pbuf[bk]])
                    P.mm(PS(bk + 1, w), [(wk(wga, k), bigv(BIGA, k, t0, w)) for k in range(8)], reads=rda + [wga_b], writes=[pbuf[bk + 1]])
                    P.mm(PS(bk + 2, w), [(wk(wgb, k), bigv(BIGA, k, t0, w)) for k in range(8)], reads=rda + [wgb_b], writes=[pbuf[bk + 2]])
                    sga = V(SGA[i2], F32, w)
                    m1o = V(M1O[i2], F32, w)
                    sgbo = V(SGBO[i2], F32, w)
                    P.op("act", lambda e, sga=sga, bk=bk, w=w: e.activation(out=sga, in_=PS(bk + 1, w), func=AF.Sigmoid), reads=[pbuf[bk + 1]], writes=[sga_b[i2]])
                    P.op("dve", lambda e, sga=sga, bk=bk, w=w, m1o=m1o: e.tensor_tensor(out=m1o, in0=PS(bk, w), in1=sga, op=ALU.mult),
                         reads=[pbuf[bk], sga_b[i2]], writes=[m1o_b[i2]])
                    P.op("act", lambda e, sgbo=sgbo, bk=bk, w=w: e.activation(out=sgbo, in_=PS(bk + 2, w), func=AF.Sigmoid), reads=[pbuf[bk + 2]], writes=[sgbo_b[i2]])
                    P.dma("pool", m1_scr[f, :, t0:t0 + w], m1o, reads=[m1o_b[i2]], writes=[m1s_b[f][j]])
                    P.dma("pool", sgb_scr[f, :, t0:t0 + w], sgbo, reads=[sgbo_b[i2]], writes=[sgbs_b[f][j]])
                    idx += 1

            check_stop(3)
            after_B = merged([b for r in zT_b for b in r])
            QA, QB, KA, KB = BIGB, BIGB + 8224, BIGB + 2 * 8224, BIGB + 3 * 8224
            VV = BIGB + 4 * 8224
            MSK = VV + 8544
            PT = MSK + 8192
            SSB = PT + 8192
            assert SSB + 4096 <= BIGB + 8 * NT * 2
            qz_b = Buf(after_B)
            qaug_b = Buf(after_B)
            qA_b = [Buf(after_B) for _ in range(NJ)]
            qB_b = [Buf(after_B) for _ in range(NJ)]
            kA_b = [Buf(after_B) for _ in range(NJ)]
            kB_b = [Buf(after_B) for _ in range(NJ)]
            v_b = [Buf(after_B) for _ in range(9)]
            vone_b = Buf(after_B)
            msk_b = Buf(after_B)
            pt_b = [Buf(after_B) for _ in range(8)]
            ssb_b = [Buf(after_B) for _ in range(2)]
            tb = new_T(4 + 4 + 4 + 4 + 4 + 6 + 2 * NJ + 9 + 3 + NJ)
            r0_b, r1_b, t0_b, osb_b, on_b = tb[0:4], tb[4:8], tb[8:12], tb[12:16], tb[16:20]
            ssq_b, lnv_b, rstd3_b, junk_b = tb[20], tb[21], tb[22], tb[23]
            oout_b = tb[24:26]
            RS = TB
            T0S = TB + 128
            OSBS = T0S + 2048
            JUNK = OSBS + 2048
            ONS = JUNK + 512
            OOUT = [ONS + 1024, ONS + 2048]
            KA1 = TB + 8192
            KB1 = KA1 + 8224
            VV1 = KB1 + 8224
            assert VV1 + 8544 <= TB + 44 * 1024
            KAs, KBs, VVs = [KA, KA1], [KB, KB1], [VV, VV1]
            kA_bs = [kA_b, tb[26:26 + NJ]]
            kB_bs = [kB_b, tb[26 + NJ:26 + 2 * NJ]]
            v_bs = [v_b, tb[26 + 2 * NJ:26 + 2 * NJ + 9]]
            kz1_b, vone1_b, kaug1_b = tb[26 + 2 * NJ + 9], tb[26 + 2 * NJ + 10], tb[26 + 2 * NJ + 11]
            qaug_bt = [Buf(after_B) for _ in range(NJ)]
            kaug_bs = [qaug_b, kaug1_b]
            vone_bs = [vone_b, vone1_b]
            os_b = [[Buf() for _ in range(NJ)] for _ in range(8)]
            oT_b = [[Buf() for _ in range(NJ)] for _ in range(8)]

            for base in (QA, KA):
                P.op("pool", lambda e, base=base: e.memset(V(base, BF16, NT, 64, 128), 0.0), writes=[qz_b])
            for base in (QB, KB):
                P.op("pool", lambda e, base=base: e.memset(V(base, BF16, NT, 0, 64), 0.0), writes=[qz_b])
            mskv = V(MSK, F32, 4 * 512)
            P.dma("sp", mskv, masks_d, writes=[msk_b])
            P.op("pool", lambda e: e.memset(V(VV, BF16, 33 * 129).rearrange("p (b c) -> p b c", c=129)[:, :, 128:129], 1.0), writes=[vone_b])
            P.op("pool", lambda e: e.memset(V(KA1, BF16, NT, 64, 128), 0.0), writes=[kz1_b])
            P.op("pool", lambda e: e.memset(V(KB1, BF16, NT, 0, 64), 0.0), writes=[kz1_b])
            P.op("pool", lambda e: e.memset(V(VV1, BF16, 33 * 129).rearrange("p (b c) -> p b c", c=129)[:, :, 128:129], 1.0), writes=[vone1_b])

            tasks = []

            def q_task(hh, j):
                def f():
                    t0, w = trng(j)
                    wq_, wq_b_ = wviews[U_P3[hh]][0]
                    bk = sctr[0] % 4
                    sctr[0] += 1
                    P.mm(PS(bk, w), [(wk(wq_, k), bigv(BIGA, k, t0, w)) for k in range(8)], reads=[aT_b[k][j] for k in range(8)] + [wq_b_], writes=[pbuf[bk]])
                    P.op("dve", lambda e: e.tensor_scalar(out=V(QA + t0 * 2, BF16, w, 0, 64), in0=PS(bk, w, 0, 64), scalar1=0.125, scalar2=None, op0=ALU.mult),
                         reads=[pbuf[bk]], writes=[qA_b[j]])
                    P.op("dve", lambda e: e.tensor_scalar(out=V(QB + t0 * 2, BF16, w, 64, 128), in0=PS(bk, w, 64, 128), scalar1=0.125, scalar2=None, op0=ALU.mult),
                         reads=[pbuf[bk]], writes=[qB_b[j]])
                return f

            def k_task(hh, j):
                def f():
                    p = (hh + PSWAP) % 2
                    t0, w = trng(j)
                    wk_, wk_b_ = wviews[U_P3[hh]][1]
                    bk = sctr[0] % 4
                    sctr[0] += 1
                    P.mm(PS(bk, w), [(wk(wk_, k), bigv(BIGA, k, t0, w)) for k in range(8)], reads=[aT_b[k][j] for k in range(8)] + [wk_b_], writes=[pbuf[bk]])
                    P.op("dve", lambda e: e.tensor_copy(out=V(KAs[p] + t0 * 2, BF16, w, 0, 64), in_=PS(bk, w, 0, 64)), reads=[pbuf[bk]], writes=[kA_bs[p][j]])
                    P.op("dve", lambda e: e.tensor_copy(out=V(KBs[p] + t0 * 2, BF16, w, 64, 128), in_=PS(bk, w, 64, 128)), reads=[pbuf[bk]], writes=[kB_bs[p][j]])
                return f

            def v_task(hh, g):
                def f():
                    p = (hh + PSWAP) % 2
                    wv_, wv_b_ = wviews[U_P3[hh]][2]
                    bk = sctr[0] % 4
                    sctr[0] += 1
                    nblk = 4 if g < 8 else 1
                    for bi in range(nblk):
                        tbk = g * 4 + bi
                        tw = 128 if tbk < 32 else 16
                        jj = tbk // 4
                        P.mm(psb[bk][0:tw, bi * 128:(bi + 1) * 128], [(bigv(BIGA, k, tbk * 128, tw), wk(wv_, k)) for k in range(8)],
                             reads=[aT_b[k][jj] for k in range(8)] + [wv_b_], writes=[pbuf[bk]])
                    tw = 128 if g < 8 else 16
                    vdst = V(VVs[p] + g * 4 * 129 * 2, BF16, nblk * 129, 0, tw).rearrange("p (b c) -> p b c", c=129)[:, :, 0:128]
                    vsrc = psb[bk][0:tw, 0:nblk * 128].rearrange("p (b c) -> p b c", c=128)
                    P.op("dve", lambda e: e.tensor_copy(out=vdst, in_=vsrc), reads=[pbuf[bk]], writes=[v_bs[p][g]])
                return f

            def kaug_task(hh):
                def f():
                    p = (hh + PSWAP) % 2
                    zb_ = qz_b if p == 0 else kz1_b
                    P.dma("sp", V(KAs[p], BF16, NT, 64, 68), augk_d[hh], reads=[zb_], writes=[kaug_bs[p]])
                    P.dma("sp", V(KBs[p], BF16, NT, 0, 4), augk_d[hh], reads=[zb_], writes=[kaug_bs[p]])
                return f
            sctr = [0]

            for h in range(8):
                prefetch(U_P3[h] + 2)
                (wq, wq_b), (wkk, wkk_b), (wv, wv_b) = wviews[U_P3[h]]
                pset = (h + PSWAP) % 2
                P.dma("sp", V(QA, BF16, NT, 64, 68), augq_d[h], reads=[qz_b], writes=[qaug_bt[0]])
                P.dma("sp", V(QB, BF16, NT, 0, 4), augq_d[h], reads=[qz_b], writes=[qaug_bt[0]])
                if h == 0:
                    kaug_task(0)()
                    for j in range(NJ):
                        q_task(0, j)()
                        k_task(0, j)()
                    for g in range(9):
                        v_task(0, g)()
                if h < 7:
                    tasks.append(kaug_task(h + 1))
                    for j in range(NJ):
                        tasks.append(k_task(h + 1, j))
                    for g in range(9):
                        tasks.append(v_task(h + 1, g))

                sweep = []
                for j in range(NJ):
                    nkb = 4 * j + 4 if j < 8 else NKB
                    for m in range(2):
                        if j < 8:
                            for kb in range(nkb):
                                sweep.append((j, m, kb, kb == nkb - 1, 1))
                        else:
                            for g_ in range(8):
                                sweep.append((j, m, 4 * g_, False, 4))
                            sweep.append((j, m, 32, True, 1))
                pending = []

                def emit_S(i):
                    j, m, kb, last, nblk = sweep[i]
                    t0, w = trng(j)
                    kw = 128 if kb < 32 else 16
                    bk = sctr[0] % 4
                    sctr[0] += 1
                    Qb, Kb = (QA, KAs[pset]) if m == 0 else (QB, KBs[pset])
                    qb_, kb_ = (qA_b, kA_bs[pset]) if m == 0 else (qB_b, kB_bs[pset])
                    xrd = [qaug_bt[0], kaug_bs[pset], qz_b] + ([kz1_b] if pset == 1 else [])
                    if nblk == 4:
                        rd = [qb_[j], kb_[kb // 4]] + xrd
                        q = P.q["pe"]
                        waits = P._collect("pe", rd, [pbuf[bk]])
                        q.n += 1
                        tok = (q.sem, q.n)
                        for b_ in range(4):
                            def fn(e, b_=b_):
                                return e.matmul(PS(bk, w, 0, 128, b_ * w), lhsT=V(Kb + (kb + b_) * 128 * 2, BF16, 128), rhs=V(Qb + t0 * 2, BF16, w),
                                                start=True, stop=True, skip_group_check=True)
                            q.ops.append((waits if b_ == 0 else [], fn, tok if b_ == 3 else None, 1))
                        P._commit(tok, rd, [pbuf[bk]])
                        ptv4 = V(PT + (i % 8) * 1024, BF16, 4 * w)
                        P.op("act", lambda e: e.activation(out=ptv4, in_=PS(bk, 4 * w), func=AF.Exp), reads=[pbuf[bk]], writes=[pt_b[i % 8]])
                        return
                    diag = (kb >= 4 * j)
                    r = kb - 4 * j if (diag and j < 8) else 0
                    c0 = 128 * r
                    wc = w - c0
                    P.mm(PS(bk, wc, 0, kw, c0), [(V(Kb + kb * 128 * 2, BF16, kw), V(Qb + (t0 + c0) * 2, BF16, wc))],
                         reads=[qb_[j], kb_[kb // 4]] + xrd, writes=[pbuf[bk]])
                    ptv = V(PT + (i % 8) * 1024 + c0 * 2, BF16, wc, 0, kw)
                    if diag:
                        ssv = V(SSB + (i % 2) * 2048 + c0 * 4, F32, wc, 0, kw)
                        P.op("dve", lambda e: e.tensor_tensor(out=ssv, in0=PS(bk, wc, 0, kw, c0), in1=V(MSK + r * 2048 + c0 * 4, F32, wc, 0, kw), op=ALU.add),
                             reads=[pbuf[bk], msk_b], writes=[ssb_b[i % 2]])
                        P.op("act", lambda e: e.activation(out=ptv, in_=ssv, func=AF.Exp), reads=[ssb_b[i % 2]], writes=[pt_b[i % 8]])
                    else:
                        P.op("act", lambda e: e.activation(out=ptv, in_=PS(bk, wc, 0, kw, c0), func=AF.Exp), reads=[pbuf[bk]], writes=[pt_b[i % 8]])

                def emit_PV(i):
                    j, m, kb, last, nblk = sweep[i]
                    t0, w = trng(j)
                    kw = 128 if kb < 32 else 16
                    r = kb - 4 * j if (kb >= 4 * j and j < 8) else 0
                    first = (kb == 0)
                    g = kb // 4
                    nqs = 4 if j < 8 else 1
                    qw = 128 if j < 8 else 16
                    if nblk == 4:
                        bank_ = pbuf[4 + 2 * m]
                        rd = [pt_b[i % 8], v_bs[pset][g], vone_bs[pset]]
                        q = P.q["pe"]
                        waits = P._collect("pe", rd, [bank_] if first else [])
                        q.n += 1
                        tok = (q.sem, q.n)
                        vbase = VVs[pset]
                        for b_ in range(4):
                            def fn(e, b_=b_, vbase=vbase):
                                return e.matmul(psb[4 + 2 * m][0:w, 0:129], lhsT=V(PT + (i % 8) * 1024 + b_ * w * 2, BF16, w),
                                                rhs=V(vbase + (kb + b_) * 129 * 2, BF16, 129), start=(first and b_ == 0), stop=False, skip_group_check=True)
                            q.ops.append((waits if b_ == 0 else [], fn, tok if b_ == 3 else None, 1))
                        P._commit(tok, rd, [bank_])
                        return
                    vblk = V(VVs[pset] + kb * 129 * 2, BF16, 129, 0, kw)
                    banks = [pbuf[4 + 2 * m], pbuf[5 + 2 * m]] if j < 8 else [pbuf[4 + 2 * m]]
                    reads = [pt_b[i % 8], v_bs[pset][g], vone_bs[pset]]
                    q = P.q["pe"]
                    waits = P._collect("pe", reads, banks if first else [])
                    fns = []
                    for qs in range(r, nqs):
                        bank = 4 + 2 * m + qs // 2
                        col = (qs % 2) * 129
                        out_ap = psb[bank][0:qw, col:col + 129]
                        lhsT = V(PT + (i % 8) * 1024 + qs * 128 * 2, BF16, qw, 0, kw)
                        st = first and (qs % 2 == 0)
                        lastq = (kb == (4 * j + qs if j < 8 else 32))

                        def fn(e, out_ap=out_ap, lhsT=lhsT, st=st, lastq=lastq):
                            return e.matmul(out_ap, lhsT=lhsT, rhs=vblk, start=st, stop=lastq, skip_group_check=True)
                        fns.append(fn)
                    q.n += 1
                    tok = (q.sem, q.n)
                    for k_, fn in enumerate(fns):
                        q.ops.append((waits if k_ == 0 else [], fn, tok if k_ == len(fns) - 1 else None, 1))
                    P._commit(tok, reads, banks)
                    if last:
                        emit_norm(i, j, m)
                        if m == 1 and h < 7:
                            tasks.append(q_task(h + 1, j))

                def emit_norm(i, j, m):
                    t0, w = trng(j)
                    nqs = 4 if j < 8 else 1
                    qw = 128 if j < 8 else 16

                    def oreg(qs, c0, c1):
                        bank = 4 + 2 * m + qs // 2
                        col = (qs % 2) * 129
                        return psb[bank][0:qw, col + c0:col + c1], pbuf[bank]

                    def rsc(c):
                        return V(RS + c * 4, F32, 1, 0, qw)

                    def T0v(qs):
                        return V(T0S + qs * 512, F32, 128, 0, qw)

                    def OSBv(qs):
                        return V(OSBS + qs * 512, F32, 128, 0, qw)

                    def ONv(qs):
                        return V(ONS + qs * 256, BF16, 128, 0, qw)
                    steps = []
                    for qs in range(nqs):
                        def f(qs=qs):
                            oap, ob_ = oreg(qs, 0, 128)
                            sap, _ = oreg(qs, 128, 129)
                            if m == 0:
                                P.op("dve", lambda e: e.reciprocal(out=rsc(qs), in_=sap), reads=[ob_], writes=[r0_b[qs]])
                                P.op("dve", lambda e: e.tensor_scalar(out=T0v(qs), in0=oap, scalar1=rsc(qs), scalar2=None, op0=ALU.mult),
                                     reads=[ob_, r0_b[qs]], writes=[t0_b[qs]])
                            else:
                                P.op("dve", lambda e: e.reciprocal(out=rsc(4 + qs), in_=sap), reads=[ob_], writes=[r1_b[qs]])
                                P.op("dve", lambda e: e.tensor_tensor(out=rsc(8 + qs), in0=rsc(4 + qs), in1=misc[0:qw, 3:4], op=ALU.mult),
                                     reads=[r1_b[qs], misc_b], writes=[r1_b[qs]])
                                P.op("dve", lambda e: e.scalar_tensor_tensor(out=OSBv(qs), in0=oap, scalar=rsc(8 + qs), in1=T0v(qs), op0=ALU.mult, op1=ALU.add),
                                     reads=[ob_, r1_b[qs], t0_b[qs]], writes=[osb_b[qs]])
                        steps.append((qs, f))
                    if m == 1:
                        def s_sumsq():
                            for qs in range(nqs):
                                P.op("dve", lambda e, qs=qs: e.scalar_tensor_tensor(out=V(JUNK, F32, 128, 0, qw), in0=OSBv(qs), scalar=1.0, in1=OSBv(qs),
                                                                                    op0=ALU.mult, op1=ALU.mult, accum_out=rsc(12 + qs)),
                                     reads=[osb_b[qs]], writes=[junk_b, ssq_b])
                        steps.append((5, s_sumsq))
                        steps.append((7, lambda: P.op("act", lambda e: e.activation(out=V(RS + 16 * 4, F32, nqs, 0, qw), in_=V(RS + 12 * 4, F32, nqs, 0, qw),
                                                                                       func=AF.Ln, scale=1.0 / 128, bias=misc[0:qw, 5:6]),
                                                      reads=[ssq_b, misc_b], writes=[lnv_b])))
                        steps.append((8, lambda: P.op("act", lambda e: e.activation(out=V(RS + 20 * 4, F32, nqs, 0, qw), in_=V(RS + 16 * 4, F32, nqs, 0, qw),
                                                                                       func=AF.Exp, scale=-0.5),
                                                      reads=[lnv_b], writes=[rstd3_b])))

                        def s_scale():
                            for qs in range(nqs):
                                P.op("dve", lambda e, qs=qs: e.tensor_scalar(out=ONv(qs), in0=OSBv(qs), scalar1=rsc(20 + qs), scalar2=None, op0=ALU.mult),
                                     reads=[osb_b[qs], rstd3_b], writes=[on_b[qs]])
                        steps.append((9, s_scale))

                        def s_out():
                            sb_ = sctr[0] % 4
                            sctr[0] += 1
                            psbf = psb[sb_][:, :].bitcast(BF16)
                            q = P.q["pe"]
                            rd = [on_b[qs] for qs in range(nqs)] + [identb_b]
                            waits = P._collect("pe", rd, [pbuf[sb_]])
                            q.n += 1
                            tok = (q.sem, q.n)
                            for qs in range(nqs):
                                def fn(e, qs=qs):
                                    return e.transpose(psbf[:, qs * 128:qs * 128 + qw], ONv(qs), identb[0:qw, 0:qw])
                                q.ops.append((waits if qs == 0 else [], fn, tok if qs == nqs - 1 else None, 1))
                            P._commit(tok, rd, [pbuf[sb_]])
                            if h < 7:
                                oo = V(OOUT[j % 2], BF16, w)
                                P.op("dve", lambda e: e.tensor_scalar(out=oo, in0=psbf[:, 0:w], scalar1=gsub, scalar2=None, op0=ALU.mult),
                                     reads=[pbuf[sb_], misc_b], writes=[oout_b[j % 2]])
                                P.dma("pool", o_scr[h, :, t0:t0 + w], oo, reads=[oout_b[j % 2]], writes=[os_b[h][j]])
                            else:
                                P.op("dve", lambda e: e.tensor_scalar(out=bigv(BIGA, 7, t0, w), in0=psbf[:, 0:w], scalar1=gsub, scalar2=None, op0=ALU.mult),
                                     reads=[pbuf[sb_], misc_b], writes=[oT_b[7][j], aT_b[7][j]])
                        steps.append((11, s_out))
                    for off_, f_ in steps:
                        pending.append((i + 3 + off_, f_))
                    pending.sort(key=lambda x: x[0])

                n = len(sweep)
                LA = 3
                for i in range(n + LA):
                    if i < n:
                        emit_S(i)
                    if i >= LA:
                        emit_PV(i - LA)
                    while pending and pending[0][0] <= i:
                        pending.pop(0)[1]()
                    if TASK_INTERLEAVE and i % 8 == 5 and tasks:
                        tasks.pop(0)()
                while pending:
                    pending.pop(0)[1]()
                while tasks:
                    tasks.pop(0)()

                if h == 7:
                    pass
                if h == 6:
                    pass

            for hh in range(7):
                P.dma("sp", bigv(BIGA, hh, 0, NT), o_scr[hh, :, :], reads=os_b[hh], writes=oT_b[hh] + aT_b[hh])

            if DEBUG:
                P.dma("sp", dbgC, V(BIGA, BF16, 8 * NT), reads=[b for r in oT_b for b in r])

            check_stop(4)
            after_B = merged([qz_b, qaug_b, msk_b, vone_b] + qaug_bt + qA_b + qB_b + kA_b + kB_b + v_b + pt_b + ssb_b)
            m2_b = [[Buf(after_B) for _ in range(NJ)] for _ in range(8)]
            NS4 = 6
            tb = new_T(2 * NS4 + 4)
            m1i_b, sgi_b, tmp_b = tb[0:NS4], tb[NS4:2 * NS4], tb[2 * NS4:2 * NS4 + 4]
            M1I = [TB + 2048 * i for i in range(NS4)]
            SGI = [TB + 2048 * (NS4 + i) for i in range(NS4)]
            TMP = [TB + 2048 * (2 * NS4 + i) for i in range(4)]
            idx = 0
            for f in range(8):
                prefetch(min(U_P4[f] + 3, U_P5))
                (wao, wao_b), = wviews[U_P4[f]]
                for j in range(NJ):
                    t0, w = trng(j)
                    bk = idx % 4
                    i3 = idx % NS4
                    m1i = V(M1I[i3], F32, w)
                    sgi = V(SGI[i3], F32, w)
                    P.dma("sp", m1i, m1_scr[f, :, t0:t0 + w], reads=[m1s_b[f][j]], writes=[m1i_b[i3]])
                    P.dma("sp", sgi, sgb_scr[f, :, t0:t0 + w], reads=[sgbs_b[f][j]], writes=[sgi_b[i3]])
                    P.mm(PS(bk, w), [(wk(wao, k), bigv(BIGA, k, t0, w)) for k in range(8)], reads=[oT_b[k][j] for k in range(8)] + [wao_b], writes=[pbuf[bk]])
                    tmp = V(TMP[idx % 4], F32, w)
                    P.op("dve", lambda e, tmp=tmp, bk=bk, w=w, sgi=sgi: e.tensor_tensor(out=tmp, in0=PS(bk, w), in1=sgi, op=ALU.mult),
                         reads=[pbuf[bk], sgi_b[i3]], writes=[tmp_b[idx % 4]])
                    P.op("pool", lambda e, tmp=tmp, m1i=m1i, f=f, t0=t0, w=w: e.tensor_tensor(out=bigv(BIGB, f, t0, w), in0=tmp, in1=m1i, op=ALU.add),
                         reads=[tmp_b[idx % 4], m1i_b[i3]], writes=[m2_b[f][j]])
                    idx += 1

            if EXTRA:
                xb = Buf(merged(Tbufs))
                for _ in range(EXTRA):
                    P.dma("sp", V(TB, F32, 4096).rearrange("p (c t) -> p c t", c=8), hT0_v[:, :, 0:512], writes=[xb])
            check_stop(5)
            prefetch(U_P5 + 1)
            wmix = wviews[U_P5]
            tb = new_T(16 + 8 + 2)
            mix_b = [tb[0:8], tb[8:16]]
            msq0_b = tb[16:24]
            sqv5_b, rstd5_b = tb[24], tb[25]
            after_wst = merged(wst_b)
            h0_b = [Buf(after_wst) for _ in range(8)]
            MIX = [TB, TB + 16384]
            MSQ0, SQV5, RSTD5 = TB + 32768, TB + 40960, TB + 43008
            H0 = WST
            h1s_b = [Buf() for _ in range(NJ)]
            fT_b = [[Buf() for _ in range(NJ)] for _ in range(8)]

            def p5_A(j, part):
                t0, w = trng(j)
                p = j % 2
                mixv = V(MIX[p], F32, 8 * w)
                for fo in range(4 * part, 4 * part + 4):
                    bk = fo % 4
                    wm, wm_b = wmix[fo]
                    P.mm(PS(bk, w), [(wk(wm, k), bigv(BIGB, k, t0, w)) for k in range(8)], reads=[m2_b[k][j] for k in range(8)] + [wm_b], writes=[pbuf[bk]])
                    P.op("dve", lambda e, fo=fo, bk=bk: e.tensor_copy(out=mixv[:, fo * w:(fo + 1) * w], in_=PS(bk, w)),
                         reads=[pbuf[bk]], writes=[mix_b[p][fo]])
                    if j == 0:
                        dst, dstb = V(MSQ0 + fo * w * 2, BF16, w), msq0_b[fo]
                    else:
                        pt0, _ = trng(j - 1)
                        dst, dstb = bigv(BIGB, fo, pt0, w), m2_b[fo][j - 1]
                    P.op("act", lambda e, fo=fo, dst=dst: e.activation(out=dst, in_=mixv[:, fo * w:(fo + 1) * w], func=AF.Square),
                         reads=[mix_b[p][fo]], writes=[dstb])

            def p5_B(j, half):
                t0, w = trng(j)
                p = j % 2
                mixv = V(MIX[p], F32, 8 * w)
                h0 = V(H0, F32, 8 * w)
                sqv = V(SQV5, F32, w)
                rstd = V(RSTD5, F32, w)
                sq2 = [(V(MSQ0 + c * w * 2, BF16, w), msq0_b[c]) for c in range(8)]
                if half == 1:
                    P.mm(PS(7, w), [(ones, v_) for v_, _ in sq2], reads=[ones_b] + [b_ for _, b_ in sq2], writes=[pbuf[7]])
                    P.op("act", lambda e: e.activation(out=sqv, in_=PS(7, w), func=AF.Ln, scale=1.0 / 1024, bias=epsb), reads=[pbuf[7], misc_b], writes=[sqv5_b])
                    P.op("act", lambda e: e.activation(out=rstd, in_=sqv, func=AF.Exp, scale=-0.5), reads=[sqv5_b], writes=[rstd5_b])
                    for c in range(8):
                        P.op("dve", lambda e, c=c: e.scalar_tensor_tensor(
                            out=bigv(BIGA, c, t0, w), in0=h0[:, c * w:(c + 1) * w], scalar=sc(C_GFPRE + c), in1=rstd, op0=ALU.mult, op1=ALU.mult),
                            reads=[h0_b[c], rstd5_b, small_b], writes=[fT_b[c][j], oT_b[c][j]])
                    return
                if j == 0:
                    sqs = [(V(MSQ0 + c * w * 2, BF16, w), msq0_b[c]) for c in range(8)]
                else:
                    pt0, _ = trng(j - 1)
                    sqs = [(bigv(BIGB, c, pt0, w), m2_b[c][j - 1]) for c in range(8)]
                P.mm(PS(6, w), [(ones, v_) for v_, _ in sqs], reads=[ones_b] + [b_ for _, b_ in sqs], writes=[pbuf[6]])
                P.op("act", lambda e: e.activation(out=sqv, in_=PS(6, w), func=AF.Ln, scale=1.0 / 1024, bias=epsb), reads=[pbuf[6], misc_b], writes=[sqv5_b])
                P.op("act", lambda e: e.activation(out=rstd, in_=sqv, func=AF.Exp, scale=-0.5), reads=[sqv5_b], writes=[rstd5_b])
                for fo in range(8):
                    P.op("dve", lambda e, fo=fo: e.scalar_tensor_tensor(
                        out=mixv[:, fo * w:(fo + 1) * w], in0=mixv[:, fo * w:(fo + 1) * w], scalar=sc(C_GPOST + fo), in1=rstd, op0=ALU.mult, op1=ALU.mult),
                        reads=[mix_b[p][fo], rstd5_b, small_b], writes=[mix_b[p][fo]])
                for fo in range(8):
                    eng = "pool" if fo % 2 == 0 else "dve"
                    P.op(eng, lambda e, fo=fo: e.tensor_tensor(out=h0[:, fo * w:(fo + 1) * w], in0=h0[:, fo * w:(fo + 1) * w], in1=mixv[:, fo * w:(fo + 1) * w], op=ALU.add),
                         reads=[h0_b[fo], mix_b[p][fo]], writes=[h0_b[fo]])
                    P.op("act", lambda e, fo=fo: e.activation(out=sq2[fo][0], in_=h0[:, fo * w:(fo + 1) * w], func=AF.Square),
                         reads=[h0_b[fo]], writes=[sq2[fo][1]])
                P.dma("pool", h1_scr[j, :, 0:8 * w], h0, reads=h0_b, writes=[h1s_b[j]])

            for j in range(NJ + 1):
                if j < NJ:
                    p5_A(j, 0)
                if j >= 1:
                    p5_B(j - 1, 0)
                if j < NJ:
                    p5_A(j, 1)
                if j >= 1:
                    p5_B(j - 1, 1)
                if j < NJ:
                    t0, w = trng(j)
                    P.dma("sp", V(H0, F32, 8 * w).rearrange("p (c t) -> p c t", c=8), hT0_v[:, :, t0:t0 + w], writes=h0_b)
            after_h0 = merged(h0_b)
            for b in wst_b:
                for k_, v_ in after_h0.items():
                    if v_ > b.r.get(k_, 0):
                        b.r[k_] = v_

            check_stop(6)
            tb = new_T(2 + 2 + 2 + 2 + 1 + 4 * NJ)
            dg6_b, sg6_b, hid_b, yb_b, ubz6_b = tb[0:2], tb[2:4], tb[4:6], tb[6:8], tb[8]
            ug_b = [tb[9:9 + NJ], tb[9 + NJ:9 + 2 * NJ]]
            uu_b = [tb[9 + 2 * NJ:9 + 3 * NJ], tb[9 + 3 * NJ:9 + 4 * NJ]]
            UG = [TB, TB + 8256]
            UU = [TB + 2 * 8256, TB + 3 * 8256]
            DG6 = [TB + 4 * 8256, TB + 4 * 8256 + 768]
            SG6 = [TB + 4 * 8256 + 1536, TB + 4 * 8256 + 1536 + 2048]
            YB = [TB + 4 * 8256 + 1536 + 4096, TB + 4 * 8256 + 1536 + 6144]
            HID = [TB + 4 * 8256 + 1536 + 8192 + i * 1024 for i in range(2)]
            assert HID[1] + 1024 <= TB + 44 * 1024
            hids_b = [[Buf() for _ in range(NJ)] for _ in range(22)]
            for base in UG + UU:
                P.op("pool", lambda e, base=base: e.memset(V(base, BF16, 2), 0.0), writes=[ubz6_b])
            items6 = [(i, j) for i in range(22) for j in range(NJ)]
            after_m2 = merged([b for r in m2_b for b in r])
            for b in wdn_b:
                b.r = dict(after_m2)
            UBK = [2, 3, 6]

            wd_left = list(wplan[U_P7])

            def p6_A(idx):
                i, j = items6[idx]
                t0, w = trng(j)
                if j == 0:
                    prefetch(min(U_P6[i] + 2, U_P7))
                    for s in range(3):
                        P.op("dve", lambda e, i=i, s=s: e.tensor_scalar(out=dgv(DG6[i % 2], s), in0=ident, scalar1=sc(C_FCW + s * 44 + i), scalar2=None, op0=ALU.mult),
                             reads=[ident_b, small_b], writes=[dg6_b[i % 2]])
                if idx >= 12 * NJ and wd_left:
                    src_, kc_, dst_, dstb_ = wd_left.pop(0)
                    load_w(src_, kc_, dst_, dstb_, eng="act")
                (wg, wg_b), (wu, wu_b) = wviews[U_P6[i]]
                bg = idx % 2
                bu = UBK[idx % 3]
                rd = [fT_b[k][j] for k in range(8)]
                P.mm(PS(bg, w), [(wk(wg, k), bigv(BIGA, k, t0, w)) for k in range(8)], reads=rd + [wg_b], writes=[pbuf[bg]])
                P.mm(PS(bu, w), [(wk(wu, k), bigv(BIGA, k, t0, w)) for k in range(8)], reads=rd + [wu_b], writes=[pbuf[bu]])
                P.op("act", lambda e: e.activation(out=V(UG[i % 2] + (2 + t0) * 2, BF16, w), in_=PS(bg, w), func=AF.Copy), reads=[pbuf[bg]], writes=[ug_b[i % 2][j]])
                P.op("act", lambda e: e.activation(out=V(UU[i % 2] + (2 + t0) * 2, BF16, w), in_=PS(bu, w), func=AF.Copy), reads=[pbuf[bu]], writes=[uu_b[i % 2][j]])

            def p6_B(idx):
                i, j = items6[idx]
                t0, w = trng(j)
                bk = 4 + idx % 2
                bu = UBK[idx % 3]
                rdg = [ug_b[i % 2][j], dg6_b[i % 2], ubz6_b] + ([ug_b[i % 2][j - 1]] if j > 0 else [])
                rdu = [uu_b[i % 2][j], ubz6_b] + ([uu_b[i % 2][j - 1]] if j > 0 else [])
                P.mm(PS(bk, w), [(dgv(DG6[i % 2], s), V(UG[i % 2] + (t0 + s) * 2, BF16, w)) for s in range(3)], reads=rdg, writes=[pbuf[bk]])
                yb = V(YB[idx % 2], F32, w)
                ybb = yb_b[idx % 2]
                P.op("act", lambda e: e.activation(out=yb, in_=V(UU[i % 2] + t0 * 2, BF16, w), func=AF.Identity, scale=sc(C_FCW + 22 + i), bias=sc(C_FCB + 22 + i)),
                     reads=rdu + [small_b], writes=[ybb])
                P.op("dve", lambda e: e.scalar_tensor_tensor(out=yb, in0=V(UU[i % 2] + (t0 + 1) * 2, BF16, w), scalar=sc(C_FCW + 44 + 22 + i), in1=yb, op0=ALU.mult, op1=ALU.add),
                     reads=rdu + [ybb, small_b], writes=[ybb])
                sg = V(SG6[idx % 2], F32, w)
                P.op("act", lambda e: e.activation(out=sg, in_=PS(bk, w), func=AF.Silu, bias=sc(C_FCB + i)), reads=[pbuf[bk], small_b], writes=[sg6_b[idx % 2]])
                P.op("dve", lambda e: e.scalar_tensor_tensor(out=yb, in0=PS(bu, w), scalar=sc(C_FCW + 88 + 22 + i), in1=yb, op0=ALU.mult, op1=ALU.add),
                     reads=[pbuf[bu], ybb, small_b], writes=[ybb])
                hv = V(HID[idx % 2], BF16, w)
                P.op("dve", lambda e: e.tensor_tensor(out=hv, in0=yb, in1=sg, op=ALU.mult),
                     reads=[ybb, sg6_b[idx % 2]], writes=[hid_b[idx % 2]])
                P.dma("pool", hid_scr[j, :, i * w:(i + 1) * w], hv, reads=[hid_b[idx % 2]], writes=[hids_b[i][j]])

            for idx in range(len(items6) + 1):
                if idx < len(items6):
                    p6_A(idx)
                if idx >= 1:
                    p6_B(idx - 1)

            check_stop(7)
            assert not wd_left and wnext[0] == U_P7
            wnext[0] = U_P7 + 1
            tb = new_T(2)
            hd_b = tb[0:2]
            HD = [TB, TB + 22528]
            after_A = merged([b for r in fT_b for b in r])
            after_W = merged(wbf_b)
            ysb_b = [Buf(after_A), Buf(after_A)]
            h1b_b = [Buf(after_A), Buf(after_A)]
            ysq_b = [Buf(after_m2), Buf(after_m2)]
            sqv7_b = [Buf(after_W), Buf(after_W)]
            rstd7_b = [Buf(after_W), Buf(after_W)]
            YSB = [BIGA, BIGA + 16384]
            H1B = [BIGA + 32768, BIGA + 49152]
            YSQ = [BIGB + 45056, BIGB + 45056 + 8192]
            SQV7 = [WBF, WBF + 2048]
            RSTD7 = [WBF + 4096, WBF + 6144]
            out_toks = {}

            def p7_A(j, part):
                t0, w = trng(j)
                p = j % 2
                hd = V(HD[p], BF16, 22 * w)
                h1b = V(H1B[p], F32, 8 * w)
                ysb = V(YSB[p], F32, 8 * w)
                ysq = V(YSQ[p], BF16, 8 * w)
                if part == 0:
                    P.dma("sp", hd, hid_scr[j, :, 0:22 * w], reads=[hids_b[i][j] for i in range(22)], writes=[hd_b[p]])
                    P.dma("sp", h1b, h1_scr[j, :, 0:8 * w], reads=[h1s_b[j]], writes=[h1b_b[p]])
                for fo in range(4 * part, 4 * part + 4):
                    bk = fo % 4
                    P.mm(PS(bk, w), [(wdn(fo, k), hd[:, k * w:(k + 1) * w]) for k in range(22)], reads=[hd_b[p], wdn_b[fo]], writes=[pbuf[bk]])
                    P.op("dve", lambda e, fo=fo, bk=bk: e.tensor_copy(out=ysb[:, fo * w:(fo + 1) * w], in_=PS(bk, w)),
                         reads=[pbuf[bk]], writes=[ysb_b[p]])
                    P.op("act", lambda e, fo=fo: e.activation(out=ysq[:, fo * w:(fo + 1) * w], in_=ysb[:, fo * w:(fo + 1) * w], func=AF.Square),
                         reads=[ysb_b[p]], writes=[ysq_b[p]])

            def p7_B(j):
                t0, w = trng(j)
                p = j % 2
                h1b = V(H1B[p], F32, 8 * w)
                ysb = V(YSB[p], F32, 8 * w)
                ysq = V(YSQ[p], BF16, 8 * w)
                sqv = V(SQV7[p], F32, w)
                rstd = V(RSTD7[p], F32, w)
                norm_stats(lambda c: ysq[:, c * w:(c + 1) * w], 8, w, 6 + p, ysq_b[p], 1.0 / 1024, sqv, sqv7_b[p], rstd, rstd7_b[p])
                for fo in range(8):
                    P.op("dve", lambda e, fo=fo: e.scalar_tensor_tensor(
                        out=ysb[:, fo * w:(fo + 1) * w], in0=ysb[:, fo * w:(fo + 1) * w], scalar=sc(C_GFPOST + fo), in1=rstd, op0=ALU.mult, op1=ALU.mult),
                        reads=[ysb_b[p], rstd7_b[p], small_b], writes=[ysb_b[p]])
                for fo in range(8):
                    P.op("pool", lambda e, fo=fo: e.tensor_tensor(out=h1b[:, fo * w:(fo + 1) * w], in0=h1b[:, fo * w:(fo + 1) * w], in1=ysb[:, fo * w:(fo + 1) * w], op=ALU.add),
                         reads=[h1b_b[p], ysb_b[p]], writes=[h1b_b[p]])
                h1b3 = h1b.rearrange("p (c t) -> p c t", c=8)
                if j == 0:
                    tok = P.dma("pool", outT_v[:, :, 0:496], h1b3[:, :, 16:512], reads=[h1b_b[p]])
                else:
                    tok = P.dma("pool", outT_v[:, :, t0 - 16:t0 - 16 + w], h1b3, reads=[h1b_b[p]])
                out_toks[tok[0]] = max(out_toks.get(tok[0], 0), tok[1])

            for j in range(NJ + 1):
                if j < NJ:
                    p7_A(j, 0)
                if j >= 1:
                    p7_B(j - 1)
                if j < NJ:
                    p7_A(j, 1)
            P.final_wait("sp", out_toks)

        try:
            body()
        except StopBuild:
            P.final_all()

        @block.sync
        def _(e):
            P.replay("sp", e)

        @block.gpsimd
        def _(e):
            P.replay("pool", e)

        @block.tensor
        def _(e):
            P.replay("pe", e)

        @block.scalar
        def _(e):
            P.replay("act", e)

        @block.vector
        def _(e):
            P.replay("dve", e)
    return nc


def host_consts():
    ident = np.eye(128, dtype=np.float32)
    pos = np.arange(NT)
    augq = np.zeros((8, 4, NT), np.float32)
    augk = np.zeros((8, 4, NT), np.float32)
    for h in range(8):
        slope = 2.0 ** (-(h + 1))
        augq[h, 0] = -slope * ((pos >> 8) << 8)
        augq[h, 1] = -slope * (pos & 255)
        augq[h, 2] = 1.0
        augq[h, 3] = 1.0
        augk[h, 0] = 1.0
        augk[h, 1] = 1.0
        augk[h, 2] = slope * ((pos >> 7) << 7)
        augk[h, 3] = slope * (pos & 127)
    ki = np.arange(128)[:, None]
    qi = np.arange(512)[None, :]
    masks = np.zeros((128, 4, 512), np.float32)
    for r in range(4):
        masks[:, r, :] = np.where(128 * r + ki <= qi, 0.0, -30000.0)
    return ident, augq.astype(ml_dtypes.bfloat16), augk.astype(ml_dtypes.bfloat16), masks.reshape(128, 2048)


def pack_small(inp):
    s = np.zeros((128, NSMALL), np.float32)

    def g(v):
        return np.asarray(v, np.float32).reshape(8, 128).T

    s[:, C_G1:C_G1 + 8] = g(inp["norm_mix_pre"][0])
    s[:, C_GPOST:C_GPOST + 8] = g(inp["norm_mix_post"][0])
    s[:, C_GFPRE:C_GFPRE + 8] = g(inp["norm_ffn_pre"][0])
    s[:, C_GFPOST:C_GFPOST + 8] = g(inp["norm_ffn_post"][0])
    s[:, C_CW:C_CW + 24] = np.asarray(inp["conv_w"][0], np.float32).reshape(3, 8, 128).transpose(2, 0, 1).reshape(128, 24)
    s[:, C_FCW:C_FCW + 132] = np.asarray(inp["ffn_conv_w"][0], np.float32).reshape(3, 44, 128).transpose(2, 0, 1).reshape(128, 132)
    s[:, C_FCB:C_FCB + 44] = np.asarray(inp["ffn_conv_b"][0], np.float32).reshape(44, 128).T
    s[:, C_SUBG] = np.asarray(inp["subln_g"][0], np.float32)
    for col, k in ((C_LQ1, "lambda_q1"), (C_LK1, "lambda_k1"), (C_LQ2, "lambda_q2"), (C_LK2, "lambda_k2")):
        s[:, col:col + 64] = np.asarray(inp[k][0], np.float32)[None, :]
    return s


_NC = None


def make_in_maps(inputs):
    x = np.asarray(inputs["x"], np.float32)
    meta = np.asarray(inputs["meta_tokens"], np.float32)
    ident, augq, augk, masks = host_consts()
    small = pack_small(inputs)
    shared = {
        "w_in": np.ascontiguousarray(np.asarray(inputs["w_in"], np.float32)[0]),
        "w_conv_out": np.ascontiguousarray(np.asarray(inputs["w_conv_out"], np.float32)[0]),
        "w_attn_out": np.ascontiguousarray(np.asarray(inputs["w_attn_out"], np.float32)[0]),
        "w_mix_out": np.ascontiguousarray(np.asarray(inputs["w_mix_out"], np.float32)[0]),
        "w_ffn_up": np.ascontiguousarray(np.asarray(inputs["w_ffn_up"], np.float32)[0]),
        "w_ffn_down": np.ascontiguousarray(np.asarray(inputs["w_ffn_down"], np.float32)[0]),
        "small": small, "ident": ident, "augq": augq, "augk": augk, "masks": masks,
    }
    in_maps = []
    for b in range(x.shape[0]):
        h0 = np.concatenate([meta, x[b]], axis=0)
        d = dict(shared)
        d["hT0"] = np.ascontiguousarray(h0.T)
        in_maps.append(d)
    return in_maps


def kernel(**inputs):
    global _NC
    in_maps = make_in_maps(inputs)
    if _NC is None:
        _NC = build_nc()
    res = run_bass_kernel_spmd(_NC, in_maps, core_ids=list(range(len(in_maps))))
    out = np.stack([np.ascontiguousarray(np.asarray(r["outT"], np.float32).T) for r in res.results])
    return out.astype(np.float32)
```

```python
import numpy as np
import ml_dtypes
import concourse.bass as bass
import concourse.mybir as mybir
from concourse.bass_utils import run_bass_kernel_spmd
from contextlib import ExitStack

F32 = mybir.dt.float32
BF16 = mybir.dt.bfloat16
AF = mybir.ActivationFunctionType
ALU = mybir.AluOpType
AX = mybir.AxisListType

NT = 4112
NJ = 9
NKB = 33
EPS = 1e-6
LAM_INIT = 0.8 - 0.6 * 1.0
DEBUG = False
FAST_RECIP = False
TASK_INTERLEAVE = True
PSWAP = 0
STOP_AFTER = 99
SUB = 99
EXTRA = 0

C_G1, C_GPOST, C_GFPRE, C_GFPOST = 0, 8, 16, 24
C_CW = 32
C_FCW = 56
C_FCB = 188
C_SUBG = 232
C_LQ1, C_LK1, C_LQ2, C_LK2 = 233, 297, 361, 425
NSMALL = 489


def trng(j):
    t0 = 512 * j
    return t0, min(512, NT - t0)


def RECIP(e, out, in_):
    if FAST_RECIP:
        return e.reciprocal_approx_fast(out=out, in_=in_)
    return e.reciprocal(out=out, in_=in_)


class Buf:
    __slots__ = ("w", "r")

    def __init__(self, after=None):
        self.w = None
        self.r = dict(after) if after else {}


def merged(bufs):
    d = {}
    for b in bufs:
        if b.w is not None:
            s, v = b.w
            if v > d.get(s, 0):
                d[s] = v
        for s, v in b.r.items():
            if v > d.get(s, 0):
                d[s] = v
    return d


class Queue:
    def __init__(self, name, sem):
        self.name = name
        self.sem = sem
        self.n = 0
        self.ops = []
        self.waited = {}


class DSem:
    def __init__(self, key):
        self.key = key
        self.count = 0


class Prog:
    def __init__(self, nc, st):
        self.nc = nc
        self.sems = {}
        self.q = {}
        for name in ("pe", "act", "dve", "pool", "sp"):
            key = "s_" + name
            self.sems[key] = st.enter_context(nc.semaphore(key))
            self.q[name] = Queue(name, key)
        self.dring = {}
        self.dri = {}
        for qn, n in (("sp", 24), ("pool", 16)):
            ring = []
            for i in range(n):
                key = f"d_{qn}{i}"
                self.sems[key] = st.enter_context(nc.semaphore(key))
                ring.append(DSem(key))
            self.dring[qn] = ring
            self.dri[qn] = 0

    def _collect(self, qn, reads, writes):
        q = self.q[qn]
        waits = {}

        def need(s, v, raw):
            if s == q.sem and (qn == "pe" or not raw):
                return
            if v > waits.get(s, 0):
                waits[s] = v

        for b in reads:
            if b.w is not None:
                need(b.w[0], b.w[1], True)
        for b in writes:
            if b.w is not None:
                need(b.w[0], b.w[1], False)
            for s, v in b.r.items():
                need(s, v, False)
        out = []
        for s, v in waits.items():
            if v > q.waited.get(s, 0):
                q.waited[s] = v
                out.append((s, v))
        return out

    def _commit(self, tok, reads, writes):
        s, v = tok
        for b in reads:
            if v > b.r.get(s, 0):
                b.r[s] = v
        for b in writes:
            b.w = tok
            b.r = {}

    def op(self, qn, fn, reads=(), writes=()):
        q = self.q[qn]
        waits = self._collect(qn, reads, writes)
        q.n += 1
        tok = (q.sem, q.n)
        q.ops.append((waits, fn, tok, 1))
        self._commit(tok, reads, writes)
        return tok

    def mm(self, out_ap, pairs, reads, writes):
        q = self.q["pe"]
        waits = self._collect("pe", reads, writes)
        n = len(pairs)
        tok = None
        for i, (l, r) in enumerate(pairs):
            def fn(e, l=l, r=r, i=i):
                return e.matmul(out_ap, lhsT=l, rhs=r, start=(i == 0), stop=(i == n - 1))
            if i == n - 1:
                q.n += 1
                tok = (q.sem, q.n)
            q.ops.append((waits if i == 0 else [], fn, tok if i == n - 1 else None, 1))
        self._commit(tok, reads, writes)
        return tok

    def dma(self, qn, out, in_, reads=(), writes=()):
        q = self.q[qn]
        ring = self.dring[qn]
        ent = ring[self.dri[qn] % len(ring)]
        self.dri[qn] += 1
        waits = self._collect(qn, reads, writes)
        if ent.count > 0 and 16 * ent.count > q.waited.get(ent.key, 0):
            q.waited[ent.key] = 16 * ent.count
            waits.append((ent.key, 16 * ent.count))
        ent.count += 1
        tok = (ent.key, 16 * ent.count)
        q.ops.append((waits, lambda e: e.dma_start(out=out, in_=in_), tok, 16))
        self._commit(tok, reads, writes)
        return tok

    def final_wait(self, qn, toks):
        q = self.q[qn]
        waits = []
        for s, v in toks.items():
            if v > q.waited.get(s, 0):
                q.waited[s] = v
                waits.append((s, v))
        q.ops.append((waits, None, None, 0))

    def final_all(self):
        toks = {}
        for q in self.q.values():
            if q.n > 0:
                toks[q.sem] = q.n
        for ring in self.dring.values():
            for ent in ring:
                if ent.count > 0:
                    toks[ent.key] = 16 * ent.count
        self.final_wait("sp", toks)

    def replay(self, qn, e):
        for waits, fn, tok, inc in self.q[qn].ops:
            for s, v in waits:
                e.wait_ge(self.sems[s], v)
            if fn is None:
                continue
            ins = fn(e)
            if tok is not None:
                ins.then_inc(self.sems[tok[0]], inc)


def build_nc():
    nc = bass.Bass("TRN2", target_bir_lowering=False)

    def din(name, shape, dt=F32):
        return nc.dram_tensor(name, list(shape), dt, kind="ExternalInput").ap()

    hT0 = din("hT0", [1024, NT])
    w_in = din("w_in", [1024, 8192])
    w_co = din("w_conv_out", [1024, 1024])
    w_ao = din("w_attn_out", [1024, 1024])
    w_mix = din("w_mix_out", [1024, 1024])
    w_up = din("w_ffn_up", [1024, 5632])
    w_dn = din("w_ffn_down", [2816, 1024])
    small_d = din("small", [128, NSMALL])
    ident_d = din("ident", [128, 128])
    augq_d = din("augq", [8, 4, NT], BF16)
    augk_d = din("augk", [8, 4, NT], BF16)
    masks_d = din("masks", [128, 4 * 512])
    outT = nc.dram_tensor("outT", [1024, 4096], F32, kind="ExternalOutput").ap()
    skind = "ExternalOutput" if DEBUG else "Internal"
    m1_scr = nc.dram_tensor("m1_scr", [8, 128, NT], F32, kind=skind).ap()
    sgb_scr = nc.dram_tensor("sgb_scr", [8, 128, NT], F32, kind=skind).ap()
    o_scr = nc.dram_tensor("o_scr", [8, 128, NT], BF16, kind=skind).ap()
    h1_scr = nc.dram_tensor("h1_scr", [NJ, 128, 8 * 512], F32, kind=skind).ap()
    hid_scr = nc.dram_tensor("hid_scr", [NJ, 128, 22 * 512], BF16, kind=skind).ap()
    if DEBUG:
        dbgA = nc.dram_tensor("dbgA", [128, 8 * NT], BF16, kind="ExternalOutput").ap()
        dbgB = nc.dram_tensor("dbgB", [128, 8 * NT], BF16, kind="ExternalOutput").ap()
        dbgC = nc.dram_tensor("dbgC", [128, 8 * NT], BF16, kind="ExternalOutput").ap()

    hT0_v = hT0.rearrange("(c p) t -> p c t", p=128)
    outT_v = outT.rearrange("(c p) t -> p c t", p=128)

    def wsrc(w, kc0, kc, col0):
        return w.rearrange("(k p) f -> p k f", p=128)[:, kc0:kc0 + kc, col0:col0 + 128]

    off = [0]

    def A(n):
        o = off[0]
        off[0] += (n + 31) // 32 * 32
        return o

    SMALL = A(NSMALL * 4)
    IDENT = A(128 * 4)
    ONES = A(128 * 2)
    MISC = A(64 * 4)
    IDENTB = A(128 * 2)
    BIGA = A(8 * NT * 2)
    BIGB = A(8 * NT * 2)
    WST = A(4 * 4096)
    WBF = A(8 * 2048)
    TB = A(44 * 1024)
    TOTAL = off[0]

    st = ExitStack()
    with st:
        pool_t = st.enter_context(nc.sbuf_tensor("pool", [128, TOTAL // 2], BF16))
        psb = [st.enter_context(nc.psum_tensor(f"ps{i}", [128, 512], F32)) for i in range(8)]
        P = Prog(nc, st)
        block = st.enter_context(nc.Block())

        def V(offb, dt, n, p0=0, p1=128):
            e0 = offb // 2
            if dt == BF16:
                return pool_t[p0:p1, e0:e0 + n]
            return pool_t[p0:p1, e0:e0 + 2 * n].bitcast(F32)

        pbuf = [Buf() for _ in range(8)]

        def PS(i, w=512, p0=0, p1=128, c0=0):
            return psb[i][p0:p1, c0:c0 + w]

        small = V(SMALL, F32, NSMALL)
        small_b = Buf()
        ident = V(IDENT, F32, 128)
        ident_b = Buf()
        ones = V(ONES, BF16, 128)
        ones_b = Buf()
        misc = V(MISC, F32, 64)
        misc_b = Buf()

        def sc(col, p0=0, p1=128):
            return small[p0:p1, col:col + 1]

        def mcol(col):
            return misc[:, col:col + 1]

        class StopBuild(Exception):
            pass

        def check_sub(k):
            if STOP_AFTER == 5 and k > SUB:
                raise StopBuild()

        def check_stop(n):
            if n > STOP_AFTER:
                raise StopBuild()

        def body():
            P.dma("sp", small, small_d, writes=[small_b])
            P.dma("sp", ident, ident_d, writes=[ident_b])
            P.op("pool", lambda e: e.memset(ones, 1.0), writes=[ones_b])
            identb = V(IDENTB, BF16, 128)
            identb_b = Buf()
            P.op("pool", lambda e: e.tensor_copy(out=identb, in_=ident), reads=[ident_b], writes=[identb_b])
            lamtmp = V(TB, F32, 64)
            lamtmp_b = Buf()
            P.op("dve", lambda e: e.tensor_tensor(out=lamtmp, in0=small[:, C_LQ1:C_LQ1 + 64], in1=small[:, C_LK1:C_LK1 + 64], op=ALU.mult),
                 reads=[small_b], writes=[lamtmp_b])
            P.op("dve", lambda e: e.reduce_sum(out=mcol(0), in_=lamtmp, axis=AX.X), reads=[lamtmp_b], writes=[misc_b])
            P.op("dve", lambda e: e.tensor_tensor(out=lamtmp, in0=small[:, C_LQ2:C_LQ2 + 64], in1=small[:, C_LK2:C_LK2 + 64], op=ALU.mult),
                 reads=[small_b, misc_b], writes=[lamtmp_b])
            P.op("dve", lambda e: e.reduce_sum(out=mcol(1), in_=lamtmp, axis=AX.X), reads=[lamtmp_b], writes=[misc_b])
            P.op("act", lambda e: e.activation(out=misc[:, 0:2], in_=misc[:, 0:2], func=AF.Exp), reads=[misc_b], writes=[misc_b])
            P.op("dve", lambda e: e.tensor_tensor(out=mcol(2), in0=mcol(0), in1=mcol(1), op=ALU.subtract), reads=[misc_b], writes=[misc_b])
            P.op("dve", lambda e: e.tensor_scalar(out=mcol(3), in0=mcol(2), scalar1=LAM_INIT, scalar2=-1.0, op0=ALU.add, op1=ALU.mult),
                 reads=[misc_b], writes=[misc_b])
            P.op("dve", lambda e: e.tensor_scalar(out=mcol(4), in0=sc(C_SUBG), scalar1=1.0 - LAM_INIT, scalar2=None, op0=ALU.mult),
                 reads=[misc_b, small_b], writes=[misc_b])
            neglam = mcol(3)
            gsub = mcol(4)

            def bigv(base, c, t0, w, p0=0, p1=128):
                return V(base + (c * NT + t0) * 2, BF16, w, p0, p1)

            wst_b = [Buf() for _ in range(4)]
            wbf_b = [Buf() for _ in range(8)]
            wctr = [0, 0]

            def load_w(src, kc, dst, dst_b, eng="pool"):
                s = wctr[0] % 4
                wctr[0] += 1
                stg = V(WST + s * 4096, F32, kc * 128)
                stg3 = stg.rearrange("p (k f) -> p k f", k=kc)
                P.dma("sp", stg3, src, writes=[wst_b[s]])
                if eng == "pool":
                    P.op("pool", lambda e: e.tensor_copy(out=dst, in_=stg), reads=[wst_b[s]], writes=[dst_b])
                else:
                    P.op("act", lambda e: e.activation(out=dst, in_=stg, func=AF.Copy), reads=[wst_b[s]], writes=[dst_b])

            def ring_slot():
                s = wctr[1] % 8
                wctr[1] += 1
                return V(WBF + s * 2048, BF16, 1024), wbf_b[s]

            def wk(wv, k):
                return wv[:, k * 128:(k + 1) * 128]

            wplan = []
            wnext = [0]

            def plan_ring(srcs):
                unit = []
                views = []
                for src in srcs:
                    unit.append([src, 8, None, None])
                    views.append(None)
                wplan.append(unit)
                return len(wplan) - 1

            wviews = {}

            cast_eng = ["act"]

            def prefetch(upto):
                while wnext[0] < min(upto, len(wplan)):
                    u = wnext[0]
                    vs = []
                    for ent in wplan[u]:
                        src, kc, dst, dstb = ent
                        if dst is None:
                            dst, dstb = ring_slot()
                        load_w(src, kc, dst, dstb, eng=cast_eng[0])
                        vs.append((dst, dstb))
                    wviews[u] = vs
                    wnext[0] += 1

            U_P1 = [plan_ring([wsrc(w_in, 0, 8, c * 128), wsrc(w_in, 0, 8, 1024 + c * 128), wsrc(w_in, 0, 8, 2048 + c * 128)]) for c in range(8)]
            U_P2 = [plan_ring([wsrc(w_co, 0, 8, f * 128), wsrc(w_in, 0, 8, 6144 + f * 128), wsrc(w_in, 0, 8, 7168 + f * 128)]) for f in range(8)]
            U_P3 = [plan_ring([wsrc(w_in, 0, 8, 3072 + h * 128), wsrc(w_in, 0, 8, 4096 + h * 128), wsrc(w_in, 0, 8, 5120 + h * 128)]) for h in range(8)]
            U_P4 = [plan_ring([wsrc(w_ao, 0, 8, f * 128)]) for f in range(8)]
            U_P5 = plan_ring([wsrc(w_mix, 0, 8, f * 128) for f in range(8)])
            U_P6 = [plan_ring([wsrc(w_up, 0, 8, i * 128), wsrc(w_up, 0, 8, 2816 + i * 128)]) for i in range(22)]
            wdn_b = [Buf() for _ in range(8)]
            unit = []
            for fo in range(8):
                for (k0, kc) in ((0, 8), (8, 8), (16, 6)):
                    dst = V(BIGB + (fo * 22 + k0) * 128 * 2, BF16, kc * 128)
                    unit.append([wsrc(w_dn, k0, kc, fo * 128), kc, dst, wdn_b[fo]])
            wplan.append(unit)
            U_P7 = len(wplan) - 1

            def wdn(fo, k):
                return V(BIGB + (fo * 22 + k) * 128 * 2, BF16, 128)

            aT_b = [[Buf() for _ in range(NJ)] for _ in range(8)]
            zT_b = [[Buf() for _ in range(NJ)] for _ in range(8)]
            Tbufs = [lamtmp_b]

            def new_T(n):
                nonlocal Tbufs
                after = merged(Tbufs)
                bs = [Buf(after) for _ in range(n)]
                Tbufs = bs
                return bs

            check_stop(0)
            prefetch(2)
            tb = new_T(6)
            hin_b, sq_b, sqv_b, rstd_b = tb[0:2], tb[2], tb[3], tb[4]
            HIN = [TB, TB + 16384]
            SQ = TB + 32768
            SQV = TB + 40960
            RSTD = TB + 43008

            def norm_stats(sq_view_fn, nchunks, w, bank, sq_buf, scale, sqv, sqv_b_, rstd, rstd_b_):
                P.mm(PS(bank, w), [(ones, sq_view_fn(c)) for c in range(nchunks)], reads=[ones_b, sq_buf], writes=[pbuf[bank]])
                P.op("act", lambda e: e.activation(out=sqv, in_=PS(bank, w), func=AF.Ln, scale=scale, bias=epsb),
                     reads=[pbuf[bank], misc_b], writes=[sqv_b_])
                P.op("act", lambda e: e.activation(out=rstd, in_=sqv, func=AF.Exp, scale=-0.5), reads=[sqv_b_], writes=[rstd_b_])

            P.op("dve", lambda e: e.memset(mcol(5), EPS), writes=[misc_b])
            epsb = mcol(5)

            for j in range(NJ):
                t0, w = trng(j)
                hb = hin_b[j % 2]
                hin = V(HIN[j % 2], F32, 8 * w)
                P.dma("sp", hin.rearrange("p (c t) -> p c t", c=8), hT0_v[:, :, t0:t0 + w], writes=[hb])
                sq = V(SQ, BF16, 8 * w)
                P.op("act", lambda e, sq=sq, hin=hin: e.activation(out=sq, in_=hin, func=AF.Square), reads=[hb], writes=[sq_b])
                sqv = V(SQV, F32, w)
                rstd = V(RSTD, F32, w)
                norm_stats(lambda c, sq=sq, w=w: sq[:, c * w:(c + 1) * w], 8, w, j % 2, sq_b, 1.0 / 1024, sqv, sqv_b, rstd, rstd_b)
                for c in range(8):
                    P.op("dve", lambda e, c=c, hin=hin, w=w, t0=t0, rstd=rstd: e.scalar_tensor_tensor(
                        out=bigv(BIGA, c, t0, w), in0=hin[:, c * w:(c + 1) * w], scalar=sc(C_G1 + c), in1=rstd, op0=ALU.mult, op1=ALU.mult),
                        reads=[hb, rstd_b, small_b], writes=[aT_b[c][j]])

            check_stop(1)
            tb = new_T(2 + 2 + 1 + 2 * NJ)
            cc_b, yb1_b = tb[0:2], tb[2:4]
            ubz_b = tb[4]
            ub_b = [tb[5:5 + NJ], tb[5 + NJ:5 + 2 * NJ]]
            CC = [TB, TB + 2048]
            YB1 = [TB + 4096, TB + 6144]
            UB = [TB + 16384, TB + 16384 + 8256]

            def dgv(base, i):
                return V(base + i * 256, BF16, 128)

            for i in range(2):
                P.op("pool", lambda e, i=i: e.memset(V(UB[i], BF16, 2), 0.0), writes=[ubz_b])

            items = [(c, j) for c in range(8) for j in range(NJ)]
            CBK = [0, 3, 6]

            def p1_A(idx):
                c, j = items[idx]
                t0, w = trng(j)
                if j == 0:
                    prefetch(U_P1[c] + 2)
                (wcb, wcb_b), (wcc, wcc_b), (wcx, wcx_b) = wviews[U_P1[c]]
                bcb = CBK[idx % 3]
                bcc = 1 + 3 * (idx % 2)
                bcx = 2 + 3 * (idx % 2)
                rd = [aT_b[k][j] for k in range(8)]
                P.mm(PS(bcb, w), [(wk(wcb, k), bigv(BIGA, k, t0, w)) for k in range(8)], reads=rd + [wcb_b], writes=[pbuf[bcb]])
                P.mm(PS(bcc, w), [(wk(wcc, k), bigv(BIGA, k, t0, w)) for k in range(8)], reads=rd + [wcc_b], writes=[pbuf[bcc]])
                P.mm(PS(bcx, w), [(wk(wcx, k), bigv(BIGA, k, t0, w)) for k in range(8)], reads=rd + [wcx_b], writes=[pbuf[bcx]])
                ccv = V(CC[idx % 2], F32, w)
                P.op("act", lambda e: e.activation(out=ccv, in_=PS(bcc, w), func=AF.Copy), reads=[pbuf[bcc]], writes=[cc_b[idx % 2]])
                ubv = V(UB[c % 2] + (2 + t0) * 2, BF16, w)
                P.op("dve", lambda e: e.tensor_tensor(out=ubv, in0=PS(bcx, w), in1=ccv, op=ALU.mult),
                     reads=[pbuf[bcx], cc_b[idx % 2]], writes=[ub_b[c % 2][j]])

            def p1_B(idx):
                c, j = items[idx]
                t0, w = trng(j)
                bcb = CBK[idx % 3]
                rd = [ub_b[c % 2][j], ubz_b] + ([ub_b[c % 2][j - 1]] if j > 0 else [])
                yb = V(YB1[idx % 2], F32, w)
                ybb = yb1_b[idx % 2]
                P.op("act", lambda e: e.activation(out=yb, in_=V(UB[c % 2] + t0 * 2, BF16, w), func=AF.Identity, scale=sc(C_CW + c), bias=0.0),
                     reads=rd + [small_b], writes=[ybb])
                P.op("dve", lambda e: e.scalar_tensor_tensor(out=yb, in0=V(UB[c % 2] + (t0 + 1) * 2, BF16, w), scalar=sc(C_CW + 8 + c), in1=yb, op0=ALU.mult, op1=ALU.add),
                     reads=rd + [ybb, small_b], writes=[ybb])
                P.op("dve", lambda e: e.scalar_tensor_tensor(out=yb, in0=V(UB[c % 2] + (t0 + 2) * 2, BF16, w), scalar=sc(C_CW + 16 + c), in1=yb, op0=ALU.mult, op1=ALU.add),
                     reads=rd + [ybb, small_b], writes=[ybb])
                P.op("dve", lambda e: e.tensor_tensor(out=bigv(BIGB, c, t0, w), in0=PS(bcb, w), in1=yb, op=ALU.mult),
                     reads=[pbuf[bcb], ybb], writes=[zT_b[c][j]])

            for idx in range(len(items) + 1):
                if idx < len(items):
                    p1_A(idx)
                if idx >= 1:
                    p1_B(idx - 1)

            if DEBUG:
                P.dma("sp", dbgA, V(BIGA, BF16, 8 * NT), reads=[b for r in aT_b for b in r])
                P.dma("sp", dbgB, V(BIGB, BF16, 8 * NT), reads=[b for r in zT_b for b in r])

            check_stop(2)
            tb = new_T(6)
            sga_b, m1o_b, sgbo_b = tb[0:2], tb[2:4], tb[4:6]
            SGA = [TB, TB + 2048]
            M1O = [TB + 4096, TB + 6144]
            SGBO = [TB + 8192, TB + 10240]
            m1s_b = [[Buf() for _ in range(NJ)] for _ in range(8)]
            sgbs_b = [[Buf() for _ in range(NJ)] for _ in range(8)]
            idx = 0
            for f in range(8):
                prefetch(U_P2[f] + 2)
                (wco, wco_b), (wga, wga_b), (wgb, wgb_b) = wviews[U_P2[f]]
                for j in range(NJ):
                    t0, w = trng(j)
                    bk = 3 * (idx % 2)
                    i2 = idx % 2
                    rda = [aT_b[k][j] for k in range(8)]
                    rdz = [zT_b[k][j] for k in range(8)]
                    P.mm(PS(bk, w), [(wk(wco, k), bigv(BIGB, k, t0, w)) for k in range(8)], reads=rdz + [wco_b], writes=[pbuf[bk]])
                    P.mm(PS(bk + 1, w), [(wk(wga, k), bigv(BIGA, k, t0, w)) for k in range(8)], reads=rda + [wga_b], writes=[pbuf[bk + 1]])
                    P.mm(PS(bk + 2, w), [(wk(wgb, k), bigv(BIGA, k, t0, w)) for k in range(8)], reads=rda + [wgb_b], writes=[pbuf[bk + 2]])
                    sga = V(SGA[i2], F32, w)
                    m1o = V(M1O[i2], F32, w)
                    sgbo = V(SGBO[i2], F32, w)
                    P.op("act", lambda e, sga=sga, bk=bk, w=w: e.activation(out=sga, in_=PS(bk + 1, w), func=AF.Sigmoid), reads=[pbuf[bk + 1]], writes=[sga_b[i2]])
                    P.op("dve", lambda e, sga=sga, bk=bk, w=w, m1o=m1o: e.tensor_tensor(out=m1o, in0=PS(bk, w), in1=sga, op=ALU.mult),
                         reads=[pbuf[bk], sga_b[i2]], writes=[m1o_b[i2]])
                    P.op("act", lambda e, sgbo=sgbo, bk=bk, w=w: e.activation(out=sgbo, in_=PS(bk + 2, w), func=AF.Sigmoid), reads=[pbuf[bk + 2]], writes=[sgbo_b[i2]])
                    P.dma("pool", m1_scr[f, :, t0:t0 + w], m1o, reads=[m1o_b[i2]], writes=[m1s_b[f][j]])
                    P.dma("pool", sgb_scr[f, :, t0:t0 + w], sgbo, reads=[sgbo_b[i2]], writes=[sgbs_b[f][j]])
                    idx += 1

            check_stop(3)
            after_B = merged([b for r in zT_b for b in r])
            QA, QB, KA, KB = BIGB, BIGB + 8224, BIGB + 2 * 8224, BIGB + 3 * 8224
            VV = BIGB + 4 * 8224
            MSK = VV + 8544
            PT = MSK + 8192
            SSB = PT + 8192
            assert SSB + 4096 <= BIGB + 8 * NT * 2
            qz_b = Buf(after_B)
            qaug_b = Buf(after_B)
            qA_b = [Buf(after_B) for _ in range(NJ)]
            qB_b = [Buf(after_B) for _ in range(NJ)]
            kA_b = [Buf(after_B) for _ in range(NJ)]
            kB_b = [Buf(after_B) for _ in range(NJ)]
            v_b = [Buf(after_B) for _ in range(9)]
            vone_b = Buf(after_B)
            msk_b = Buf(after_B)
            pt_b = [Buf(after_B) for _ in range(8)]
            ssb_b = [Buf(after_B) for _ in range(2)]
            tb = new_T(4 + 4 + 4 + 4 + 4 + 6 + 2 * NJ + 9 + 3 + NJ)
            r0_b, r1_b, t0_b, osb_b, on_b = tb[0:4], tb[4:8], tb[8:12], tb[12:16], tb[16:20]
            ssq_b, lnv_b, rstd3_b, junk_b = tb[20], tb[21], tb[22], tb[23]
            oout_b = tb[24:26]
            RS = TB
            T0S = TB + 128
            OSBS = T0S + 2048
            JUNK = OSBS + 2048
            ONS = JUNK + 512
            OOUT = [ONS + 1024, ONS + 2048]
            KA1 = TB + 8192
            KB1 = KA1 + 8224
            VV1 = KB1 + 8224
            assert VV1 + 8544 <= TB + 44 * 1024
            KAs, KBs, VVs = [KA, KA1], [KB, KB1], [VV, VV1]
            kA_bs = [kA_b, tb[26:26 + NJ]]
            kB_bs = [kB_b, tb[26 + NJ:26 + 2 * NJ]]
            v_bs = [v_b, tb[26 + 2 * NJ:26 + 2 * NJ + 9]]
            kz1_b, vone1_b, kaug1_b = tb[26 + 2 * NJ + 9], tb[26 + 2 * NJ + 10], tb[26 + 2 * NJ + 11]
            qaug_bt = [Buf(after_B) for _ in range(NJ)]
            kaug_bs = [qaug_b, kaug1_b]
            vone_bs = [vone_b, vone1_b]
            os_b = [[Buf() for _ in range(NJ)] for _ in range(8)]
            oT_b = [[Buf() for _ in range(NJ)] for _ in range(8)]

            for base in (QA, KA):
                P.op("pool", lambda e, base=base: e.memset(V(base, BF16, NT, 64, 128), 0.0), writes=[qz_b])
            for base in (QB, KB):
                P.op("pool", lambda e, base=base: e.memset(V(base, BF16, NT, 0, 64), 0.0), writes=[qz_b])
            mskv = V(MSK, F32, 4 * 512)
            P.dma("sp", mskv, masks_d, writes=[msk_b])
            P.op("pool", lambda e: e.memset(V(VV, BF16, 33 * 129).rearrange("p (b c) -> p b c", c=129)[:, :, 128:129], 1.0), writes=[vone_b])
            P.op("pool", lambda e: e.memset(V(KA1, BF16, NT, 64, 128), 0.0), writes=[kz1_b])
            P.op("pool", lambda e: e.memset(V(KB1, BF16, NT, 0, 64), 0.0), writes=[kz1_b])
            P.op("pool", lambda e: e.memset(V(VV1, BF16, 33 * 129).rearrange("p (b c) -> p b c", c=129)[:, :, 128:129], 1.0), writes=[vone1_b])

            tasks = []

            def q_task(hh, j):
                def f():
                    t0, w = trng(j)
                    wq_, wq_b_ = wviews[U_P3[hh]][0]
                    bk = sctr[0] % 4
                    sctr[0] += 1
                    P.mm(PS(bk, w), [(wk(wq_, k), bigv(BIGA, k, t0, w)) for k in range(8)], reads=[aT_b[k][j] for k in range(8)] + [wq_b_], writes=[pbuf[bk]])
                    P.op("dve", lambda e: e.tensor_scalar(out=V(QA + t0 * 2, BF16, w, 0, 64), in0=PS(bk, w, 0, 64), scalar1=0.125, scalar2=None, op0=ALU.mult),
                         reads=[pbuf[bk]], writes=[qA_b[j]])
                    P.op("dve", lambda e: e.tensor_scalar(out=V(QB + t0 * 2, BF16, w, 64, 128), in0=PS(bk, w, 64, 128), scalar1=0.125, scalar2=None, op0=ALU.mult),
                         reads=[pbuf[bk]], writes=[qB_b[j]])
                return f

            def k_task(hh, j):
                def f():
                    p = (hh + PSWAP) % 2
                    t0, w = trng(j)
                    wk_, wk_b_ = wviews[U_P3[hh]][1]
                    bk = sctr[0] % 4
                    sctr[0] += 1
                    P.mm(PS(bk, w), [(wk(wk_, k), bigv(BIGA, k, t0, w)) for k in range(8)], reads=[aT_b[k][j] for k in range(8)] + [wk_b_], writes=[pbuf[bk]])
                    P.op("dve", lambda e: e.tensor_copy(out=V(KAs[p] + t0 * 2, BF16, w, 0, 64), in_=PS(bk, w, 0, 64)), reads=[pbuf[bk]], writes=[kA_bs[p][j]])
                    P.op("dve", lambda e: e.tensor_copy(out=V(KBs[p] + t0 * 2, BF16, w, 64, 128), in_=PS(bk, w, 64, 128)), reads=[pbuf[bk]], writes=[kB_bs[p][j]])
                return f

            def v_task(hh, g):
                def f():
                    p = (hh + PSWAP) % 2
                    wv_, wv_b_ = wviews[U_P3[hh]][2]
                    bk = sctr[0] % 4
                    sctr[0] += 1
                    nblk = 4 if g < 8 else 1
                    for bi in range(nblk):
                        tbk = g * 4 + bi
                        tw = 128 if tbk < 32 else 16
                        jj = tbk // 4
                        P.mm(psb[bk][0:tw, bi * 128:(bi + 1) * 128], [(bigv(BIGA, k, tbk * 128, tw), wk(wv_, k)) for k in range(8)],
                             reads=[aT_b[k][jj] for k in range(8)] + [wv_b_], writes=[pbuf[bk]])
                    tw = 128 if g < 8 else 16
                    vdst = V(VVs[p] + g * 4 * 129 * 2, BF16, nblk * 129, 0, tw).rearrange("p (b c) -> p b c", c=129)[:, :, 0:128]
                    vsrc = psb[bk][0:tw, 0:nblk * 128].rearrange("p (b c) -> p b c", c=128)
                    P.op("dve", lambda e: e.tensor_copy(out=vdst, in_=vsrc), reads=[pbuf[bk]], writes=[v_bs[p][g]])
                return f

            def kaug_task(hh):
                def f():
                    p = (hh + PSWAP) % 2
                    zb_ = qz_b if p == 0 else kz1_b
                    P.dma("sp", V(KAs[p], BF16, NT, 64, 68), augk_d[hh], reads=[zb_], writes=[kaug_bs[p]])
                    P.dma("sp", V(KBs[p], BF16, NT, 0, 4), augk_d[hh], reads=[zb_], writes=[kaug_bs[p]])
                return f
            sctr = [0]

            cast_eng[0] = "pool"
            for h in range(8):
                prefetch(U_P3[h] + 2)
                (wq, wq_b), (wkk, wkk_b), (wv, wv_b) = wviews[U_P3[h]]
                pset = (h + PSWAP) % 2
                P.dma("sp", V(QA, BF16, NT, 64, 68), augq_d[h], reads=[qz_b], writes=[qaug_bt[0]])
                P.dma("sp", V(QB, BF16, NT, 0, 4), augq_d[h], reads=[qz_b], writes=[qaug_bt[0]])
                if h == 0:
                    kaug_task(0)()
                    for j in range(NJ):
                        q_task(0, j)()
                        k_task(0, j)()
                    for g in range(9):
                        v_task(0, g)()
                if h < 7:
                    tasks.append(kaug_task(h + 1))
                    for j in range(NJ):
                        tasks.append(k_task(h + 1, j))
                    for g in range(9):
                        tasks.append(v_task(h + 1, g))

                sweep = []
                for j in range(NJ):
                    nkb = 4 * j + 4 if j < 8 else NKB
                    for m in range(2):
                        if j < 8:
                            for kb in range(nkb):
                                sweep.append((j, m, kb, kb == nkb - 1, 1))
                        else:
                            for g_ in range(8):
                                sweep.append((j, m, 4 * g_, False, 4))
                            sweep.append((j, m, 32, True, 1))
                pending = []

                def emit_S(i):
                    j, m, kb, last, nblk = sweep[i]
                    t0, w = trng(j)
                    kw = 128 if kb < 32 else 16
                    bk = sctr[0] % 4
                    sctr[0] += 1
                    Qb, Kb = (QA, KAs[pset]) if m == 0 else (QB, KBs[pset])
                    qb_, kb_ = (qA_b, kA_bs[pset]) if m == 0 else (qB_b, kB_bs[pset])
                    xrd = [qaug_bt[0], kaug_bs[pset], qz_b] + ([kz1_b] if pset == 1 else [])
                    if nblk == 4:
                        rd = [qb_[j], kb_[kb // 4]] + xrd
                        q = P.q["pe"]
                        waits = P._collect("pe", rd, [pbuf[bk]])
                        q.n += 1
                        tok = (q.sem, q.n)
                        for b_ in range(4):
                            def fn(e, b_=b_):
                                return e.matmul(PS(bk, w, 0, 128, b_ * w), lhsT=V(Kb + (kb + b_) * 128 * 2, BF16, 128), rhs=V(Qb + t0 * 2, BF16, w),
                                                start=True, stop=True, skip_group_check=True)
                            q.ops.append((waits if b_ == 0 else [], fn, tok if b_ == 3 else None, 1))
                        P._commit(tok, rd, [pbuf[bk]])
                        ptv4 = V(PT + (i % 8) * 1024, BF16, 4 * w)
                        P.op("act", lambda e: e.activation(out=ptv4, in_=PS(bk, 4 * w), func=AF.Exp), reads=[pbuf[bk]], writes=[pt_b[i % 8]])
                        return
                    diag = (kb >= 4 * j)
                    r = kb - 4 * j if (diag and j < 8) else 0
                    c0 = 128 * r
                    wc = w - c0
                    P.mm(PS(bk, wc, 0, kw, c0), [(V(Kb + kb * 128 * 2, BF16, kw), V(Qb + (t0 + c0) * 2, BF16, wc))],
                         reads=[qb_[j], kb_[kb // 4]] + xrd, writes=[pbuf[bk]])
                    ptv = V(PT + (i % 8) * 1024 + c0 * 2, BF16, wc, 0, kw)
                    if diag:
                        ssv = V(SSB + (i % 2) * 2048 + c0 * 4, F32, wc, 0, kw)
                        P.op("dve", lambda e: e.tensor_tensor(out=ssv, in0=PS(bk, wc, 0, kw, c0), in1=V(MSK + r * 2048 + c0 * 4, F32, wc, 0, kw), op=ALU.add),
                             reads=[pbuf[bk], msk_b], writes=[ssb_b[i % 2]])
                        P.op("act", lambda e: e.activation(out=ptv, in_=ssv, func=AF.Exp), reads=[ssb_b[i % 2]], writes=[pt_b[i % 8]])
                    else:
                        P.op("act", lambda e: e.activation(out=ptv, in_=PS(bk, wc, 0, kw, c0), func=AF.Exp), reads=[pbuf[bk]], writes=[pt_b[i % 8]])

                def emit_PV(i):
                    j, m, kb, last, nblk = sweep[i]
                    t0, w = trng(j)
                    kw = 128 if kb < 32 else 16
                    r = kb - 4 * j if (kb >= 4 * j and j < 8) else 0
                    first = (kb == 0)
                    g = kb // 4
                    nqs = 4 if j < 8 else 1
                    qw = 128 if j < 8 else 16
                    if nblk == 4:
                        bank_ = pbuf[4 + 2 * m]
                        rd = [pt_b[i % 8], v_bs[pset][g], vone_bs[pset]]
                        q = P.q["pe"]
                        waits = P._collect("pe", rd, [bank_] if first else [])
                        q.n += 1
                        tok = (q.sem, q.n)
                        vbase = VVs[pset]
                        for b_ in range(4):
                            def fn(e, b_=b_, vbase=vbase):
                                return e.matmul(psb[4 + 2 * m][0:w, 0:129], lhsT=V(PT + (i % 8) * 1024 + b_ * w * 2, BF16, w),
                                                rhs=V(vbase + (kb + b_) * 129 * 2, BF16, 129), start=(first and b_ == 0), stop=False, skip_group_check=True)
                            q.ops.append((waits if b_ == 0 else [], fn, tok if b_ == 3 else None, 1))
                        P._commit(tok, rd, [bank_])
                        return
                    vblk = V(VVs[pset] + kb * 129 * 2, BF16, 129, 0, kw)
                    banks = [pbuf[4 + 2 * m], pbuf[5 + 2 * m]] if j < 8 else [pbuf[4 + 2 * m]]
                    reads = [pt_b[i % 8], v_bs[pset][g], vone_bs[pset]]
                    q = P.q["pe"]
                    waits = P._collect("pe", reads, banks if first else [])
                    fns = []
                    for qs in range(r, nqs):
                        bank = 4 + 2 * m + qs // 2
                        col = (qs % 2) * 129
                        out_ap = psb[bank][0:qw, col:col + 129]
                        lhsT = V(PT + (i % 8) * 1024 + qs * 128 * 2, BF16, qw, 0, kw)
                        st = first and (qs % 2 == 0)
                        lastq = (kb == (4 * j + qs if j < 8 else 32))

                        def fn(e, out_ap=out_ap, lhsT=lhsT, st=st, lastq=lastq):
                            return e.matmul(out_ap, lhsT=lhsT, rhs=vblk, start=st, stop=lastq, skip_group_check=True)
                        fns.append(fn)
                    q.n += 1
                    tok = (q.sem, q.n)
                    for k_, fn in enumerate(fns):
                        q.ops.append((waits if k_ == 0 else [], fn, tok if k_ == len(fns) - 1 else None, 1))
                    P._commit(tok, reads, banks)
                    if last:
                        emit_norm(i, j, m)
                        if m == 1 and h < 7:
                            tasks.append(q_task(h + 1, j))

                def emit_norm(i, j, m):
                    t0, w = trng(j)
                    nqs = 4 if j < 8 else 1
                    qw = 128 if j < 8 else 16

                    def oreg(qs, c0, c1):
                        bank = 4 + 2 * m + qs // 2
                        col = (qs % 2) * 129
                        return psb[bank][0:qw, col + c0:col + c1], pbuf[bank]

                    def rsc(c):
                        return V(RS + c * 4, F32, 1, 0, qw)

                    def T0v(qs):
                        return V(T0S + qs * 512, F32, 128, 0, qw)

                    def OSBv(qs):
                        return V(OSBS + qs * 512, F32, 128, 0, qw)

                    def ONv(qs):
                        return V(ONS + qs * 256, BF16, 128, 0, qw)
                    steps = []
                    for qs in range(nqs):
                        def f(qs=qs):
                            oap, ob_ = oreg(qs, 0, 128)
                            sap, _ = oreg(qs, 128, 129)
                            if m == 0:
                                P.op("dve", lambda e: e.reciprocal(out=rsc(qs), in_=sap), reads=[ob_], writes=[r0_b[qs]])
                                P.op("dve", lambda e: e.tensor_scalar(out=T0v(qs), in0=oap, scalar1=rsc(qs), scalar2=None, op0=ALU.mult),
                                     reads=[ob_, r0_b[qs]], writes=[t0_b[qs]])
                            else:
                                P.op("dve", lambda e: e.reciprocal(out=rsc(4 + qs), in_=sap), reads=[ob_], writes=[r1_b[qs]])
                                P.op("dve", lambda e: e.tensor_tensor(out=rsc(8 + qs), in0=rsc(4 + qs), in1=misc[0:qw, 3:4], op=ALU.mult),
                                     reads=[r1_b[qs], misc_b], writes=[r1_b[qs]])
                                P.op("dve", lambda e: e.scalar_tensor_tensor(out=OSBv(qs), in0=oap, scalar=rsc(8 + qs), in1=T0v(qs), op0=ALU.mult, op1=ALU.add),
                                     reads=[ob_, r1_b[qs], t0_b[qs]], writes=[osb_b[qs]])
                        steps.append((qs, f))
                    if m == 1:
                        def s_sumsq():
                            for qs in range(nqs):
                                P.op("dve", lambda e, qs=qs: e.scalar_tensor_tensor(out=V(JUNK, F32, 128, 0, qw), in0=OSBv(qs), scalar=1.0, in1=OSBv(qs),
                                                                                    op0=ALU.mult, op1=ALU.mult, accum_out=rsc(12 + qs)),
                                     reads=[osb_b[qs]], writes=[junk_b, ssq_b])
                        steps.append((5, s_sumsq))
                        steps.append((7, lambda: P.op("act", lambda e: e.activation(out=V(RS + 16 * 4, F32, nqs, 0, qw), in_=V(RS + 12 * 4, F32, nqs, 0, qw),
                                                                                       func=AF.Ln, scale=1.0 / 128, bias=misc[0:qw, 5:6]),
                                                      reads=[ssq_b, misc_b], writes=[lnv_b])))
                        steps.append((8, lambda: P.op("act", lambda e: e.activation(out=V(RS + 20 * 4, F32, nqs, 0, qw), in_=V(RS + 16 * 4, F32, nqs, 0, qw),
                                                                                       func=AF.Exp, scale=-0.5),
                                                      reads=[lnv_b], writes=[rstd3_b])))

                        def s_scale():
                            for qs in range(nqs):
                                P.op("dve", lambda e, qs=qs: e.tensor_scalar(out=ONv(qs), in0=OSBv(qs), scalar1=rsc(20 + qs), scalar2=None, op0=ALU.mult),
                                     reads=[osb_b[qs], rstd3_b], writes=[on_b[qs]])
                        steps.append((9, s_scale))

                        def s_out():
                            sb_ = sctr[0] % 4
                            sctr[0] += 1
                            psbf = psb[sb_][:, :].bitcast(BF16)
                            q = P.q["pe"]
                            rd = [on_b[qs] for qs in range(nqs)] + [identb_b]
                            waits = P._collect("pe", rd, [pbuf[sb_]])
                            q.n += 1
                            tok = (q.sem, q.n)
                            for qs in range(nqs):
                                def fn(e, qs=qs):
                                    return e.transpose(psbf[:, qs * 128:qs * 128 + qw], ONv(qs), identb[0:qw, 0:qw])
                                q.ops.append((waits if qs == 0 else [], fn, tok if qs == nqs - 1 else None, 1))
                            P._commit(tok, rd, [pbuf[sb_]])
                            if h < 7:
                                oo = V(OOUT[j % 2], BF16, w)
                                P.op("dve", lambda e: e.tensor_scalar(out=oo, in0=psbf[:, 0:w], scalar1=gsub, scalar2=None, op0=ALU.mult),
                                     reads=[pbuf[sb_], misc_b], writes=[oout_b[j % 2]])
                                P.dma("pool", o_scr[h, :, t0:t0 + w], oo, reads=[oout_b[j % 2]], writes=[os_b[h][j]])
                            else:
                                P.op("dve", lambda e: e.tensor_scalar(out=bigv(BIGA, 7, t0, w), in0=psbf[:, 0:w], scalar1=gsub, scalar2=None, op0=ALU.mult),
                                     reads=[pbuf[sb_], misc_b], writes=[oT_b[7][j], aT_b[7][j]])
                        steps.append((11, s_out))
                    for off_, f_ in steps:
                        pending.append((i + 3 + off_, f_))
                    pending.sort(key=lambda x: x[0])

                n = len(sweep)
                LA = 3
                for i in range(n + LA):
                    if i < n:
                        emit_S(i)
                    if i >= LA:
                        emit_PV(i - LA)
                    while pending and pending[0][0] <= i:
                        pending.pop(0)[1]()
                    if TASK_INTERLEAVE and i % 8 == 5 and tasks:
                        tasks.pop(0)()
                while pending:
                    pending.pop(0)[1]()
                while tasks:
                    tasks.pop(0)()

                if h == 7:
                    pass
                if h == 6:
                    pass

            for hh in range(7):
                P.dma("sp", bigv(BIGA, hh, 0, NT), o_scr[hh, :, :], reads=os_b[hh], writes=oT_b[hh] + aT_b[hh])

            if DEBUG:
                P.dma("sp", dbgC, V(BIGA, BF16, 8 * NT), reads=[b for r in oT_b for b in r])

            check_stop(4)
            after_B = merged([qz_b, qaug_b, msk_b, vone_b] + qaug_bt + qA_b + qB_b + kA_b + kB_b + v_b + pt_b + ssb_b)
            m2_b = [[Buf(after_B) for _ in range(NJ)] for _ in range(8)]
            NS4 = 6
            tb = new_T(2 * NS4 + 4)
            m1i_b, sgi_b, tmp_b = tb[0:NS4], tb[NS4:2 * NS4], tb[2 * NS4:2 * NS4 + 4]
            M1I = [TB + 2048 * i for i in range(NS4)]
            SGI = [TB + 2048 * (NS4 + i) for i in range(NS4)]
            TMP = [TB + 2048 * (2 * NS4 + i) for i in range(4)]
            idx = 0
            for f in range(8):
                prefetch(min(U_P4[f] + 3, U_P5))
                (wao, wao_b), = wviews[U_P4[f]]
                for j in range(NJ):
                    t0, w = trng(j)
                    bk = idx % 4
                    i3 = idx % NS4
                    m1i = V(M1I[i3], F32, w)
                    sgi = V(SGI[i3], F32, w)
                    P.dma("sp", m1i, m1_scr[f, :, t0:t0 + w], reads=[m1s_b[f][j]], writes=[m1i_b[i3]])
                    P.dma("sp", sgi, sgb_scr[f, :, t0:t0 + w], reads=[sgbs_b[f][j]], writes=[sgi_b[i3]])
                    P.mm(PS(bk, w), [(wk(wao, k), bigv(BIGA, k, t0, w)) for k in range(8)], reads=[oT_b[k][j] for k in range(8)] + [wao_b], writes=[pbuf[bk]])
                    tmp = V(TMP[idx % 4], F32, w)
                    P.op("dve", lambda e, tmp=tmp, bk=bk, w=w, sgi=sgi: e.tensor_tensor(out=tmp, in0=PS(bk, w), in1=sgi, op=ALU.mult),
                         reads=[pbuf[bk], sgi_b[i3]], writes=[tmp_b[idx % 4]])
                    P.op("pool", lambda e, tmp=tmp, m1i=m1i, f=f, t0=t0, w=w: e.tensor_tensor(out=bigv(BIGB, f, t0, w), in0=tmp, in1=m1i, op=ALU.add),
                         reads=[tmp_b[idx % 4], m1i_b[i3]], writes=[m2_b[f][j]])
                    idx += 1

            if EXTRA:
                xb = Buf(merged(Tbufs))
                for _ in range(EXTRA):
                    P.dma("sp", V(TB, F32, 4096).rearrange("p (c t) -> p c t", c=8), hT0_v[:, :, 0:512], writes=[xb])
            check_stop(5)
            prefetch(U_P5 + 1)
            wmix = wviews[U_P5]
            tb = new_T(16 + 8 + 2)
            mix_b = [tb[0:8], tb[8:16]]
            msq0_b = tb[16:24]
            sqv5_b, rstd5_b = tb[24], tb[25]
            after_wst = merged(wst_b)
            h0_b = [Buf(after_wst) for _ in range(8)]
            MIX = [TB, TB + 16384]
            MSQ0, SQV5, RSTD5 = TB + 32768, TB + 40960, TB + 43008
            H0 = WST
            h1s_b = [Buf() for _ in range(NJ)]
            fT_b = [[Buf() for _ in range(NJ)] for _ in range(8)]

            def p5_A(j, part):
                t0, w = trng(j)
                p = j % 2
                mixv = V(MIX[p], F32, 8 * w)
                for fo in range(4 * part, 4 * part + 4):
                    bk = fo % 4
                    wm, wm_b = wmix[fo]
                    P.mm(PS(bk, w), [(wk(wm, k), bigv(BIGB, k, t0, w)) for k in range(8)], reads=[m2_b[k][j] for k in range(8)] + [wm_b], writes=[pbuf[bk]])
                    P.op("dve", lambda e, fo=fo, bk=bk: e.tensor_copy(out=mixv[:, fo * w:(fo + 1) * w], in_=PS(bk, w)),
                         reads=[pbuf[bk]], writes=[mix_b[p][fo]])
                    if j == 0:
                        dst, dstb = V(MSQ0 + fo * w * 2, BF16, w), msq0_b[fo]
                    else:
                        pt0, _ = trng(j - 1)
                        dst, dstb = bigv(BIGB, fo, pt0, w), m2_b[fo][j - 1]
                    P.op("act", lambda e, fo=fo, dst=dst: e.activation(out=dst, in_=mixv[:, fo * w:(fo + 1) * w], func=AF.Square),
                         reads=[mix_b[p][fo]], writes=[dstb])

            def p5_B(j, half):
                t0, w = trng(j)
                p = j % 2
                mixv = V(MIX[p], F32, 8 * w)
                h0 = V(H0, F32, 8 * w)
                sqv = V(SQV5, F32, w)
                rstd = V(RSTD5, F32, w)
                sq2 = [(V(MSQ0 + c * w * 2, BF16, w), msq0_b[c]) for c in range(8)]
                if half == 1:
                    P.mm(PS(7, w), [(ones, v_) for v_, _ in sq2], reads=[ones_b] + [b_ for _, b_ in sq2], writes=[pbuf[7]])
                    P.op("act", lambda e: e.activation(out=sqv, in_=PS(7, w), func=AF.Ln, scale=1.0 / 1024, bias=epsb), reads=[pbuf[7], misc_b], writes=[sqv5_b])
                    P.op("act", lambda e: e.activation(out=rstd, in_=sqv, func=AF.Exp, scale=-0.5), reads=[sqv5_b], writes=[rstd5_b])
                    for c in range(8):
                        P.op("dve", lambda e, c=c: e.scalar_tensor_tensor(
                            out=bigv(BIGA, c, t0, w), in0=h0[:, c * w:(c + 1) * w], scalar=sc(C_GFPRE + c), in1=rstd, op0=ALU.mult, op1=ALU.mult),
                            reads=[h0_b[c], rstd5_b, small_b], writes=[fT_b[c][j], oT_b[c][j]])
                    return
                if j == 0:
                    sqs = [(V(MSQ0 + c * w * 2, BF16, w), msq0_b[c]) for c in range(8)]
                else:
                    pt0, _ = trng(j - 1)
                    sqs = [(bigv(BIGB, c, pt0, w), m2_b[c][j - 1]) for c in range(8)]
                P.mm(PS(6, w), [(ones, v_) for v_, _ in sqs], reads=[ones_b] + [b_ for _, b_ in sqs], writes=[pbuf[6]])
                P.op("act", lambda e: e.activation(out=sqv, in_=PS(6, w), func=AF.Ln, scale=1.0 / 1024, bias=epsb), reads=[pbuf[6], misc_b], writes=[sqv5_b])
                P.op("act", lambda e: e.activation(out=rstd, in_=sqv, func=AF.Exp, scale=-0.5), reads=[sqv5_b], writes=[rstd5_b])
                for fo in range(8):
                    P.op("dve", lambda e, fo=fo: e.scalar_tensor_tensor(
                        out=mixv[:, fo * w:(fo + 1) * w], in0=mixv[:, fo * w:(fo + 1) * w], scalar=sc(C_GPOST + fo), in1=rstd, op0=ALU.mult, op1=ALU.mult),
                        reads=[mix_b[p][fo], rstd5_b, small_b], writes=[mix_b[p][fo]])
                for fo in range(8):
                    eng = "pool" if fo % 2 == 0 else "dve"
                    P.op(eng, lambda e, fo=fo: e.tensor_tensor(out=h0[:, fo * w:(fo + 1) * w], in0=h0[:, fo * w:(fo + 1) * w], in1=mixv[:, fo * w:(fo + 1) * w], op=ALU.add),
                         reads=[h0_b[fo], mix_b[p][fo]], writes=[h0_b[fo]])
                    P.op("act", lambda e, fo=fo: e.activation(out=sq2[fo][0], in_=h0[:, fo * w:(fo + 1) * w], func=AF.Square),
                         reads=[h0_b[fo]], writes=[sq2[fo][1]])
                P.dma("pool", h1_scr[j, :, 0:8 * w], h0, reads=h0_b, writes=[h1s_b[j]])

            for j in range(NJ + 1):
                if j < NJ:
                    p5_A(j, 0)
                if j >= 1:
                    p5_B(j - 1, 0)
                if j < NJ:
                    p5_A(j, 1)
                if j >= 1:
                    p5_B(j - 1, 1)
                if j < NJ:
                    t0, w = trng(j)
                    P.dma("sp", V(H0, F32, 8 * w).rearrange("p (c t) -> p c t", c=8), hT0_v[:, :, t0:t0 + w], writes=h0_b)
            after_h0 = merged(h0_b)
            for b in wst_b:
                for k_, v_ in after_h0.items():
                    if v_ > b.r.get(k_, 0):
                        b.r[k_] = v_

            check_stop(6)
            tb = new_T(2 + 2 + 2 + 2 + 1 + 4 * NJ)
            dg6_b, sg6_b, hid_b, yb_b, ubz6_b = tb[0:2], tb[2:4], tb[4:6], tb[6:8], tb[8]
            ug_b = [tb[9:9 + NJ], tb[9 + NJ:9 + 2 * NJ]]
            uu_b = [tb[9 + 2 * NJ:9 + 3 * NJ], tb[9 + 3 * NJ:9 + 4 * NJ]]
            UG = [TB, TB + 8256]
            UU = [TB + 2 * 8256, TB + 3 * 8256]
            DG6 = [TB + 4 * 8256, TB + 4 * 8256 + 768]
            SG6 = [TB + 4 * 8256 + 1536, TB + 4 * 8256 + 1536 + 2048]
            YB = [TB + 4 * 8256 + 1536 + 4096, TB + 4 * 8256 + 1536 + 6144]
            HID = [TB + 4 * 8256 + 1536 + 8192 + i * 1024 for i in range(2)]
            assert HID[1] + 1024 <= TB + 44 * 1024
            hids_b = [[Buf() for _ in range(NJ)] for _ in range(22)]
            for base in UG + UU:
                P.op("pool", lambda e, base=base: e.memset(V(base, BF16, 2), 0.0), writes=[ubz6_b])
            items6 = [(i, j) for i in range(22) for j in range(NJ)]
            after_m2 = merged([b for r in m2_b for b in r])
            for b in wdn_b:
                b.r = dict(after_m2)
            UBK = [2, 3, 6]

            wd_left = list(wplan[U_P7])

            def p6_A(idx):
                i, j = items6[idx]
                t0, w = trng(j)
                if j == 0:
                    prefetch(min(U_P6[i] + 2, U_P7))
                    for s in range(3):
                        P.op("dve", lambda e, i=i, s=s: e.tensor_scalar(out=dgv(DG6[i % 2], s), in0=ident, scalar1=sc(C_FCW + s * 44 + i), scalar2=None, op0=ALU.mult),
                             reads=[ident_b, small_b], writes=[dg6_b[i % 2]])
                if idx >= 12 * NJ and wd_left:
                    src_, kc_, dst_, dstb_ = wd_left.pop(0)
                    load_w(src_, kc_, dst_, dstb_, eng="act")
                (wg, wg_b), (wu, wu_b) = wviews[U_P6[i]]
                bg = idx % 2
                bu = UBK[idx % 3]
                rd = [fT_b[k][j] for k in range(8)]
                P.mm(PS(bg, w), [(wk(wg, k), bigv(BIGA, k, t0, w)) for k in range(8)], reads=rd + [wg_b], writes=[pbuf[bg]])
                P.mm(PS(bu, w), [(wk(wu, k), bigv(BIGA, k, t0, w)) for k in range(8)], reads=rd + [wu_b], writes=[pbuf[bu]])
                P.op("act", lambda e: e.activation(out=V(UG[i % 2] + (2 + t0) * 2, BF16, w), in_=PS(bg, w), func=AF.Copy), reads=[pbuf[bg]], writes=[ug_b[i % 2][j]])
                P.op("act", lambda e: e.activation(out=V(UU[i % 2] + (2 + t0) * 2, BF16, w), in_=PS(bu, w), func=AF.Copy), reads=[pbuf[bu]], writes=[uu_b[i % 2][j]])

            def p6_B(idx):
                i, j = items6[idx]
                t0, w = trng(j)
                bk = 4 + idx % 2
                bu = UBK[idx % 3]
                rdg = [ug_b[i % 2][j], dg6_b[i % 2], ubz6_b] + ([ug_b[i % 2][j - 1]] if j > 0 else [])
                rdu = [uu_b[i % 2][j], ubz6_b] + ([uu_b[i % 2][j - 1]] if j > 0 else [])
                P.mm(PS(bk, w), [(dgv(DG6[i % 2], s), V(UG[i % 2] + (t0 + s) * 2, BF16, w)) for s in range(3)], reads=rdg, writes=[pbuf[bk]])
                yb = V(YB[idx % 2], F32, w)
                ybb = yb_b[idx % 2]
                P.op("act", lambda e: e.activation(out=yb, in_=V(UU[i % 2] + t0 * 2, BF16, w), func=AF.Identity, scale=sc(C_FCW + 22 + i), bias=sc(C_FCB + 22 + i)),
                     reads=rdu + [small_b], writes=[ybb])
                P.op("dve", lambda e: e.scalar_tensor_tensor(out=yb, in0=V(UU[i % 2] + (t0 + 1) * 2, BF16, w), scalar=sc(C_FCW + 44 + 22 + i), in1=yb, op0=ALU.mult, op1=ALU.add),
                     reads=rdu + [ybb, small_b], writes=[ybb])
                sg = V(SG6[idx % 2], F32, w)
                P.op("act", lambda e: e.activation(out=sg, in_=PS(bk, w), func=AF.Silu, bias=sc(C_FCB + i)), reads=[pbuf[bk], small_b], writes=[sg6_b[idx % 2]])
                P.op("dve", lambda e: e.scalar_tensor_tensor(out=yb, in0=PS(bu, w), scalar=sc(C_FCW + 88 + 22 + i), in1=yb, op0=ALU.mult, op1=ALU.add),
                     reads=[pbuf[bu], ybb, small_b], writes=[ybb])
                hv = V(HID[idx % 2], BF16, w)
                P.op("dve", lambda e: e.tensor_tensor(out=hv, in0=yb, in1=sg, op=ALU.mult),
                     reads=[ybb, sg6_b[idx % 2]], writes=[hid_b[idx % 2]])
                P.dma("pool", hid_scr[j, :, i * w:(i + 1) * w], hv, reads=[hid_b[idx % 2]], writes=[hids_b[i][j]])

            for idx in range(len(items6) + 1):
                if idx < len(items6):
                    p6_A(idx)
                if idx >= 1:
                    p6_B(idx - 1)

            check_stop(7)
            assert not wd_left and wnext[0] == U_P7
            wnext[0] = U_P7 + 1
            tb = new_T(2)
            hd_b = tb[0:2]
            HD = [TB, TB + 22528]
            after_A = merged([b for r in fT_b for b in r])
            after_W = merged(wbf_b)
            ysb_b = [Buf(after_A), Buf(after_A)]
            h1b_b = [Buf(after_A), Buf(after_A)]
            ysq_b = [Buf(after_m2), Buf(after_m2)]
            sqv7_b = [Buf(after_W), Buf(after_W)]
            rstd7_b = [Buf(after_W), Buf(after_W)]
            YSB = [BIGA, BIGA + 16384]
            H1B = [BIGA + 32768, BIGA + 49152]
            YSQ = [BIGB + 45056, BIGB + 45056 + 8192]
            SQV7 = [WBF, WBF + 2048]
            RSTD7 = [WBF + 4096, WBF + 6144]
            out_toks = {}

            def p7_A(j, part):
                t0, w = trng(j)
                p = j % 2
                hd = V(HD[p], BF16, 22 * w)
                h1b = V(H1B[p], F32, 8 * w)
                ysb = V(YSB[p], F32, 8 * w)
                ysq = V(YSQ[p], BF16, 8 * w)
                if part == 0:
                    P.dma("sp", hd, hid_scr[j, :, 0:22 * w], reads=[hids_b[i][j] for i in range(22)], writes=[hd_b[p]])
                    P.dma("sp", h1b, h1_scr[j, :, 0:8 * w], reads=[h1s_b[j]], writes=[h1b_b[p]])
                for fo in range(4 * part, 4 * part + 4):
                    bk = fo % 4
                    P.mm(PS(bk, w), [(wdn(fo, k), hd[:, k * w:(k + 1) * w]) for k in range(22)], reads=[hd_b[p], wdn_b[fo]], writes=[pbuf[bk]])
                    P.op("dve", lambda e, fo=fo, bk=bk: e.tensor_copy(out=ysb[:, fo * w:(fo + 1) * w], in_=PS(bk, w)),
                         reads=[pbuf[bk]], writes=[ysb_b[p]])
                    P.op("act", lambda e, fo=fo: e.activation(out=ysq[:, fo * w:(fo + 1) * w], in_=ysb[:, fo * w:(fo + 1) * w], func=AF.Square),
                         reads=[ysb_b[p]], writes=[ysq_b[p]])

            def p7_B(j):
                t0, w = trng(j)
                p = j % 2
                h1b = V(H1B[p], F32, 8 * w)
                ysb = V(YSB[p], F32, 8 * w)
                ysq = V(YSQ[p], BF16, 8 * w)
                sqv = V(SQV7[p], F32, w)
                rstd = V(RSTD7[p], F32, w)
                norm_stats(lambda c: ysq[:, c * w:(c + 1) * w], 8, w, 6 + p, ysq_b[p], 1.0 / 1024, sqv, sqv7_b[p], rstd, rstd7_b[p])
                for fo in range(8):
                    P.op("dve", lambda e, fo=fo: e.scalar_tensor_tensor(
                        out=ysb[:, fo * w:(fo + 1) * w], in0=ysb[:, fo * w:(fo + 1) * w], scalar=sc(C_GFPOST + fo), in1=rstd, op0=ALU.mult, op1=ALU.mult),
                        reads=[ysb_b[p], rstd7_b[p], small_b], writes=[ysb_b[p]])
                for fo in range(8):
                    P.op("pool", lambda e, fo=fo: e.tensor_tensor(out=h1b[:, fo * w:(fo + 1) * w], in0=h1b[:, fo * w:(fo + 1) * w], in1=ysb[:, fo * w:(fo + 1) * w], op=ALU.add),
                         reads=[h1b_b[p], ysb_b[p]], writes=[h1b_b[p]])
                h1b3 = h1b.rearrange("p (c t) -> p c t", c=8)
                if j == 0:
                    tok = P.dma("pool", outT_v[:, :, 0:496], h1b3[:, :, 16:512], reads=[h1b_b[p]])
                else:
                    tok = P.dma("pool", outT_v[:, :, t0 - 16:t0 - 16 + w], h1b3, reads=[h1b_b[p]])
                out_toks[tok[0]] = max(out_toks.get(tok[0], 0), tok[1])

            for j in range(NJ + 1):
                if j < NJ:
                    p7_A(j, 0)
                if j >= 1:
                    p7_B(j - 1)
                if j < NJ:
                    p7_A(j, 1)
            P.final_wait("sp", out_toks)

        try:
            body()
        except StopBuild:
            P.final_all()

        @block.sync
        def _(e):
            P.replay("sp", e)

        @block.gpsimd
        def _(e):
            P.replay("pool", e)

        @block.tensor
        def _(e):
            P.replay("pe", e)

        @block.scalar
        def _(e):
            P.replay("act", e)

        @block.vector
        def _(e):
            P.replay("dve", e)
    return nc


def host_consts():
    ident = np.eye(128, dtype=np.float32)
    pos = np.arange(NT)
    augq = np.zeros((8, 4, NT), np.float32)
    augk = np.zeros((8, 4, NT), np.float32)
    for h in range(8):
        slope = 2.0 ** (-(h + 1))
        augq[h, 0] = -slope * ((pos >> 8) << 8)
        augq[h, 1] = -slope * (pos & 255)
        augq[h, 2] = 1.0
        augq[h, 3] = 1.0
        augk[h, 0] = 1.0
        augk[h, 1] = 1.0
        augk[h, 2] = slope * ((pos >> 7) << 7)
        augk[h, 3] = slope * (pos & 127)
    ki = np.arange(128)[:, None]
    qi = np.arange(512)[None, :]
    masks = np.zeros((128, 4, 512), np.float32)
    for r in range(4):
        masks[:, r, :] = np.where(128 * r + ki <= qi, 0.0, -30000.0)
    return ident, augq.astype(ml_dtypes.bfloat16), augk.astype(ml_dtypes.bfloat16), masks.reshape(128, 2048)


def pack_small(inp):
    s = np.zeros((128, NSMALL), np.float32)

    def g(v):
        return np.asarray(v, np.float32).reshape(8, 128).T

    s[:, C_G1:C_G1 + 8] = g(inp["norm_mix_pre"][0])
    s[:, C_GPOST:C_GPOST + 8] = g(inp["norm_mix_post"][0])
    s[:, C_GFPRE:C_GFPRE + 8] = g(inp["norm_ffn_pre"][0])
    s[:, C_GFPOST:C_GFPOST + 8] = g(inp["norm_ffn_post"][0])
    s[:, C_CW:C_CW + 24] = np.asarray(inp["conv_w"][0], np.float32).reshape(3, 8, 128).transpose(2, 0, 1).reshape(128, 24)
    s[:, C_FCW:C_FCW + 132] = np.asarray(inp["ffn_conv_w"][0], np.float32).reshape(3, 44, 128).transpose(2, 0, 1).reshape(128, 132)
    s[:, C_FCB:C_FCB + 44] = np.asarray(inp["ffn_conv_b"][0], np.float32).reshape(44, 128).T
    s[:, C_SUBG] = np.asarray(inp["subln_g"][0], np.float32)
    for col, k in ((C_LQ1, "lambda_q1"), (C_LK1, "lambda_k1"), (C_LQ2, "lambda_q2"), (C_LK2, "lambda_k2")):
        s[:, col:col + 64] = np.asarray(inp[k][0], np.float32)[None, :]
    return s


_NC = None


def make_in_maps(inputs):
    x = np.asarray(inputs["x"], np.float32)
    meta = np.asarray(inputs["meta_tokens"], np.float32)
    ident, augq, augk, masks = host_consts()
    small = pack_small(inputs)
    shared = {
        "w_in": np.ascontiguousarray(np.asarray(inputs["w_in"], np.float32)[0]),
        "w_conv_out": np.ascontiguousarray(np.asarray(inputs["w_conv_out"], np.float32)[0]),
        "w_attn_out": np.ascontiguousarray(np.asarray(inputs["w_attn_out"], np.float32)[0]),
        "w_mix_out": np.ascontiguousarray(np.asarray(inputs["w_mix_out"], np.float32)[0]),
        "w_ffn_up": np.ascontiguousarray(np.asarray(inputs["w_ffn_up"], np.float32)[0]),
        "w_ffn_down": np.ascontiguousarray(np.asarray(inputs["w_ffn_down"], np.float32)[0]),
        "small": small, "ident": ident, "augq": augq, "augk": augk, "masks": masks,
    }
    in_maps = []
    for b in range(x.shape[0]):
        h0 = np.concatenate([meta, x[b]], axis=0)
        d = dict(shared)
        d["hT0"] = np.ascontiguousarray(h0.T)
        in_maps.append(d)
    return in_maps


def kernel(**inputs):
    global _NC
    in_maps = make_in_maps(inputs)
    if _NC is None:
        _NC = build_nc()
    res = run_bass_kernel_spmd(_NC, in_maps, core_ids=list(range(len(in_maps))))
    out = np.stack([np.ascontiguousarray(np.asarray(r["outT"], np.float32).T) for r in res.results])
    return out.astype(np.float32)
```

```python
import numpy as np
import ml_dtypes
import concourse.bass as bass
import concourse.mybir as mybir
from concourse.bass_utils import run_bass_kernel_spmd
from contextlib import ExitStack

F32 = mybir.dt.float32
BF16 = mybir.dt.bfloat16
AF = mybir.ActivationFunctionType
ALU = mybir.AluOpType
AX = mybir.AxisListType

NT = 4112
NJ = 9
NKB = 33
EPS = 1e-6
LAM_INIT = 0.8 - 0.6 * 1.0
DEBUG = False
FAST_RECIP = False
TASK_INTERLEAVE = True
PSWAP = 0
STOP_AFTER = 99
SUB = 99
EXTRA = 0

C_G1, C_GPOST, C_GFPRE, C_GFPOST = 0, 8, 16, 24
C_CW = 32
C_FCW = 56
C_FCB = 188
C_SUBG = 232
C_LQ1, C_LK1, C_LQ2, C_LK2 = 233, 297, 361, 425
NSMALL = 489


def trng(j):
    t0 = 512 * j
    return t0, min(512, NT - t0)


def RECIP(e, out, in_):
    if FAST_RECIP:
        return e.reciprocal_approx_fast(out=out, in_=in_)
    return e.reciprocal(out=out, in_=in_)


class Buf:
    __slots__ = ("w", "r")

    def __init__(self, after=None):
        self.w = None
        self.r = dict(after) if after else {}


def merged(bufs):
    d = {}
    for b in bufs:
        if b.w is not None:
            s, v = b.w
            if v > d.get(s, 0):
                d[s] = v
        for s, v in b.r.items():
            if v > d.get(s, 0):
                d[s] = v
    return d


class Queue:
    def __init__(self, name, sem):
        self.name = name
        self.sem = sem
        self.n = 0
        self.ops = []
        self.waited = {}


class DSem:
    def __init__(self, key):
        self.key = key
        self.count = 0


class Prog:
    def __init__(self, nc, st):
        self.nc = nc
        self.sems = {}
        self.q = {}
        for name in ("pe", "act", "dve", "pool", "sp"):
            key = "s_" + name
            self.sems[key] = st.enter_context(nc.semaphore(key))
            self.q[name] = Queue(name, key)
        self.dring = {}
        self.dri = {}
        for qn, n in (("sp", 24), ("pool", 16)):
            ring = []
            for i in range(n):
                key = f"d_{qn}{i}"
                self.sems[key] = st.enter_context(nc.semaphore(key))
                ring.append(DSem(key))
            self.dring[qn] = ring
            self.dri[qn] = 0

    def _collect(self, qn, reads, writes):
        q = self.q[qn]
        waits = {}

        def need(s, v, raw):
            if s == q.sem and (qn == "pe" or not raw):
                return
            if v > waits.get(s, 0):
                waits[s] = v

        for b in reads:
            if b.w is not None:
                need(b.w[0], b.w[1], True)
        for b in writes:
            if b.w is not None:
                need(b.w[0], b.w[1], False)
            for s, v in b.r.items():
                need(s, v, False)
        out = []
        for s, v in waits.items():
            if v > q.waited.get(s, 0):
                q.waited[s] = v
                out.append((s, v))
        return out

    def _commit(self, tok, reads, writes):
        s, v = tok
        for b in reads:
            if v > b.r.get(s, 0):
                b.r[s] = v
        for b in writes:
            b.w = tok
            b.r = {}

    def op(self, qn, fn, reads=(), writes=()):
        q = self.q[qn]
        waits = self._collect(qn, reads, writes)
        q.n += 1
        tok = (q.sem, q.n)
        q.ops.append((waits, fn, tok, 1))
        self._commit(tok, reads, writes)
        return tok

    def mm(self, out_ap, pairs, reads, writes):
        q = self.q["pe"]
        waits = self._collect("pe", reads, writes)
        n = len(pairs)
        tok = None
        for i, (l, r) in enumerate(pairs):
            def fn(e, l=l, r=r, i=i):
                return e.matmul(out_ap, lhsT=l, rhs=r, start=(i == 0), stop=(i == n - 1))
            if i == n - 1:
                q.n += 1
                tok = (q.sem, q.n)
            q.ops.append((waits if i == 0 else [], fn, tok if i == n - 1 else None, 1))
        self._commit(tok, reads, writes)
        return tok

    def dma(self, qn, out, in_, reads=(), writes=()):
        q = self.q[qn]
        ring = self.dring[qn]
        ent = ring[self.dri[qn] % len(ring)]
        self.dri[qn] += 1
        waits = self._collect(qn, reads, writes)
        if ent.count > 0 and 16 * ent.count > q.waited.get(ent.key, 0):
            q.waited[ent.key] = 16 * ent.count
            waits.append((ent.key, 16 * ent.count))
        ent.count += 1
        tok = (ent.key, 16 * ent.count)
        q.ops.append((waits, lambda e: e.dma_start(out=out, in_=in_), tok, 16))
        self._commit(tok, reads, writes)
        return tok

    def final_wait(self, qn, toks):
        q = self.q[qn]
        waits = []
        for s, v in toks.items():
            if v > q.waited.get(s, 0):
                q.waited[s] = v
                waits.append((s, v))
        q.ops.append((waits, None, None, 0))

    def final_all(self):
        toks = {}
        for q in self.q.values():
            if q.n > 0:
                toks[q.sem] = q.n
        for ring in self.dring.values():
            for ent in ring:
                if ent.count > 0:
                    toks[ent.key] = 16 * ent.count
        self.final_wait("sp", toks)

    def replay(self, qn, e):
        for waits, fn, tok, inc in self.q[qn].ops:
            for s, v in waits:
                e.wait_ge(self.sems[s], v)
            if fn is None:
                continue
            ins = fn(e)
            if tok is not None:
                ins.then_inc(self.sems[tok[0]], inc)


def build_nc():
    nc = bass.Bass("TRN2", target_bir_lowering=False)

    def din(name, shape, dt=F32):
        return nc.dram_tensor(name, list(shape), dt, kind="ExternalInput").ap()

    hT0 = din("hT0", [1024, NT])
    w_in = din("w_in", [1024, 8192])
    w_co = din("w_conv_out", [1024, 1024])
    w_ao = din("w_attn_out", [1024, 1024])
    w_mix = din("w_mix_out", [1024, 1024])
    w_up = din("w_ffn_up", [1024, 5632])
    w_dn = din("w_ffn_down", [2816, 1024])
    small_d = din("small", [128, NSMALL])
    ident_d = din("ident", [128, 128])
    augq_d = din("augq", [8, 4, NT], BF16)
    augk_d = din("augk", [8, 4, NT], BF16)
    masks_d = din("masks", [128, 4 * 512])
    outT = nc.dram_tensor("outT", [1024, 4096], F32, kind="ExternalOutput").ap()
    skind = "ExternalOutput" if DEBUG else "Internal"
    m1_scr = nc.dram_tensor("m1_scr", [8, 128, NT], F32, kind=skind).ap()
    sgb_scr = nc.dram_tensor("sgb_scr", [8, 128, NT], F32, kind=skind).ap()
    o_scr = nc.dram_tensor("o_scr", [8, 128, NT], BF16, kind=skind).ap()
    h1_scr = nc.dram_tensor("h1_scr", [NJ, 128, 8 * 512], F32, kind=skind).ap()
    hid_scr = nc.dram_tensor("hid_scr", [NJ, 128, 22 * 512], BF16, kind=skind).ap()
    if DEBUG:
        dbgA = nc.dram_tensor("dbgA", [128, 8 * NT], BF16, kind="ExternalOutput").ap()
        dbgB = nc.dram_tensor("dbgB", [128, 8 * NT], BF16, kind="ExternalOutput").ap()
        dbgC = nc.dram_tensor("dbgC", [128, 8 * NT], BF16, kind="ExternalOutput").ap()

    hT0_v = hT0.rearrange("(c p) t -> p c t", p=128)
    outT_v = outT.rearrange("(c p) t -> p c t", p=128)

    def wsrc(w, kc0, kc, col0):
        return w.rearrange("(k p) f -> p k f", p=128)[:, kc0:kc0 + kc, col0:col0 + 128]

    off = [0]

    def A(n):
        o = off[0]
        off[0] += (n + 31) // 32 * 32
        return o

    SMALL = A(NSMALL * 4)
    IDENT = A(128 * 4)
    ONES = A(128 * 2)
    MISC = A(64 * 4)
    IDENTB = A(128 * 2)
    BIGA = A(8 * NT * 2)
    BIGB = A(8 * NT * 2)
    WST = A(4 * 4096)
    WBF = A(8 * 2048)
    TB = A(44 * 1024)
    TOTAL = off[0]

    st = ExitStack()
    with st:
        pool_t = st.enter_context(nc.sbuf_tensor("pool", [128, TOTAL // 2], BF16))
        psb = [st.enter_context(nc.psum_tensor(f"ps{i}", [128, 512], F32)) for i in range(8)]
        P = Prog(nc, st)
        block = st.enter_context(nc.Block())

        def V(offb, dt, n, p0=0, p1=128):
            e0 = offb // 2
            if dt == BF16:
                return pool_t[p0:p1, e0:e0 + n]
            return pool_t[p0:p1, e0:e0 + 2 * n].bitcast(F32)

        pbuf = [Buf() for _ in range(8)]

        def PS(i, w=512, p0=0, p1=128, c0=0):
            return psb[i][p0:p1, c0:c0 + w]

        small = V(SMALL, F32, NSMALL)
        small_b = Buf()
        ident = V(IDENT, F32, 128)
        ident_b = Buf()
        ones = V(ONES, BF16, 128)
        ones_b = Buf()
        misc = V(MISC, F32, 64)
        misc_b = Buf()

        def sc(col, p0=0, p1=128):
            return small[p0:p1, col:col + 1]

        def mcol(col):
            return misc[:, col:col + 1]

        class StopBuild(Exception):
            pass

        def check_sub(k):
            if STOP_AFTER == 5 and k > SUB:
                raise StopBuild()

        def check_stop(n):
            if n > STOP_AFTER:
                raise StopBuild()

        def body():
            P.dma("sp", small, small_d, writes=[small_b])
            P.dma("sp", ident, ident_d, writes=[ident_b])
            P.op("pool", lambda e: e.memset(ones, 1.0), writes=[ones_b])
            identb = V(IDENTB, BF16, 128)
            identb_b = Buf()
            P.op("pool", lambda e: e.tensor_copy(out=identb, in_=ident), reads=[ident_b], writes=[identb_b])
            lamtmp = V(TB, F32, 64)
            lamtmp_b = Buf()
            P.op("dve", lambda e: e.tensor_tensor(out=lamtmp, in0=small[:, C_LQ1:C_LQ1 + 64], in1=small[:, C_LK1:C_LK1 + 64], op=ALU.mult),
                 reads=[small_b], writes=[lamtmp_b])
            P.op("dve", lambda e: e.reduce_sum(out=mcol(0), in_=lamtmp, axis=AX.X), reads=[lamtmp_b], writes=[misc_b])
            P.op("dve", lambda e: e.tensor_tensor(out=lamtmp, in0=small[:, C_LQ2:C_LQ2 + 64], in1=small[:, C_LK2:C_LK2 + 64], op=ALU.mult),
                 reads=[small_b, misc_b], writes=[lamtmp_b])
            P.op("dve", lambda e: e.reduce_sum(out=mcol(1), in_=lamtmp, axis=AX.X), reads=[lamtmp_b], writes=[misc_b])
            P.op("act", lambda e: e.activation(out=misc[:, 0:2], in_=misc[:, 0:2], func=AF.Exp), reads=[misc_b], writes=[misc_b])
            P.op("dve", lambda e: e.tensor_tensor(out=mcol(2), in0=mcol(0), in1=mcol(1), op=ALU.subtract), reads=[misc_b], writes=[misc_b])
            P.op("dve", lambda e: e.tensor_scalar(out=mcol(3), in0=mcol(2), scalar1=LAM_INIT, scalar2=-1.0, op0=ALU.add, op1=ALU.mult),
                 reads=[misc_b], writes=[misc_b])
            P.op("dve", lambda e: e.tensor_scalar(out=mcol(4), in0=sc(C_SUBG), scalar1=1.0 - LAM_INIT, scalar2=None, op0=ALU.mult),
                 reads=[misc_b, small_b], writes=[misc_b])
            neglam = mcol(3)
            gsub = mcol(4)

            def bigv(base, c, t0, w, p0=0, p1=128):
                return V(base + (c * NT + t0) * 2, BF16, w, p0, p1)

            wst_b = [Buf() for _ in range(4)]
            wbf_b = [Buf() for _ in range(8)]
            wctr = [0, 0]

            deferred_casts = []

            def load_w(src, kc, dst, dst_b, eng="pool"):
                s = wctr[0] % 4
                wctr[0] += 1
                stg = V(WST + s * 4096, F32, kc * 128)
                stg3 = stg.rearrange("p (k f) -> p k f", k=kc)
                P.dma("sp", stg3, src, writes=[wst_b[s]])
                if eng == "pool":
                    P.op("pool", lambda e: e.tensor_copy(out=dst, in_=stg), reads=[wst_b[s]], writes=[dst_b])
                elif eng == "act":
                    P.op("act", lambda e: e.activation(out=dst, in_=stg, func=AF.Copy), reads=[wst_b[s]], writes=[dst_b])
                else:
                    deferred_casts.append(lambda: P.op("act", lambda e: e.activation(out=dst, in_=stg, func=AF.Copy), reads=[wst_b[s]], writes=[dst_b]))

            def ring_slot():
                s = wctr[1] % 8
                wctr[1] += 1
                return V(WBF + s * 2048, BF16, 1024), wbf_b[s]

            def wk(wv, k):
                return wv[:, k * 128:(k + 1) * 128]

            wplan = []
            wnext = [0]

            def plan_ring(srcs):
                unit = []
                views = []
                for src in srcs:
                    unit.append([src, 8, None, None])
                    views.append(None)
                wplan.append(unit)
                return len(wplan) - 1

            wviews = {}

            cast_eng = ["act"]

            def prefetch(upto):
                while wnext[0] < min(upto, len(wplan)):
                    u = wnext[0]
                    vs = []
                    for ent in wplan[u]:
                        src, kc, dst, dstb = ent
                        if dst is None:
                            dst, dstb = ring_slot()
                        load_w(src, kc, dst, dstb, eng=cast_eng[0])
                        vs.append((dst, dstb))
                    wviews[u] = vs
                    wnext[0] += 1

            U_P1 = [plan_ring([wsrc(w_in, 0, 8, c * 128), wsrc(w_in, 0, 8, 1024 + c * 128), wsrc(w_in, 0, 8, 2048 + c * 128)]) for c in range(8)]
            U_P2 = [plan_ring([wsrc(w_co, 0, 8, f * 128), wsrc(w_in, 0, 8, 6144 + f * 128), wsrc(w_in, 0, 8, 7168 + f * 128)]) for f in range(8)]
            U_P3 = [plan_ring([wsrc(w_in, 0, 8, 3072 + h * 128), wsrc(w_in, 0, 8, 4096 + h * 128), wsrc(w_in, 0, 8, 5120 + h * 128)]) for h in range(8)]
            U_P4 = [plan_ring([wsrc(w_ao, 0, 8, f * 128)]) for f in range(8)]
            U_P5 = plan_ring([wsrc(w_mix, 0, 8, f * 128) for f in range(8)])
            U_P6 = [plan_ring([wsrc(w_up, 0, 8, i * 128), wsrc(w_up, 0, 8, 2816 + i * 128)]) for i in range(22)]
            wdn_b = [Buf() for _ in range(8)]
            unit = []
            for fo in range(8):
                for (k0, kc) in ((0, 8), (8, 8), (16, 6)):
                    dst = V(BIGB + (fo * 22 + k0) * 128 * 2, BF16, kc * 128)
                    unit.append([wsrc(w_dn, k0, kc, fo * 128), kc, dst, wdn_b[fo]])
            wplan.append(unit)
            U_P7 = len(wplan) - 1

            def wdn(fo, k):
                return V(BIGB + (fo * 22 + k) * 128 * 2, BF16, 128)

            aT_b = [[Buf() for _ in range(NJ)] for _ in range(8)]
            zT_b = [[Buf() for _ in range(NJ)] for _ in range(8)]
            Tbufs = [lamtmp_b]

            def new_T(n):
                nonlocal Tbufs
                after = merged(Tbufs)
                bs = [Buf(after) for _ in range(n)]
                Tbufs = bs
                return bs

            check_stop(0)
            prefetch(2)
            tb = new_T(6)
            hin_b, sq_b, sqv_b, rstd_b = tb[0:2], tb[2], tb[3], tb[4]
            HIN = [TB, TB + 16384]
            SQ = TB + 32768
            SQV = TB + 40960
            RSTD = TB + 43008

            def norm_stats(sq_view_fn, nchunks, w, bank, sq_buf, scale, sqv, sqv_b_, rstd, rstd_b_):
                P.mm(PS(bank, w), [(ones, sq_view_fn(c)) for c in range(nchunks)], reads=[ones_b, sq_buf], writes=[pbuf[bank]])
                P.op("act", lambda e: e.activation(out=sqv, in_=PS(bank, w), func=AF.Ln, scale=scale, bias=epsb),
                     reads=[pbuf[bank], misc_b], writes=[sqv_b_])
                P.op("act", lambda e: e.activation(out=rstd, in_=sqv, func=AF.Exp, scale=-0.5), reads=[sqv_b_], writes=[rstd_b_])

            P.op("dve", lambda e: e.memset(mcol(5), EPS), writes=[misc_b])
            epsb = mcol(5)

            for j in range(NJ):
                t0, w = trng(j)
                hb = hin_b[j % 2]
                hin = V(HIN[j % 2], F32, 8 * w)
                P.dma("sp", hin.rearrange("p (c t) -> p c t", c=8), hT0_v[:, :, t0:t0 + w], writes=[hb])
                sq = V(SQ, BF16, 8 * w)
                P.op("act", lambda e, sq=sq, hin=hin: e.activation(out=sq, in_=hin, func=AF.Square), reads=[hb], writes=[sq_b])
                sqv = V(SQV, F32, w)
                rstd = V(RSTD, F32, w)
                norm_stats(lambda c, sq=sq, w=w: sq[:, c * w:(c + 1) * w], 8, w, j % 2, sq_b, 1.0 / 1024, sqv, sqv_b, rstd, rstd_b)
                for c in range(8):
                    P.op("dve", lambda e, c=c, hin=hin, w=w, t0=t0, rstd=rstd: e.scalar_tensor_tensor(
                        out=bigv(BIGA, c, t0, w), in0=hin[:, c * w:(c + 1) * w], scalar=sc(C_G1 + c), in1=rstd, op0=ALU.mult, op1=ALU.mult),
                        reads=[hb, rstd_b, small_b], writes=[aT_b[c][j]])

            check_stop(1)
            tb = new_T(2 + 2 + 1 + 2 * NJ)
            cc_b, yb1_b = tb[0:2], tb[2:4]
            ubz_b = tb[4]
            ub_b = [tb[5:5 + NJ], tb[5 + NJ:5 + 2 * NJ]]
            CC = [TB, TB + 2048]
            YB1 = [TB + 4096, TB + 6144]
            UB = [TB + 16384, TB + 16384 + 8256]

            def dgv(base, i):
                return V(base + i * 256, BF16, 128)

            for i in range(2):
                P.op("pool", lambda e, i=i: e.memset(V(UB[i], BF16, 2), 0.0), writes=[ubz_b])

            items = [(c, j) for c in range(8) for j in range(NJ)]
            CBK = [0, 3, 6]

            def p1_A(idx):
                c, j = items[idx]
                t0, w = trng(j)
                if j == 0:
                    prefetch(U_P1[c] + 2)
                (wcb, wcb_b), (wcc, wcc_b), (wcx, wcx_b) = wviews[U_P1[c]]
                bcb = CBK[idx % 3]
                bcc = 1 + 3 * (idx % 2)
                bcx = 2 + 3 * (idx % 2)
                rd = [aT_b[k][j] for k in range(8)]
                P.mm(PS(bcb, w), [(wk(wcb, k), bigv(BIGA, k, t0, w)) for k in range(8)], reads=rd + [wcb_b], writes=[pbuf[bcb]])
                P.mm(PS(bcc, w), [(wk(wcc, k), bigv(BIGA, k, t0, w)) for k in range(8)], reads=rd + [wcc_b], writes=[pbuf[bcc]])
                P.mm(PS(bcx, w), [(wk(wcx, k), bigv(BIGA, k, t0, w)) for k in range(8)], reads=rd + [wcx_b], writes=[pbuf[bcx]])
                ccv = V(CC[idx % 2], F32, w)
                P.op("act", lambda e: e.activation(out=ccv, in_=PS(bcc, w), func=AF.Copy), reads=[pbuf[bcc]], writes=[cc_b[idx % 2]])
                ubv = V(UB[c % 2] + (2 + t0) * 2, BF16, w)
                P.op("dve", lambda e: e.tensor_tensor(out=ubv, in0=PS(bcx, w), in1=ccv, op=ALU.mult),
                     reads=[pbuf[bcx], cc_b[idx % 2]], writes=[ub_b[c % 2][j]])

            def p1_B(idx):
                c, j = items[idx]
                t0, w = trng(j)
                bcb = CBK[idx % 3]
                rd = [ub_b[c % 2][j], ubz_b] + ([ub_b[c % 2][j - 1]] if j > 0 else [])
                yb = V(YB1[idx % 2], F32, w)
                ybb = yb1_b[idx % 2]
                P.op("act", lambda e: e.activation(out=yb, in_=V(UB[c % 2] + t0 * 2, BF16, w), func=AF.Identity, scale=sc(C_CW + c), bias=0.0),
                     reads=rd + [small_b], writes=[ybb])
                P.op("dve", lambda e: e.scalar_tensor_tensor(out=yb, in0=V(UB[c % 2] + (t0 + 1) * 2, BF16, w), scalar=sc(C_CW + 8 + c), in1=yb, op0=ALU.mult, op1=ALU.add),
                     reads=rd + [ybb, small_b], writes=[ybb])
                P.op("dve", lambda e: e.scalar_tensor_tensor(out=yb, in0=V(UB[c % 2] + (t0 + 2) * 2, BF16, w), scalar=sc(C_CW + 16 + c), in1=yb, op0=ALU.mult, op1=ALU.add),
                     reads=rd + [ybb, small_b], writes=[ybb])
                P.op("dve", lambda e: e.tensor_tensor(out=bigv(BIGB, c, t0, w), in0=PS(bcb, w), in1=yb, op=ALU.mult),
                     reads=[pbuf[bcb], ybb], writes=[zT_b[c][j]])

            for idx in range(len(items) + 1):
                if idx < len(items):
                    p1_A(idx)
                if idx >= 1:
                    p1_B(idx - 1)

            if DEBUG:
                P.dma("sp", dbgA, V(BIGA, BF16, 8 * NT), reads=[b for r in aT_b for b in r])
                P.dma("sp", dbgB, V(BIGB, BF16, 8 * NT), reads=[b for r in zT_b for b in r])

            check_stop(2)
            tb = new_T(6)
            sga_b, m1o_b, sgbo_b = tb[0:2], tb[2:4], tb[4:6]
            SGA = [TB, TB + 2048]
            M1O = [TB + 4096, TB + 6144]
            SGBO = [TB + 8192, TB + 10240]
            m1s_b = [[Buf() for _ in range(NJ)] for _ in range(8)]
            sgbs_b = [[Buf() for _ in range(NJ)] for _ in range(8)]
            idx = 0
            for f in range(8):
                prefetch(U_P2[f] + 2)
                (wco, wco_b), (wga, wga_b), (wgb, wgb_b) = wviews[U_P2[f]]
                for j in range(NJ):
                    t0, w = trng(j)
                    bk = 3 * (idx % 2)
                    i2 = idx % 2
                    rda = [aT_b[k][j] for k in range(8)]
                    rdz = [zT_b[k][j] for k in range(8)]
                    P.mm(PS(bk, w), [(wk(wco, k), bigv(BIGB, k, t0, w)) for k in range(8)], reads=rdz + [wco_b], writes=[pbuf[bk]])
                    P.mm(PS(bk + 1, w), [(wk(wga, k), bigv(BIGA, k, t0, w)) for k in range(8)], reads=rda + [wga_b], writes=[pbuf[bk + 1]])
                    P.mm(PS(bk + 2, w), [(wk(wgb, k), bigv(BIGA, k, t0, w)) for k in range(8)], reads=rda + [wgb_b], writes=[pbuf[bk + 2]])
                    sga = V(SGA[i2], F32, w)
                    m1o = V(M1O[i2], F32, w)
                    sgbo = V(SGBO[i2], F32, w)
                    P.op("act", lambda e, sga=sga, bk=bk, w=w: e.activation(out=sga, in_=PS(bk + 1, w), func=AF.Sigmoid), reads=[pbuf[bk + 1]], writes=[sga_b[i2]])
                    P.op("dve", lambda e, sga=sga, bk=bk, w=w, m1o=m1o: e.tensor_tensor(out=m1o, in0=PS(bk, w), in1=sga, op=ALU.mult),
                         reads=[pbuf[bk], sga_b[i2]], writes=[m1o_b[i2]])
                    P.op("act", lambda e, sgbo=sgbo, bk=bk, w=w: e.activation(out=sgbo, in_=PS(bk + 2, w), func=AF.Sigmoid), reads=[pbuf[bk + 2]], writes=[sgbo_b[i2]])
                    P.dma("pool", m1_scr[f, :, t0:t0 + w], m1o, reads=[m1o_b[i2]], writes=[m1s_b[f][j]])
                    P.dma("pool", sgb_scr[f, :, t0:t0 + w], sgbo, reads=[sgbo_b[i2]], writes=[sgbs_b[f][j]])
                    idx += 1

            check_stop(3)
            after_B = merged([b for r in zT_b for b in r])
            QA, QB, KA, KB = BIGB, BIGB + 8224, BIGB + 2 * 8224, BIGB + 3 * 8224
            VV = BIGB + 4 * 8224
            MSK = VV + 8544
            PT = MSK + 8192
            SSB = PT + 8192
            assert SSB + 4096 <= BIGB + 8 * NT * 2
            qz_b = Buf(after_B)
            qaug_b = Buf(after_B)
            qA_b = [Buf(after_B) for _ in range(NJ)]
            qB_b = [Buf(after_B) for _ in range(NJ)]
            kA_b = [Buf(after_B) for _ in range(NJ)]
            kB_b = [Buf(after_B) for _ in range(NJ)]
            v_b = [Buf(after_B) for _ in range(9)]
            vone_b = Buf(after_B)
            msk_b = Buf(after_B)
            pt_b = [Buf(after_B) for _ in range(8)]
            ssb_b = [Buf(after_B) for _ in range(2)]
            tb = new_T(4 + 4 + 4 + 4 + 4 + 6 + 2 * NJ + 9 + 3 + NJ)
            r0_b, r1_b, t0_b, osb_b, on_b = tb[0:4], tb[4:8], tb[8:12], tb[12:16], tb[16:20]
            ssq_b, lnv_b, rstd3_b, junk_b = tb[20], tb[21], tb[22], tb[23]
            oout_b = tb[24:26]
            RS = TB
            T0S = TB + 128
            OSBS = T0S + 2048
            JUNK = OSBS + 2048
            ONS = JUNK + 512
            OOUT = [ONS + 1024, ONS + 2048]
            KA1 = TB + 8192
            KB1 = KA1 + 8224
            VV1 = KB1 + 8224
            assert VV1 + 8544 <= TB + 44 * 1024
            KAs, KBs, VVs = [KA, KA1], [KB, KB1], [VV, VV1]
            kA_bs = [kA_b, tb[26:26 + NJ]]
            kB_bs = [kB_b, tb[26 + NJ:26 + 2 * NJ]]
            v_bs = [v_b, tb[26 + 2 * NJ:26 + 2 * NJ + 9]]
            kz1_b, vone1_b, kaug1_b = tb[26 + 2 * NJ + 9], tb[26 + 2 * NJ + 10], tb[26 + 2 * NJ + 11]
            qaug_bt = [Buf(after_B) for _ in range(NJ)]
            kaug_bs = [qaug_b, kaug1_b]
            vone_bs = [vone_b, vone1_b]
            os_b = [[Buf() for _ in range(NJ)] for _ in range(8)]
            oT_b = [[Buf() for _ in range(NJ)] for _ in range(8)]

            for base in (QA, KA):
                P.op("pool", lambda e, base=base: e.memset(V(base, BF16, NT, 64, 128), 0.0), writes=[qz_b])
            for base in (QB, KB):
                P.op("pool", lambda e, base=base: e.memset(V(base, BF16, NT, 0, 64), 0.0), writes=[qz_b])
            mskv = V(MSK, F32, 4 * 512)
            P.dma("sp", mskv, masks_d, writes=[msk_b])
            P.op("pool", lambda e: e.memset(V(VV, BF16, 33 * 129).rearrange("p (b c) -> p b c", c=129)[:, :, 128:129], 1.0), writes=[vone_b])
            P.op("pool", lambda e: e.memset(V(KA1, BF16, NT, 64, 128), 0.0), writes=[kz1_b])
            P.op("pool", lambda e: e.memset(V(KB1, BF16, NT, 0, 64), 0.0), writes=[kz1_b])
            P.op("pool", lambda e: e.memset(V(VV1, BF16, 33 * 129).rearrange("p (b c) -> p b c", c=129)[:, :, 128:129], 1.0), writes=[vone1_b])

            tasks = []

            def q_task(hh, j):
                def f():
                    t0, w = trng(j)
                    wq_, wq_b_ = wviews[U_P3[hh]][0]
                    bk = sctr[0] % 4
                    sctr[0] += 1
                    P.mm(PS(bk, w), [(wk(wq_, k), bigv(BIGA, k, t0, w)) for k in range(8)], reads=[aT_b[k][j] for k in range(8)] + [wq_b_], writes=[pbuf[bk]])
                    P.op("dve", lambda e: e.tensor_scalar(out=V(QA + t0 * 2, BF16, w, 0, 64), in0=PS(bk, w, 0, 64), scalar1=0.125, scalar2=None, op0=ALU.mult),
                         reads=[pbuf[bk]], writes=[qA_b[j]])
                    P.op("dve", lambda e: e.tensor_scalar(out=V(QB + t0 * 2, BF16, w, 64, 128), in0=PS(bk, w, 64, 128), scalar1=0.125, scalar2=None, op0=ALU.mult),
                         reads=[pbuf[bk]], writes=[qB_b[j]])
                return f

            def k_task(hh, j):
                def f():
                    p = (hh + PSWAP) % 2
                    t0, w = trng(j)
                    wk_, wk_b_ = wviews[U_P3[hh]][1]
                    bk = sctr[0] % 4
                    sctr[0] += 1
                    P.mm(PS(bk, w), [(wk(wk_, k), bigv(BIGA, k, t0, w)) for k in range(8)], reads=[aT_b[k][j] for k in range(8)] + [wk_b_], writes=[pbuf[bk]])
                    P.op("dve", lambda e: e.tensor_copy(out=V(KAs[p] + t0 * 2, BF16, w, 0, 64), in_=PS(bk, w, 0, 64)), reads=[pbuf[bk]], writes=[kA_bs[p][j]])
                    P.op("dve", lambda e: e.tensor_copy(out=V(KBs[p] + t0 * 2, BF16, w, 64, 128), in_=PS(bk, w, 64, 128)), reads=[pbuf[bk]], writes=[kB_bs[p][j]])
                return f

            def v_task(hh, g):
                def f():
                    p = (hh + PSWAP) % 2
                    wv_, wv_b_ = wviews[U_P3[hh]][2]
                    bk = sctr[0] % 4
                    sctr[0] += 1
                    nblk = 4 if g < 8 else 1
                    for bi in range(nblk):
                        tbk = g * 4 + bi
                        tw = 128 if tbk < 32 else 16
                        jj = tbk // 4
                        P.mm(psb[bk][0:tw, bi * 128:(bi + 1) * 128], [(bigv(BIGA, k, tbk * 128, tw), wk(wv_, k)) for k in range(8)],
                             reads=[aT_b[k][jj] for k in range(8)] + [wv_b_], writes=[pbuf[bk]])
                    tw = 128 if g < 8 else 16
                    vdst = V(VVs[p] + g * 4 * 129 * 2, BF16, nblk * 129, 0, tw).rearrange("p (b c) -> p b c", c=129)[:, :, 0:128]
                    vsrc = psb[bk][0:tw, 0:nblk * 128].rearrange("p (b c) -> p b c", c=128)
                    P.op("dve", lambda e: e.tensor_copy(out=vdst, in_=vsrc), reads=[pbuf[bk]], writes=[v_bs[p][g]])
                return f

            def kaug_task(hh):
                def f():
                    p = (hh + PSWAP) % 2
                    zb_ = qz_b if p == 0 else kz1_b
                    P.dma("sp", V(KAs[p], BF16, NT, 64, 68), augk_d[hh], reads=[zb_], writes=[kaug_bs[p]])
                    P.dma("sp", V(KBs[p], BF16, NT, 0, 4), augk_d[hh], reads=[zb_], writes=[kaug_bs[p]])
                return f
            sctr = [0]

            cast_eng[0] = "pool"
            for h in range(8):
                prefetch(U_P3[h] + 2)
                (wq, wq_b), (wkk, wkk_b), (wv, wv_b) = wviews[U_P3[h]]
                pset = (h + PSWAP) % 2
                P.dma("sp", V(QA, BF16, NT, 64, 68), augq_d[h], reads=[qz_b], writes=[qaug_bt[0]])
                P.dma("sp", V(QB, BF16, NT, 0, 4), augq_d[h], reads=[qz_b], writes=[qaug_bt[0]])
                if h == 0:
                    kaug_task(0)()
                    for j in range(NJ):
                        q_task(0, j)()
                        k_task(0, j)()
                    for g in range(9):
                        v_task(0, g)()
                if h < 7:
                    tasks.append(kaug_task(h + 1))
                    for j in range(NJ):
                        tasks.append(k_task(h + 1, j))
                    for g in range(9):
                        tasks.append(v_task(h + 1, g))

                sweep = []
                for j in range(NJ):
                    nkb = 4 * j + 4 if j < 8 else NKB
                    for m in range(2):
                        if j < 8:
                            for kb in range(nkb):
                                sweep.append((j, m, kb, kb == nkb - 1, 1))
                        else:
                            for g_ in range(8):
                                sweep.append((j, m, 4 * g_, False, 4))
                            sweep.append((j, m, 32, True, 1))
                pending = []

                def emit_S(i):
                    j, m, kb, last, nblk = sweep[i]
                    t0, w = trng(j)
                    kw = 128 if kb < 32 else 16
                    bk = sctr[0] % 4
                    sctr[0] += 1
                    Qb, Kb = (QA, KAs[pset]) if m == 0 else (QB, KBs[pset])
                    qb_, kb_ = (qA_b, kA_bs[pset]) if m == 0 else (qB_b, kB_bs[pset])
                    xrd = [qaug_bt[0], kaug_bs[pset], qz_b] + ([kz1_b] if pset == 1 else [])
                    if nblk == 4:
                        rd = [qb_[j], kb_[kb // 4]] + xrd
                        q = P.q["pe"]
                        waits = P._collect("pe", rd, [pbuf[bk]])
                        q.n += 1
                        tok = (q.sem, q.n)
                        for b_ in range(4):
                            def fn(e, b_=b_):
                                return e.matmul(PS(bk, w, 0, 128, b_ * w), lhsT=V(Kb + (kb + b_) * 128 * 2, BF16, 128), rhs=V(Qb + t0 * 2, BF16, w),
                                                start=True, stop=True, skip_group_check=True)
                            q.ops.append((waits if b_ == 0 else [], fn, tok if b_ == 3 else None, 1))
                        P._commit(tok, rd, [pbuf[bk]])
                        ptv4 = V(PT + (i % 8) * 1024, BF16, 4 * w)
                        P.op("act", lambda e: e.activation(out=ptv4, in_=PS(bk, 4 * w), func=AF.Exp), reads=[pbuf[bk]], writes=[pt_b[i % 8]])
                        return
                    diag = (kb >= 4 * j)
                    r = kb - 4 * j if (diag and j < 8) else 0
                    c0 = 128 * r
                    wc = w - c0
                    P.mm(PS(bk, wc, 0, kw, c0), [(V(Kb + kb * 128 * 2, BF16, kw), V(Qb + (t0 + c0) * 2, BF16, wc))],
                         reads=[qb_[j], kb_[kb // 4]] + xrd, writes=[pbuf[bk]])
                    ptv = V(PT + (i % 8) * 1024 + c0 * 2, BF16, wc, 0, kw)
                    if diag:
                        ssv = V(SSB + (i % 2) * 2048 + c0 * 4, F32, wc, 0, kw)
                        P.op("dve", lambda e: e.tensor_tensor(out=ssv, in0=PS(bk, wc, 0, kw, c0), in1=V(MSK + r * 2048 + c0 * 4, F32, wc, 0, kw), op=ALU.add),
                             reads=[pbuf[bk], msk_b], writes=[ssb_b[i % 2]])
                        P.op("act", lambda e: e.activation(out=ptv, in_=ssv, func=AF.Exp), reads=[ssb_b[i % 2]], writes=[pt_b[i % 8]])
                    else:
                        P.op("act", lambda e: e.activation(out=ptv, in_=PS(bk, wc, 0, kw, c0), func=AF.Exp), reads=[pbuf[bk]], writes=[pt_b[i % 8]])

                def emit_PV(i):
                    j, m, kb, last, nblk = sweep[i]
                    t0, w = trng(j)
                    kw = 128 if kb < 32 else 16
                    r = kb - 4 * j if (kb >= 4 * j and j < 8) else 0
                    first = (kb == 0)
                    g = kb // 4
                    nqs = 4 if j < 8 else 1
                    qw = 128 if j < 8 else 16
                    if nblk == 4:
                        bank_ = pbuf[4 + 2 * m]
                        rd = [pt_b[i % 8], v_bs[pset][g], vone_bs[pset]]
                        q = P.q["pe"]
                        waits = P._collect("pe", rd, [bank_] if first else [])
                        q.n += 1
                        tok = (q.sem, q.n)
                        vbase = VVs[pset]
                        for b_ in range(4):
                            def fn(e, b_=b_, vbase=vbase):
                                return e.matmul(psb[4 + 2 * m][0:w, 0:129], lhsT=V(PT + (i % 8) * 1024 + b_ * w * 2, BF16, w),
                                                rhs=V(vbase + (kb + b_) * 129 * 2, BF16, 129), start=(first and b_ == 0), stop=False, skip_group_check=True)
                            q.ops.append((waits if b_ == 0 else [], fn, tok if b_ == 3 else None, 1))
                        P._commit(tok, rd, [bank_])
                        return
                    vblk = V(VVs[pset] + kb * 129 * 2, BF16, 129, 0, kw)
                    banks = [pbuf[4 + 2 * m], pbuf[5 + 2 * m]] if j < 8 else [pbuf[4 + 2 * m]]
                    reads = [pt_b[i % 8], v_bs[pset][g], vone_bs[pset]]
                    q = P.q["pe"]
                    waits = P._collect("pe", reads, banks if first else [])
                    fns = []
                    for qs in range(r, nqs):
                        bank = 4 + 2 * m + qs // 2
                        col = (qs % 2) * 129
                        out_ap = psb[bank][0:qw, col:col + 129]
                        lhsT = V(PT + (i % 8) * 1024 + qs * 128 * 2, BF16, qw, 0, kw)
                        st = first and (qs % 2 == 0)
                        lastq = (kb == (4 * j + qs if j < 8 else 32))

                        def fn(e, out_ap=out_ap, lhsT=lhsT, st=st, lastq=lastq):
                            return e.matmul(out_ap, lhsT=lhsT, rhs=vblk, start=st, stop=lastq, skip_group_check=True)
                        fns.append(fn)
                    q.n += 1
                    tok = (q.sem, q.n)
                    for k_, fn in enumerate(fns):
                        q.ops.append((waits if k_ == 0 else [], fn, tok if k_ == len(fns) - 1 else None, 1))
                    P._commit(tok, reads, banks)
                    if last:
                        emit_norm(i, j, m)
                        if m == 1 and h < 7:
                            tasks.append(q_task(h + 1, j))

                def emit_norm(i, j, m):
                    t0, w = trng(j)
                    nqs = 4 if j < 8 else 1
                    qw = 128 if j < 8 else 16

                    def oreg(qs, c0, c1):
                        bank = 4 + 2 * m + qs // 2
                        col = (qs % 2) * 129
                        return psb[bank][0:qw, col + c0:col + c1], pbuf[bank]

                    def rsc(c):
                        return V(RS + c * 4, F32, 1, 0, qw)

                    def T0v(qs):
                        return V(T0S + qs * 512, F32, 128, 0, qw)

                    def OSBv(qs):
                        return V(OSBS + qs * 512, F32, 128, 0, qw)

                    def ONv(qs):
                        return V(ONS + qs * 256, BF16, 128, 0, qw)
                    steps = []
                    for qs in range(nqs):
                        def f(qs=qs):
                            oap, ob_ = oreg(qs, 0, 128)
                            sap, _ = oreg(qs, 128, 129)
                            if m == 0:
                                P.op("dve", lambda e: e.reciprocal(out=rsc(qs), in_=sap), reads=[ob_], writes=[r0_b[qs]])
                                P.op("dve", lambda e: e.tensor_scalar(out=T0v(qs), in0=oap, scalar1=rsc(qs), scalar2=None, op0=ALU.mult),
                                     reads=[ob_, r0_b[qs]], writes=[t0_b[qs]])
                            else:
                                P.op("dve", lambda e: e.reciprocal(out=rsc(4 + qs), in_=sap), reads=[ob_], writes=[r1_b[qs]])
                                P.op("dve", lambda e: e.tensor_tensor(out=rsc(8 + qs), in0=rsc(4 + qs), in1=misc[0:qw, 3:4], op=ALU.mult),
                                     reads=[r1_b[qs], misc_b], writes=[r1_b[qs]])
                                P.op("dve", lambda e: e.scalar_tensor_tensor(out=OSBv(qs), in0=oap, scalar=rsc(8 + qs), in1=T0v(qs), op0=ALU.mult, op1=ALU.add),
                                     reads=[ob_, r1_b[qs], t0_b[qs]], writes=[osb_b[qs]])
                        steps.append((qs, f))
                    if m == 1:
                        def s_sumsq():
                            for qs in range(nqs):
                                P.op("dve", lambda e, qs=qs: e.scalar_tensor_tensor(out=V(JUNK, F32, 128, 0, qw), in0=OSBv(qs), scalar=1.0, in1=OSBv(qs),
                                                                                    op0=ALU.mult, op1=ALU.mult, accum_out=rsc(12 + qs)),
                                     reads=[osb_b[qs]], writes=[junk_b, ssq_b])
                        steps.append((5, s_sumsq))
                        steps.append((7, lambda: P.op("act", lambda e: e.activation(out=V(RS + 16 * 4, F32, nqs, 0, qw), in_=V(RS + 12 * 4, F32, nqs, 0, qw),
                                                                                       func=AF.Ln, scale=1.0 / 128, bias=misc[0:qw, 5:6]),
                                                      reads=[ssq_b, misc_b], writes=[lnv_b])))
                        steps.append((8, lambda: P.op("act", lambda e: e.activation(out=V(RS + 20 * 4, F32, nqs, 0, qw), in_=V(RS + 16 * 4, F32, nqs, 0, qw),
                                                                                       func=AF.Exp, scale=-0.5),
                                                      reads=[lnv_b], writes=[rstd3_b])))

                        def s_scale():
                            for qs in range(nqs):
                                P.op("dve", lambda e, qs=qs: e.tensor_scalar(out=ONv(qs), in0=OSBv(qs), scalar1=rsc(20 + qs), scalar2=None, op0=ALU.mult),
                                     reads=[osb_b[qs], rstd3_b], writes=[on_b[qs]])
                        steps.append((9, s_scale))

                        def s_out():
                            sb_ = sctr[0] % 4
                            sctr[0] += 1
                            psbf = psb[sb_][:, :].bitcast(BF16)
                            q = P.q["pe"]
                            rd = [on_b[qs] for qs in range(nqs)] + [identb_b]
                            waits = P._collect("pe", rd, [pbuf[sb_]])
                            q.n += 1
                            tok = (q.sem, q.n)
                            for qs in range(nqs):
                                def fn(e, qs=qs):
                                    return e.transpose(psbf[:, qs * 128:qs * 128 + qw], ONv(qs), identb[0:qw, 0:qw])
                                q.ops.append((waits if qs == 0 else [], fn, tok if qs == nqs - 1 else None, 1))
                            P._commit(tok, rd, [pbuf[sb_]])
                            if h < 7:
                                oo = V(OOUT[j % 2], BF16, w)
                                P.op("dve", lambda e: e.tensor_scalar(out=oo, in0=psbf[:, 0:w], scalar1=gsub, scalar2=None, op0=ALU.mult),
                                     reads=[pbuf[sb_], misc_b], writes=[oout_b[j % 2]])
                                P.dma("pool", o_scr[h, :, t0:t0 + w], oo, reads=[oout_b[j % 2]], writes=[os_b[h][j]])
                            else:
                                P.op("dve", lambda e: e.tensor_scalar(out=bigv(BIGA, 7, t0, w), in0=psbf[:, 0:w], scalar1=gsub, scalar2=None, op0=ALU.mult),
                                     reads=[pbuf[sb_], misc_b], writes=[oT_b[7][j], aT_b[7][j]])
                        steps.append((11, s_out))
                    for off_, f_ in steps:
                        pending.append((i + 3 + off_, f_))
                    pending.sort(key=lambda x: x[0])

                n = len(sweep)
                LA = 3
                for i in range(n + LA):
                    if i < n:
                        emit_S(i)
                    if i >= LA:
                        emit_PV(i - LA)
                    while pending and pending[0][0] <= i:
                        pending.pop(0)[1]()
                    if TASK_INTERLEAVE and i % 8 == 5 and tasks:
                        tasks.pop(0)()
                while pending:
                    pending.pop(0)[1]()
                while tasks:
                    tasks.pop(0)()

                if h == 7:
                    pass
                if h == 6:
                    pass

            cast_eng[0] = "act"
            for hh in range(7):
                P.dma("sp", bigv(BIGA, hh, 0, NT), o_scr[hh, :, :], reads=os_b[hh], writes=oT_b[hh] + aT_b[hh])

            if DEBUG:
                P.dma("sp", dbgC, V(BIGA, BF16, 8 * NT), reads=[b for r in oT_b for b in r])

            check_stop(4)
            after_B = merged([qz_b, qaug_b, msk_b, vone_b] + qaug_bt + qA_b + qB_b + kA_b + kB_b + v_b + pt_b + ssb_b)
            m2_b = [[Buf(after_B) for _ in range(NJ)] for _ in range(8)]
            NS4 = 6
            tb = new_T(2 * NS4 + 4)
            m1i_b, sgi_b, tmp_b = tb[0:NS4], tb[NS4:2 * NS4], tb[2 * NS4:2 * NS4 + 4]
            M1I = [TB + 2048 * i for i in range(NS4)]
            SGI = [TB + 2048 * (NS4 + i) for i in range(NS4)]
            TMP = [TB + 2048 * (2 * NS4 + i) for i in range(4)]
            idx = 0
            for f in range(8):
                prefetch(min(U_P4[f] + 3, U_P5))
                (wao, wao_b), = wviews[U_P4[f]]
                for j in range(NJ):
                    t0, w = trng(j)
                    bk = idx % 4
                    i3 = idx % NS4
                    m1i = V(M1I[i3], F32, w)
                    sgi = V(SGI[i3], F32, w)
                    P.dma("sp", m1i, m1_scr[f, :, t0:t0 + w], reads=[m1s_b[f][j]], writes=[m1i_b[i3]])
                    P.dma("sp", sgi, sgb_scr[f, :, t0:t0 + w], reads=[sgbs_b[f][j]], writes=[sgi_b[i3]])
                    P.mm(PS(bk, w), [(wk(wao, k), bigv(BIGA, k, t0, w)) for k in range(8)], reads=[oT_b[k][j] for k in range(8)] + [wao_b], writes=[pbuf[bk]])
                    tmp = V(TMP[idx % 4], F32, w)
                    P.op("dve", lambda e, tmp=tmp, bk=bk, w=w, sgi=sgi: e.tensor_tensor(out=tmp, in0=PS(bk, w), in1=sgi, op=ALU.mult),
                         reads=[pbuf[bk], sgi_b[i3]], writes=[tmp_b[idx % 4]])
                    P.op("pool", lambda e, tmp=tmp, m1i=m1i, f=f, t0=t0, w=w: e.tensor_tensor(out=bigv(BIGB, f, t0, w), in0=tmp, in1=m1i, op=ALU.add),
                         reads=[tmp_b[idx % 4], m1i_b[i3]], writes=[m2_b[f][j]])
                    idx += 1

            if EXTRA:
                xb = Buf(merged(Tbufs))
                for _ in range(EXTRA):
                    P.dma("sp", V(TB, F32, 4096).rearrange("p (c t) -> p c t", c=8), hT0_v[:, :, 0:512], writes=[xb])
            check_stop(5)
            prefetch(U_P5 + 1)
            wmix = wviews[U_P5]
            tb = new_T(16 + 8 + 2)
            mix_b = [tb[0:8], tb[8:16]]
            msq0_b = tb[16:24]
            sqv5_b, rstd5_b = tb[24], tb[25]
            after_wst = merged(wst_b)
            h0_b = [Buf(after_wst) for _ in range(8)]
            MIX = [TB, TB + 16384]
            MSQ0, SQV5, RSTD5 = TB + 32768, TB + 40960, TB + 43008
            H0 = WST
            h1s_b = [Buf() for _ in range(NJ)]
            fT_b = [[Buf() for _ in range(NJ)] for _ in range(8)]

            def p5_A(j, part):
                t0, w = trng(j)
                p = j % 2
                mixv = V(MIX[p], F32, 8 * w)
                for fo in range(4 * part, 4 * part + 4):
                    bk = fo % 4
                    wm, wm_b = wmix[fo]
                    P.mm(PS(bk, w), [(wk(wm, k), bigv(BIGB, k, t0, w)) for k in range(8)], reads=[m2_b[k][j] for k in range(8)] + [wm_b], writes=[pbuf[bk]])
                    P.op("dve", lambda e, fo=fo, bk=bk: e.tensor_copy(out=mixv[:, fo * w:(fo + 1) * w], in_=PS(bk, w)),
                         reads=[pbuf[bk]], writes=[mix_b[p][fo]])
                    if j == 0:
                        dst, dstb = V(MSQ0 + fo * w * 2, BF16, w), msq0_b[fo]
                    else:
                        pt0, _ = trng(j - 1)
                        dst, dstb = bigv(BIGB, fo, pt0, w), m2_b[fo][j - 1]
                    P.op("act", lambda e, fo=fo, dst=dst: e.activation(out=dst, in_=mixv[:, fo * w:(fo + 1) * w], func=AF.Square),
                         reads=[mix_b[p][fo]], writes=[dstb])

            def p5_B(j, half):
                t0, w = trng(j)
                p = j % 2
                mixv = V(MIX[p], F32, 8 * w)
                h0 = V(H0, F32, 8 * w)
                sqv = V(SQV5, F32, w)
                rstd = V(RSTD5, F32, w)
                sq2 = [(V(MSQ0 + c * w * 2, BF16, w), msq0_b[c]) for c in range(8)]
                if half == 1:
                    P.mm(PS(7, w), [(ones, v_) for v_, _ in sq2], reads=[ones_b] + [b_ for _, b_ in sq2], writes=[pbuf[7]])
                    P.op("act", lambda e: e.activation(out=sqv, in_=PS(7, w), func=AF.Ln, scale=1.0 / 1024, bias=epsb), reads=[pbuf[7], misc_b], writes=[sqv5_b])
                    P.op("act", lambda e: e.activation(out=rstd, in_=sqv, func=AF.Exp, scale=-0.5), reads=[sqv5_b], writes=[rstd5_b])
                    for c in range(8):
                        P.op("dve", lambda e, c=c: e.scalar_tensor_tensor(
                            out=bigv(BIGA, c, t0, w), in0=h0[:, c * w:(c + 1) * w], scalar=sc(C_GFPRE + c), in1=rstd, op0=ALU.mult, op1=ALU.mult),
                            reads=[h0_b[c], rstd5_b, small_b], writes=[fT_b[c][j], oT_b[c][j]])
                    return
                if j == 0:
                    sqs = [(V(MSQ0 + c * w * 2, BF16, w), msq0_b[c]) for c in range(8)]
                else:
                    pt0, _ = trng(j - 1)
                    sqs = [(bigv(BIGB, c, pt0, w), m2_b[c][j - 1]) for c in range(8)]
                P.mm(PS(6, w), [(ones, v_) for v_, _ in sqs], reads=[ones_b] + [b_ for _, b_ in sqs], writes=[pbuf[6]])
                P.op("act", lambda e: e.activation(out=sqv, in_=PS(6, w), func=AF.Ln, scale=1.0 / 1024, bias=epsb), reads=[pbuf[6], misc_b], writes=[sqv5_b])
                P.op("act", lambda e: e.activation(out=rstd, in_=sqv, func=AF.Exp, scale=-0.5), reads=[sqv5_b], writes=[rstd5_b])
                for fo in range(8):
                    P.op("dve", lambda e, fo=fo: e.scalar_tensor_tensor(
                        out=mixv[:, fo * w:(fo + 1) * w], in0=mixv[:, fo * w:(fo + 1) * w], scalar=sc(C_GPOST + fo), in1=rstd, op0=ALU.mult, op1=ALU.mult),
                        reads=[mix_b[p][fo], rstd5_b, small_b], writes=[mix_b[p][fo]])
                for fo in range(8):
                    eng = "pool" if fo % 2 == 0 else "dve"
                    P.op(eng, lambda e, fo=fo: e.tensor_tensor(out=h0[:, fo * w:(fo + 1) * w], in0=h0[:, fo * w:(fo + 1) * w], in1=mixv[:, fo * w:(fo + 1) * w], op=ALU.add),
                         reads=[h0_b[fo], mix_b[p][fo]], writes=[h0_b[fo]])
                    P.op("act", lambda e, fo=fo: e.activation(out=sq2[fo][0], in_=h0[:, fo * w:(fo + 1) * w], func=AF.Square),
                         reads=[h0_b[fo]], writes=[sq2[fo][1]])
                P.dma("pool", h1_scr[j, :, 0:8 * w], h0, reads=h0_b, writes=[h1s_b[j]])

            for j in range(NJ + 1):
                if j < NJ:
                    p5_A(j, 0)
                if j >= 1:
                    p5_B(j - 1, 0)
                if j < NJ:
                    p5_A(j, 1)
                if j >= 1:
                    p5_B(j - 1, 1)
                if j < NJ:
                    t0, w = trng(j)
                    P.dma("sp", V(H0, F32, 8 * w).rearrange("p (c t) -> p c t", c=8), hT0_v[:, :, t0:t0 + w], writes=h0_b)
            after_h0 = merged(h0_b)
            for b in wst_b:
                for k_, v_ in after_h0.items():
                    if v_ > b.r.get(k_, 0):
                        b.r[k_] = v_

            check_stop(6)
            tb = new_T(2 + 2 + 2 + 2 + 1 + 4 * NJ)
            dg6_b, sg6_b, hid_b, yb_b, ubz6_b = tb[0:2], tb[2:4], tb[4:6], tb[6:8], tb[8]
            ug_b = [tb[9:9 + NJ], tb[9 + NJ:9 + 2 * NJ]]
            uu_b = [tb[9 + 2 * NJ:9 + 3 * NJ], tb[9 + 3 * NJ:9 + 4 * NJ]]
            UG = [TB, TB + 8256]
            UU = [TB + 2 * 8256, TB + 3 * 8256]
            DG6 = [TB + 4 * 8256, TB + 4 * 8256 + 768]
            SG6 = [TB + 4 * 8256 + 1536, TB + 4 * 8256 + 1536 + 2048]
            YB = [TB + 4 * 8256 + 1536 + 4096, TB + 4 * 8256 + 1536 + 6144]
            HID = [TB + 4 * 8256 + 1536 + 8192 + i * 1024 for i in range(2)]
            assert HID[1] + 1024 <= TB + 44 * 1024
            hids_b = [[Buf() for _ in range(NJ)] for _ in range(22)]
            for base in UG + UU:
                P.op("pool", lambda e, base=base: e.memset(V(base, BF16, 2), 0.0), writes=[ubz6_b])
            items6 = [(i, j) for i in range(22) for j in range(NJ)]
            after_m2 = merged([b for r in m2_b for b in r])
            for b in wdn_b:
                b.r = dict(after_m2)
            UBK = [2, 3, 6]

            wd_left = list(wplan[U_P7])
            cast_eng[0] = "act_later"

            def p6_A(idx):
                i, j = items6[idx]
                t0, w = trng(j)
                if j == 0:
                    prefetch(min(U_P6[i] + 2, U_P7))
                    if i == 0:
                        while deferred_casts:
                            deferred_casts.pop(0)()
                    for s in range(3):
                        P.op("dve", lambda e, i=i, s=s: e.tensor_scalar(out=dgv(DG6[i % 2], s), in0=ident, scalar1=sc(C_FCW + s * 44 + i), scalar2=None, op0=ALU.mult),
                             reads=[ident_b, small_b], writes=[dg6_b[i % 2]])
                if j >= 1 and deferred_casts:
                    deferred_casts.pop(0)()
                if idx >= 12 * NJ and wd_left:
                    src_, kc_, dst_, dstb_ = wd_left.pop(0)
                    load_w(src_, kc_, dst_, dstb_, eng="act")
                (wg, wg_b), (wu, wu_b) = wviews[U_P6[i]]
                bg = idx % 2
                bu = UBK[idx % 3]
                rd = [fT_b[k][j] for k in range(8)]
                P.mm(PS(bg, w), [(wk(wg, k), bigv(BIGA, k, t0, w)) for k in range(8)], reads=rd + [wg_b], writes=[pbuf[bg]])
                P.mm(PS(bu, w), [(wk(wu, k), bigv(BIGA, k, t0, w)) for k in range(8)], reads=rd + [wu_b], writes=[pbuf[bu]])
                P.op("act", lambda e: e.activation(out=V(UG[i % 2] + (2 + t0) * 2, BF16, w), in_=PS(bg, w), func=AF.Copy), reads=[pbuf[bg]], writes=[ug_b[i % 2][j]])
                P.op("act", lambda e: e.activation(out=V(UU[i % 2] + (2 + t0) * 2, BF16, w), in_=PS(bu, w), func=AF.Copy), reads=[pbuf[bu]], writes=[uu_b[i % 2][j]])

            def p6_B(idx):
                i, j = items6[idx]
                t0, w = trng(j)
                bk = 4 + idx % 2
                bu = UBK[idx % 3]
                rdg = [ug_b[i % 2][j], dg6_b[i % 2], ubz6_b] + ([ug_b[i % 2][j - 1]] if j > 0 else [])
                rdu = [uu_b[i % 2][j], ubz6_b] + ([uu_b[i % 2][j - 1]] if j > 0 else [])
                P.mm(PS(bk, w), [(dgv(DG6[i % 2], s), V(UG[i % 2] + (t0 + s) * 2, BF16, w)) for s in range(3)], reads=rdg, writes=[pbuf[bk]])
                yb = V(YB[idx % 2], F32, w)
                ybb = yb_b[idx % 2]
                P.op("act", lambda e: e.activation(out=yb, in_=V(UU[i % 2] + t0 * 2, BF16, w), func=AF.Identity, scale=sc(C_FCW + 22 + i), bias=sc(C_FCB + 22 + i)),
                     reads=rdu + [small_b], writes=[ybb])
                P.op("dve", lambda e: e.scalar_tensor_tensor(out=yb, in0=V(UU[i % 2] + (t0 + 1) * 2, BF16, w), scalar=sc(C_FCW + 44 + 22 + i), in1=yb, op0=ALU.mult, op1=ALU.add),
                     reads=rdu + [ybb, small_b], writes=[ybb])
                sg = V(SG6[idx % 2], F32, w)
                P.op("act", lambda e: e.activation(out=sg, in_=PS(bk, w), func=AF.Silu, bias=sc(C_FCB + i)), reads=[pbuf[bk], small_b], writes=[sg6_b[idx % 2]])
                P.op("dve", lambda e: e.scalar_tensor_tensor(out=yb, in0=PS(bu, w), scalar=sc(C_FCW + 88 + 22 + i), in1=yb, op0=ALU.mult, op1=ALU.add),
                     reads=[pbuf[bu], ybb, small_b], writes=[ybb])
                hv = V(HID[idx % 2], BF16, w)
                P.op("dve", lambda e: e.tensor_tensor(out=hv, in0=yb, in1=sg, op=ALU.mult),
                     reads=[ybb, sg6_b[idx % 2]], writes=[hid_b[idx % 2]])
                P.dma("pool", hid_scr[j, :, i * w:(i + 1) * w], hv, reads=[hid_b[idx % 2]], writes=[hids_b[i][j]])

            for idx in range(len(items6) + 1):
                if idx < len(items6):
                    p6_A(idx)
                if idx >= 1:
                    p6_B(idx - 1)

            check_stop(7)
            assert not wd_left and wnext[0] == U_P7
            wnext[0] = U_P7 + 1
            tb = new_T(2)
            hd_b = tb[0:2]
            HD = [TB, TB + 22528]
            after_A = merged([b for r in fT_b for b in r])
            after_W = merged(wbf_b)
            ysb_b = [Buf(after_A), Buf(after_A)]
            h1b_b = [Buf(after_A), Buf(after_A)]
            ysq_b = [Buf(after_m2), Buf(after_m2)]
            sqv7_b = [Buf(after_W), Buf(after_W)]
            rstd7_b = [Buf(after_W), Buf(after_W)]
            YSB = [BIGA, BIGA + 16384]
            H1B = [BIGA + 32768, BIGA + 49152]
            YSQ = [BIGB + 45056, BIGB + 45056 + 8192]
            SQV7 = [WBF, WBF + 2048]
            RSTD7 = [WBF + 4096, WBF + 6144]
            out_toks = {}

            def p7_A(j, part):
                t0, w = trng(j)
                p = j % 2
                hd = V(HD[p], BF16, 22 * w)
                h1b = V(H1B[p], F32, 8 * w)
                ysb = V(YSB[p], F32, 8 * w)
                ysq = V(YSQ[p], BF16, 8 * w)
                if part == 0:
                    P.dma("sp", hd, hid_scr[j, :, 0:22 * w], reads=[hids_b[i][j] for i in range(22)], writes=[hd_b[p]])
                    P.dma("sp", h1b, h1_scr[j, :, 0:8 * w], reads=[h1s_b[j]], writes=[h1b_b[p]])
                for fo in range(4 * part, 4 * part + 4):
                    bk = fo % 4
                    P.mm(PS(bk, w), [(wdn(fo, k), hd[:, k * w:(k + 1) * w]) for k in range(22)], reads=[hd_b[p], wdn_b[fo]], writes=[pbuf[bk]])
                    P.op("dve", lambda e, fo=fo, bk=bk: e.tensor_copy(out=ysb[:, fo * w:(fo + 1) * w], in_=PS(bk, w)),
                         reads=[pbuf[bk]], writes=[ysb_b[p]])
                    P.op("act", lambda e, fo=fo: e.activation(out=ysq[:, fo * w:(fo + 1) * w], in_=ysb[:, fo * w:(fo + 1) * w], func=AF.Square),
                         reads=[ysb_b[p]], writes=[ysq_b[p]])

            def p7_B(j):
                t0, w = trng(j)
                p = j % 2
                h1b = V(H1B[p], F32, 8 * w)
                ysb = V(YSB[p], F32, 8 * w)
                ysq = V(YSQ[p], BF16, 8 * w)
                sqv = V(SQV7[p], F32, w)
                rstd = V(RSTD7[p], F32, w)
                norm_stats(lambda c: ysq[:, c * w:(c + 1) * w], 8, w, 6 + p, ysq_b[p], 1.0 / 1024, sqv, sqv7_b[p], rstd, rstd7_b[p])
                for fo in range(8):
                    P.op("dve", lambda e, fo=fo: e.scalar_tensor_tensor(
                        out=ysb[:, fo * w:(fo + 1) * w], in0=ysb[:, fo * w:(fo + 1) * w], scalar=sc(C_GFPOST + fo), in1=rstd, op0=ALU.mult, op1=ALU.mult),
                        reads=[ysb_b[p], rstd7_b[p], small_b], writes=[ysb_b[p]])
                for fo in range(8):
                    P.op("pool", lambda e, fo=fo: e.tensor_tensor(out=h1b[:, fo * w:(fo + 1) * w], in0=h1b[:, fo * w:(fo + 1) * w], in1=ysb[:, fo * w:(fo + 1) * w], op=ALU.add),
                         reads=[h1b_b[p], ysb_b[p]], writes=[h1b_b[p]])
                h1b3 = h1b.rearrange("p (c t) -> p c t", c=8)
                if j == 0:
                    tok = P.dma("pool", outT_v[:, :, 0:496], h1b3[:, :, 16:512], reads=[h1b_b[p]])
                else:
                    tok = P.dma("pool", outT_v[:, :, t0 - 16:t0 - 16 + w], h1b3, reads=[h1b_b[p]])
                out_toks[tok[0]] = max(out_toks.get(tok[0], 0), tok[1])

            for j in range(NJ + 1):
                if j < NJ:
                    p7_A(j, 0)
                if j >= 1:
                    p7_B(j - 1)
                if j < NJ:
                    p7_A(j, 1)
            P.final_wait("sp", out_toks)

        try:
            body()
        except StopBuild:
            P.final_all()

        @block.sync
        def _(e):
            P.replay("sp", e)

        @block.gpsimd
        def _(e):
            P.replay("pool", e)

        @block.tensor
        def _(e):
            P.replay("pe", e)

        @block.scalar
        def _(e):
            P.replay("act", e)

        @block.vector
        def _(e):
            P.replay("dve", e)
    return nc


def host_consts():
    ident = np.eye(128, dtype=np.float32)
    pos = np.arange(NT)
    augq = np.zeros((8, 4, NT), np.float32)
    augk = np.zeros((8, 4, NT), np.float32)
    for h in range(8):
        slope = 2.0 ** (-(h + 1))
        augq[h, 0] = -slope * ((pos >> 8) << 8)
        augq[h, 1] = -slope * (pos & 255)
        augq[h, 2] = 1.0
        augq[h, 3] = 1.0
        augk[h, 0] = 1.0
        augk[h, 1] = 1.0
        augk[h, 2] = slope * ((pos >> 7) << 7)
        augk[h, 3] = slope * (pos & 127)
    ki = np.arange(128)[:, None]
    qi = np.arange(512)[None, :]
    masks = np.zeros((128, 4, 512), np.float32)
    for r in range(4):
        masks[:, r, :] = np.where(128 * r + ki <= qi, 0.0, -30000.0)
    return ident, augq.astype(ml_dtypes.bfloat16), augk.astype(ml_dtypes.bfloat16), masks.reshape(128, 2048)


def pack_small(inp):
    s = np.zeros((128, NSMALL), np.float32)

    def g(v):
        return np.asarray(v, np.float32).reshape(8, 128).T

    s[:, C_G1:C_G1 + 8] = g(inp["norm_mix_pre"][0])
    s[:, C_GPOST:C_GPOST + 8] = g(inp["norm_mix_post"][0])
    s[:, C_GFPRE:C_GFPRE + 8] = g(inp["norm_ffn_pre"][0])
    s[:, C_GFPOST:C_GFPOST + 8] = g(inp["norm_ffn_post"][0])
    s[:, C_CW:C_CW + 24] = np.asarray(inp["conv_w"][0], np.float32).reshape(3, 8, 128).transpose(2, 0, 1).reshape(128, 24)
    s[:, C_FCW:C_FCW + 132] = np.asarray(inp["ffn_conv_w"][0], np.float32).reshape(3, 44, 128).transpose(2, 0, 1).reshape(128, 132)
    s[:, C_FCB:C_FCB + 44] = np.asarray(inp["ffn_conv_b"][0], np.float32).reshape(44, 128).T
    s[:, C_SUBG] = np.asarray(inp["subln_g"][0], np.float32)
    for col, k in ((C_LQ1, "lambda_q1"), (C_LK1, "lambda_k1"), (C_LQ2, "lambda_q2"), (C_LK2, "lambda_k2")):
        s[:, col:col + 64] = np.asarray(inp[k][0], np.float32)[None, :]
    return s


_NC = None


def make_in_maps(inputs):
    x = np.asarray(inputs["x"], np.float32)
    meta = np.asarray(inputs["meta_tokens"], np.float32)
    ident, augq, augk, masks = host_consts()
    small = pack_small(inputs)
    shared = {
        "w_in": np.ascontiguousarray(np.asarray(inputs["w_in"], np.float32)[0]),
        "w_conv_out": np.ascontiguousarray(np.asarray(inputs["w_conv_out"], np.float32)[0]),
        "w_attn_out": np.ascontiguousarray(np.asarray(inputs["w_attn_out"], np.float32)[0]),
        "w_mix_out": np.ascontiguousarray(np.asarray(inputs["w_mix_out"], np.float32)[0]),
        "w_ffn_up": np.ascontiguousarray(np.asarray(inputs["w_ffn_up"], np.float32)[0]),
        "w_ffn_down": np.ascontiguousarray(np.asarray(inputs["w_ffn_down"], np.float32)[0]),
        "small": small, "ident": ident, "augq": augq, "augk": augk, "masks": masks,
    }
    in_maps = []
    for b in range(x.shape[0]):
        h0 = np.concatenate([meta, x[b]], axis=0)
        d = dict(shared)
        d["hT0"] = np.ascontiguousarray(h0.T)
        in_maps.append(d)
    return in_maps


def kernel(**inputs):
    global _NC
    in_maps = make_in_maps(inputs)
    if _NC is None:
        _NC = build_nc()
    res = run_bass_kernel_spmd(_NC, in_maps, core_ids=list(range(len(in_maps))))
    out = np.stack([np.ascontiguousarray(np.asarray(r["outT"], np.float32).T) for r in res.results])
    return out.astype(np.float32)
```

```python
import numpy as np
import ml_dtypes
import concourse.bass as bass
import concourse.mybir as mybir
from concourse.bass_utils import run_bass_kernel_spmd
from contextlib import ExitStack

F32 = mybir.dt.float32
BF16 = mybir.dt.bfloat16
AF = mybir.ActivationFunctionType
ALU = mybir.AluOpType
AX = mybir.AxisListType

NT = 4112
NJ = 9
NKB = 33
EPS = 1e-6
LAM_INIT = 0.8 - 0.6 * 1.0
DEBUG = False
FAST_RECIP = False
TASK_INTERLEAVE = True
PSWAP = 0
STOP_AFTER = 99
SUB = 99
EXTRA = 0

C_G1, C_GPOST, C_GFPRE, C_GFPOST = 0, 8, 16, 24
C_CW = 32
C_FCW = 56
C_FCB = 188
C_SUBG = 232
C_LQ1, C_LK1, C_LQ2, C_LK2 = 233, 297, 361, 425
NSMALL = 489


def trng(j):
    t0 = 512 * j
    return t0, min(512, NT - t0)


def RECIP(e, out, in_):
    if FAST_RECIP:
        return e.reciprocal_approx_fast(out=out, in_=in_)
    return e.reciprocal(out=out, in_=in_)


class Buf:
    __slots__ = ("w", "r")

    def __init__(self, after=None):
        self.w = None
        self.r = dict(after) if after else {}


def merged(bufs):
    d = {}
    for b in bufs:
        if b.w is not None:
            s, v = b.w
            if v > d.get(s, 0):
                d[s] = v
        for s, v in b.r.items():
            if v > d.get(s, 0):
                d[s] = v
    return d


class Queue:
    def __init__(self, name, sem):
        self.name = name
        self.sem = sem
        self.n = 0
        self.ops = []
        self.waited = {}


class DSem:
    def __init__(self, key):
        self.key = key
        self.count = 0


class Prog:
    def __init__(self, nc, st):
        self.nc = nc
        self.sems = {}
        self.q = {}
        for name in ("pe", "act", "dve", "pool", "sp"):
            key = "s_" + name
            self.sems[key] = st.enter_context(nc.semaphore(key))
            self.q[name] = Queue(name, key)
        self.dring = {}
        self.dri = {}
        for qn, n in (("sp", 24), ("pool", 16)):
            ring = []
            for i in range(n):
                key = f"d_{qn}{i}"
                self.sems[key] = st.enter_context(nc.semaphore(key))
                ring.append(DSem(key))
            self.dring[qn] = ring
            self.dri[qn] = 0

    def _collect(self, qn, reads, writes):
        q = self.q[qn]
        waits = {}

        def need(s, v, raw):
            if s == q.sem and (qn == "pe" or not raw):
                return
            if v > waits.get(s, 0):
                waits[s] = v

        for b in reads:
            if b.w is not None:
                need(b.w[0], b.w[1], True)
        for b in writes:
            if b.w is not None:
                need(b.w[0], b.w[1], False)
            for s, v in b.r.items():
                need(s, v, False)
        out = []
        for s, v in waits.items():
            if v > q.waited.get(s, 0):
                q.waited[s] = v
                out.append((s, v))
        return out

    def _commit(self, tok, reads, writes):
        s, v = tok
        for b in reads:
            if v > b.r.get(s, 0):
                b.r[s] = v
        for b in writes:
            b.w = tok
            b.r = {}

    def op(self, qn, fn, reads=(), writes=()):
        q = self.q[qn]
        waits = self._collect(qn, reads, writes)
        q.n += 1
        tok = (q.sem, q.n)
        q.ops.append((waits, fn, tok, 1))
        self._commit(tok, reads, writes)
        return tok

    def mm(self, out_ap, pairs, reads, writes):
        q = self.q["pe"]
        waits = self._collect("pe", reads, writes)
        n = len(pairs)
        tok = None
        for i, (l, r) in enumerate(pairs):
            def fn(e, l=l, r=r, i=i):
                return e.matmul(out_ap, lhsT=l, rhs=r, start=(i == 0), stop=(i == n - 1))
            if i == n - 1:
                q.n += 1
                tok = (q.sem, q.n)
            q.ops.append((waits if i == 0 else [], fn, tok if i == n - 1 else None, 1))
        self._commit(tok, reads, writes)
        return tok

    def dma(self, qn, out, in_, reads=(), writes=()):
        q = self.q[qn]
        ring = self.dring[qn]
        ent = ring[self.dri[qn] % len(ring)]
        self.dri[qn] += 1
        waits = self._collect(qn, reads, writes)
        if ent.count > 0 and 16 * ent.count > q.waited.get(ent.key, 0):
            q.waited[ent.key] = 16 * ent.count
            waits.append((ent.key, 16 * ent.count))
        ent.count += 1
        tok = (ent.key, 16 * ent.count)
        q.ops.append((waits, lambda e: e.dma_start(out=out, in_=in_), tok, 16))
        self._commit(tok, reads, writes)
        return tok

    def final_wait(self, qn, toks):
        q = self.q[qn]
        waits = []
        for s, v in toks.items():
            if v > q.waited.get(s, 0):
                q.waited[s] = v
                waits.append((s, v))
        q.ops.append((waits, None, None, 0))

    def final_all(self):
        toks = {}
        for q in self.q.values():
            if q.n > 0:
                toks[q.sem] = q.n
        for ring in self.dring.values():
            for ent in ring:
                if ent.count > 0:
                    toks[ent.key] = 16 * ent.count
        self.final_wait("sp", toks)

    def replay(self, qn, e):
        for waits, fn, tok, inc in self.q[qn].ops:
            for s, v in waits:
                e.wait_ge(self.sems[s], v)
            if fn is None:
                continue
            ins = fn(e)
            if tok is not None:
                ins.then_inc(self.sems[tok[0]], inc)


def build_nc():
    nc = bass.Bass("TRN2", target_bir_lowering=False)

    def din(name, shape, dt=F32):
        return nc.dram_tensor(name, list(shape), dt, kind="ExternalInput").ap()

    hT0 = din("hT0", [1024, NT])
    w_in = din("w_in", [1024, 8192])
    w_co = din("w_conv_out", [1024, 1024])
    w_ao = din("w_attn_out", [1024, 1024])
    w_mix = din("w_mix_out", [1024, 1024])
    w_up = din("w_ffn_up", [1024, 5632])
    w_dn = din("w_ffn_down", [2816, 1024])
    small_d = din("small", [128, NSMALL])
    ident_d = din("ident", [128, 128])
    augq_d = din("augq", [8, 4, NT], BF16)
    augk_d = din("augk", [8, 4, NT], BF16)
    masks_d = din("masks", [128, 4 * 512])
    outT = nc.dram_tensor("outT", [1024, 4096], F32, kind="ExternalOutput").ap()
    skind = "ExternalOutput" if DEBUG else "Internal"
    m1_scr = nc.dram_tensor("m1_scr", [8, 128, NT], F32, kind=skind).ap()
    sgb_scr = nc.dram_tensor("sgb_scr", [8, 128, NT], F32, kind=skind).ap()
    o_scr = nc.dram_tensor("o_scr", [8, 128, NT], BF16, kind=skind).ap()
    h1_scr = nc.dram_tensor("h1_scr", [NJ, 128, 8 * 512], F32, kind=skind).ap()
    hid_scr = nc.dram_tensor("hid_scr", [NJ, 128, 22 * 512], BF16, kind=skind).ap()
    if DEBUG:
        dbgA = nc.dram_tensor("dbgA", [128, 8 * NT], BF16, kind="ExternalOutput").ap()
        dbgB = nc.dram_tensor("dbgB", [128, 8 * NT], BF16, kind="ExternalOutput").ap()
        dbgC = nc.dram_tensor("dbgC", [128, 8 * NT], BF16, kind="ExternalOutput").ap()

    hT0_v = hT0.rearrange("(c p) t -> p c t", p=128)
    outT_v = outT.rearrange("(c p) t -> p c t", p=128)

    def wsrc(w, kc0, kc, col0):
        return w.rearrange("(k p) f -> p k f", p=128)[:, kc0:kc0 + kc, col0:col0 + 128]

    off = [0]

    def A(n):
        o = off[0]
        off[0] += (n + 31) // 32 * 32
        return o

    SMALL = A(NSMALL * 4)
    IDENT = A(128 * 4)
    ONES = A(128 * 2)
    MISC = A(64 * 4)
    IDENTB = A(128 * 2)
    BIGA = A(8 * NT * 2)
    BIGB = A(8 * NT * 2)
    WST = A(4 * 4096)
    WBF = A(8 * 2048)
    TB = A(44 * 1024)
    TOTAL = off[0]

    st = ExitStack()
    with st:
        pool_t = st.enter_context(nc.sbuf_tensor("pool", [128, TOTAL // 2], BF16))
        psb = [st.enter_context(nc.psum_tensor(f"ps{i}", [128, 512], F32)) for i in range(8)]
        P = Prog(nc, st)
        block = st.enter_context(nc.Block())

        def V(offb, dt, n, p0=0, p1=128):
            e0 = offb // 2
            if dt == BF16:
                return pool_t[p0:p1, e0:e0 + n]
            return pool_t[p0:p1, e0:e0 + 2 * n].bitcast(F32)

        pbuf = [Buf() for _ in range(8)]

        def PS(i, w=512, p0=0, p1=128, c0=0):
            return psb[i][p0:p1, c0:c0 + w]

        small = V(SMALL, F32, NSMALL)
        small_b = Buf()
        ident = V(IDENT, F32, 128)
        ident_b = Buf()
        ones = V(ONES, BF16, 128)
        ones_b = Buf()
        misc = V(MISC, F32, 64)
        misc_b = Buf()

        def sc(col, p0=0, p1=128):
            return small[p0:p1, col:col + 1]

        def mcol(col):
            return misc[:, col:col + 1]

        class StopBuild(Exception):
            pass

        def check_sub(k):
            if STOP_AFTER == 5 and k > SUB:
                raise StopBuild()

        def check_stop(n):
            if n > STOP_AFTER:
                raise StopBuild()

        def body():
            P.dma("sp", small, small_d, writes=[small_b])
            P.dma("sp", ident, ident_d, writes=[ident_b])
            P.op("pool", lambda e: e.memset(ones, 1.0), writes=[ones_b])
            identb = V(IDENTB, BF16, 128)
            identb_b = Buf()
            P.op("pool", lambda e: e.tensor_copy(out=identb, in_=ident), reads=[ident_b], writes=[identb_b])
            lamtmp = V(TB, F32, 64)
            lamtmp_b = Buf()
            P.op("dve", lambda e: e.tensor_tensor(out=lamtmp, in0=small[:, C_LQ1:C_LQ1 + 64], in1=small[:, C_LK1:C_LK1 + 64], op=ALU.mult),
                 reads=[small_b], writes=[lamtmp_b])
            P.op("dve", lambda e: e.reduce_sum(out=mcol(0), in_=lamtmp, axis=AX.X), reads=[lamtmp_b], writes=[misc_b])
            P.op("dve", lambda e: e.tensor_tensor(out=lamtmp, in0=small[:, C_LQ2:C_LQ2 + 64], in1=small[:, C_LK2:C_LK2 + 64], op=ALU.mult),
                 reads=[small_b, misc_b], writes=[lamtmp_b])
            P.op("dve", lambda e: e.reduce_sum(out=mcol(1), in_=lamtmp, axis=AX.X), reads=[lamtmp_b], writes=[misc_b])
            P.op("act", lambda e: e.activation(out=misc[:, 0:2], in_=misc[:, 0:2], func=AF.Exp), reads=[misc_b], writes=[misc_b])
            P.op("dve", lambda e: e.tensor_tensor(out=mcol(2), in0=mcol(0), in1=mcol(1), op=ALU.subtract), reads=[misc_b], writes=[misc_b])
            P.op("dve", lambda e: e.tensor_scalar(out=mcol(3), in0=mcol(2), scalar1=LAM_INIT, scalar2=-1.0, op0=ALU.add, op1=ALU.mult),
                 reads=[misc_b], writes=[misc_b])
            P.op("dve", lambda e: e.tensor_scalar(out=mcol(4), in0=sc(C_SUBG), scalar1=1.0 - LAM_INIT, scalar2=None, op0=ALU.mult),
                 reads=[misc_b, small_b], writes=[misc_b])
            neglam = mcol(3)
            gsub = mcol(4)

            def bigv(base, c, t0, w, p0=0, p1=128):
                return V(base + (c * NT + t0) * 2, BF16, w, p0, p1)

            wst_b = [Buf() for _ in range(4)]
            wbf_b = [Buf() for _ in range(8)]
            wctr = [0, 0]

            deferred_casts = []

            def load_w(src, kc, dst, dst_b, eng="pool"):
                s = wctr[0] % 4
                wctr[0] += 1
                stg = V(WST + s * 4096, F32, kc * 128)
                stg3 = stg.rearrange("p (k f) -> p k f", k=kc)
                P.dma("sp", stg3, src, writes=[wst_b[s]])
                if eng == "pool":
                    P.op("pool", lambda e: e.tensor_copy(out=dst, in_=stg), reads=[wst_b[s]], writes=[dst_b])
                elif eng == "act":
                    P.op("act", lambda e: e.activation(out=dst, in_=stg, func=AF.Copy), reads=[wst_b[s]], writes=[dst_b])
                else:
                    deferred_casts.append(lambda: P.op("act", lambda e: e.activation(out=dst, in_=stg, func=AF.Copy), reads=[wst_b[s]], writes=[dst_b]))

            def ring_slot():
                s = wctr[1] % 8
                wctr[1] += 1
                return V(WBF + s * 2048, BF16, 1024), wbf_b[s]

            def wk(wv, k):
                return wv[:, k * 128:(k + 1) * 128]

            wplan = []
            wnext = [0]

            def plan_ring(srcs):
                unit = []
                views = []
                for src in srcs:
                    unit.append([src, 8, None, None])
                    views.append(None)
                wplan.append(unit)
                return len(wplan) - 1

            wviews = {}

            cast_eng = ["act"]

            def prefetch(upto):
                while wnext[0] < min(upto, len(wplan)):
                    u = wnext[0]
                    vs = []
                    for ent in wplan[u]:
                        src, kc, dst, dstb = ent
                        if dst is None:
                            dst, dstb = ring_slot()
                        load_w(src, kc, dst, dstb, eng=cast_eng[0])
                        vs.append((dst, dstb))
                    wviews[u] = vs
                    wnext[0] += 1

            U_P1 = [plan_ring([wsrc(w_in, 0, 8, c * 128), wsrc(w_in, 0, 8, 1024 + c * 128), wsrc(w_in, 0, 8, 2048 + c * 128)]) for c in range(8)]
            U_P2 = [plan_ring([wsrc(w_co, 0, 8, f * 128), wsrc(w_in, 0, 8, 6144 + f * 128), wsrc(w_in, 0, 8, 7168 + f * 128)]) for f in range(8)]
            U_P3 = [plan_ring([wsrc(w_in, 0, 8, 3072 + h * 128), wsrc(w_in, 0, 8, 4096 + h * 128), wsrc(w_in, 0, 8, 5120 + h * 128)]) for h in range(8)]
            U_P4 = [plan_ring([wsrc(w_ao, 0, 8, f * 128)]) for f in range(8)]
            U_P5 = plan_ring([wsrc(w_mix, 0, 8, f * 128) for f in range(8)])
            U_P6 = [plan_ring([wsrc(w_up, 0, 8, i * 128), wsrc(w_up, 0, 8, 2816 + i * 128)]) for i in range(22)]
            wdn_b = [Buf() for _ in range(8)]
            unit = []
            for fo in range(8):
                for (k0, kc) in ((0, 8), (8, 8), (16, 6)):
                    dst = V(BIGB + (fo * 22 + k0) * 128 * 2, BF16, kc * 128)
                    unit.append([wsrc(w_dn, k0, kc, fo * 128), kc, dst, wdn_b[fo]])
            wplan.append(unit)
            U_P7 = len(wplan) - 1

            def wdn(fo, k):
                return V(BIGB + (fo * 22 + k) * 128 * 2, BF16, 128)

            aT_b = [[Buf() for _ in range(NJ)] for _ in range(8)]
            zT_b = [[Buf() for _ in range(NJ)] for _ in range(8)]
            Tbufs = [lamtmp_b]

            def new_T(n):
                nonlocal Tbufs
                after = merged(Tbufs)
                bs = [Buf(after) for _ in range(n)]
                Tbufs = bs
                return bs

            check_stop(0)
            tb = new_T(6)
            hin_b, sq_b, sqv_b, rstd_b = tb[0:2], tb[2], tb[3], tb[4]
            HIN = [TB, TB + 16384]
            SQ = TB + 32768
            SQV = TB + 40960
            RSTD = TB + 43008

            def norm_stats(sq_view_fn, nchunks, w, bank, sq_buf, scale, sqv, sqv_b_, rstd, rstd_b_):
                P.mm(PS(bank, w), [(ones, sq_view_fn(c)) for c in range(nchunks)], reads=[ones_b, sq_buf], writes=[pbuf[bank]])
                P.op("act", lambda e: e.activation(out=sqv, in_=PS(bank, w), func=AF.Ln, scale=scale, bias=epsb),
                     reads=[pbuf[bank], misc_b], writes=[sqv_b_])
                P.op("act", lambda e: e.activation(out=rstd, in_=sqv, func=AF.Exp, scale=-0.5), reads=[sqv_b_], writes=[rstd_b_])

            P.op("dve", lambda e: e.memset(mcol(5), EPS), writes=[misc_b])
            epsb = mcol(5)

            for j in range(NJ):
                t0, w = trng(j)
                hb = hin_b[j % 2]
                hin = V(HIN[j % 2], F32, 8 * w)
                P.dma("sp", hin.rearrange("p (c t) -> p c t", c=8), hT0_v[:, :, t0:t0 + w], writes=[hb])
                if j == 1:
                    prefetch(2)
                sq = V(SQ, BF16, 8 * w)
                P.op("act", lambda e, sq=sq, hin=hin: e.activation(out=sq, in_=hin, func=AF.Square), reads=[hb], writes=[sq_b])
                sqv = V(SQV, F32, w)
                rstd = V(RSTD, F32, w)
                norm_stats(lambda c, sq=sq, w=w: sq[:, c * w:(c + 1) * w], 8, w, j % 2, sq_b, 1.0 / 1024, sqv, sqv_b, rstd, rstd_b)
                for c in range(8):
                    P.op("dve", lambda e, c=c, hin=hin, w=w, t0=t0, rstd=rstd: e.scalar_tensor_tensor(
                        out=bigv(BIGA, c, t0, w), in0=hin[:, c * w:(c + 1) * w], scalar=sc(C_G1 + c), in1=rstd, op0=ALU.mult, op1=ALU.mult),
                        reads=[hb, rstd_b, small_b], writes=[aT_b[c][j]])

            check_stop(1)
            tb = new_T(2 + 2 + 1 + 2 * NJ)
            cc_b, yb1_b = tb[0:2], tb[2:4]
            ubz_b = tb[4]
            ub_b = [tb[5:5 + NJ], tb[5 + NJ:5 + 2 * NJ]]
            CC = [TB, TB + 2048]
            YB1 = [TB + 4096, TB + 6144]
            UB = [TB + 16384, TB + 16384 + 8256]

            def dgv(base, i):
                return V(base + i * 256, BF16, 128)

            for i in range(2):
                P.op("pool", lambda e, i=i: e.memset(V(UB[i], BF16, 2), 0.0), writes=[ubz_b])

            items = [(c, j) for c in range(8) for j in range(NJ)]
            CBK = [0, 3, 6]

            def p1_A(idx):
                c, j = items[idx]
                t0, w = trng(j)
                if j == 0:
                    prefetch(U_P1[c] + 2)
                (wcb, wcb_b), (wcc, wcc_b), (wcx, wcx_b) = wviews[U_P1[c]]
                bcb = CBK[idx % 3]
                bcc = 1 + 3 * (idx % 2)
                bcx = 2 + 3 * (idx % 2)
                rd = [aT_b[k][j] for k in range(8)]
                P.mm(PS(bcb, w), [(wk(wcb, k), bigv(BIGA, k, t0, w)) for k in range(8)], reads=rd + [wcb_b], writes=[pbuf[bcb]])
                P.mm(PS(bcc, w), [(wk(wcc, k), bigv(BIGA, k, t0, w)) for k in range(8)], reads=rd + [wcc_b], writes=[pbuf[bcc]])
                P.mm(PS(bcx, w), [(wk(wcx, k), bigv(BIGA, k, t0, w)) for k in range(8)], reads=rd + [wcx_b], writes=[pbuf[bcx]])
                ccv = V(CC[idx % 2], F32, w)
                P.op("act", lambda e: e.activation(out=ccv, in_=PS(bcc, w), func=AF.Copy), reads=[pbuf[bcc]], writes=[cc_b[idx % 2]])
                ubv = V(UB[c % 2] + (2 + t0) * 2, BF16, w)
                P.op("dve", lambda e: e.tensor_tensor(out=ubv, in0=PS(bcx, w), in1=ccv, op=ALU.mult),
                     reads=[pbuf[bcx], cc_b[idx % 2]], writes=[ub_b[c % 2][j]])

            def p1_B(idx):
                c, j = items[idx]
                t0, w = trng(j)
                bcb = CBK[idx % 3]
                rd = [ub_b[c % 2][j], ubz_b] + ([ub_b[c % 2][j - 1]] if j > 0 else [])
                yb = V(YB1[idx % 2], F32, w)
                ybb = yb1_b[idx % 2]
                P.op("act", lambda e: e.activation(out=yb, in_=V(UB[c % 2] + t0 * 2, BF16, w), func=AF.Identity, scale=sc(C_CW + c), bias=0.0),
                     reads=rd + [small_b], writes=[ybb])
                P.op("dve", lambda e: e.scalar_tensor_tensor(out=yb, in0=V(UB[c % 2] + (t0 + 1) * 2, BF16, w), scalar=sc(C_CW + 8 + c), in1=yb, op0=ALU.mult, op1=ALU.add),
                     reads=rd + [ybb, small_b], writes=[ybb])
                P.op("dve", lambda e: e.scalar_tensor_tensor(out=yb, in0=V(UB[c % 2] + (t0 + 2) * 2, BF16, w), scalar=sc(C_CW + 16 + c), in1=yb, op0=ALU.mult, op1=ALU.add),
                     reads=rd + [ybb, small_b], writes=[ybb])
                P.op("dve", lambda e: e.tensor_tensor(out=bigv(BIGB, c, t0, w), in0=PS(bcb, w), in1=yb, op=ALU.mult),
                     reads=[pbuf[bcb], ybb], writes=[zT_b[c][j]])

            for idx in range(len(items) + 1):
                if idx < len(items):
                    p1_A(idx)
                if idx >= 1:
                    p1_B(idx - 1)

            if DEBUG:
                P.dma("sp", dbgA, V(BIGA, BF16, 8 * NT), reads=[b for r in aT_b for b in r])
                P.dma("sp", dbgB, V(BIGB, BF16, 8 * NT), reads=[b for r in zT_b for b in r])

            check_stop(2)
            tb = new_T(6)
            sga_b, m1o_b, sgbo_b = tb[0:2], tb[2:4], tb[4:6]
            SGA = [TB, TB + 2048]
            M1O = [TB + 4096, TB + 6144]
            SGBO = [TB + 8192, TB + 10240]
            m1s_b = [[Buf() for _ in range(NJ)] for _ in range(8)]
            sgbs_b = [[Buf() for _ in range(NJ)] for _ in range(8)]
            idx = 0
            for f in range(8):
                prefetch(U_P2[f] + 2)
                (wco, wco_b), (wga, wga_b), (wgb, wgb_b) = wviews[U_P2[f]]
                for j in range(NJ):
                    t0, w = trng(j)
                    bk = 3 * (idx % 2)
                    i2 = idx % 2
                    rda = [aT_b[k][j] for k in range(8)]
                    rdz = [zT_b[k][j] for k in range(8)]
                    P.mm(PS(bk, w), [(wk(wco, k), bigv(BIGB, k, t0, w)) for k in range(8)], reads=rdz + [wco_b], writes=[pbuf[bk]])
                    P.mm(PS(bk + 1, w), [(wk(wga, k), bigv(BIGA, k, t0, w)) for k in range(8)], reads=rda + [wga_b], writes=[pbuf[bk + 1]])
                    P.mm(PS(bk + 2, w), [(wk(wgb, k), bigv(BIGA, k, t0, w)) for k in range(8)], reads=rda + [wgb_b], writes=[pbuf[bk + 2]])
                    sga = V(SGA[i2], F32, w)
                    m1o = V(M1O[i2], F32, w)
                    sgbo = V(SGBO[i2], F32, w)
                    P.op("act", lambda e, sga=sga, bk=bk, w=w: e.activation(out=sga, in_=PS(bk + 1, w), func=AF.Sigmoid), reads=[pbuf[bk + 1]], writes=[sga_b[i2]])
                    P.op("dve", lambda e, sga=sga, bk=bk, w=w, m1o=m1o: e.tensor_tensor(out=m1o, in0=PS(bk, w), in1=sga, op=ALU.mult),
                         reads=[pbuf[bk], sga_b[i2]], writes=[m1o_b[i2]])
                    P.op("act", lambda e, sgbo=sgbo, bk=bk, w=w: e.activation(out=sgbo, in_=PS(bk + 2, w), func=AF.Sigmoid), reads=[pbuf[bk + 2]], writes=[sgbo_b[i2]])
                    P.dma("pool", m1_scr[f, :, t0:t0 + w], m1o, reads=[m1o_b[i2]], writes=[m1s_b[f][j]])
                    P.dma("pool", sgb_scr[f, :, t0:t0 + w], sgbo, reads=[sgbo_b[i2]], writes=[sgbs_b[f][j]])
                    idx += 1

            check_stop(3)
            after_B = merged([b for r in zT_b for b in r])
            QA, QB, KA, KB = BIGB, BIGB + 8224, BIGB + 2 * 8224, BIGB + 3 * 8224
            VV = BIGB + 4 * 8224
            MSK = VV + 8544
            PT = MSK + 8192
            SSB = PT + 8192
            assert SSB + 4096 <= BIGB + 8 * NT * 2
            qz_b = Buf(after_B)
            qaug_b = Buf(after_B)
            qA_b = [Buf(after_B) for _ in range(NJ)]
            qB_b = [Buf(after_B) for _ in range(NJ)]
            kA_b = [Buf(after_B) for _ in range(NJ)]
            kB_b = [Buf(after_B) for _ in range(NJ)]
            v_b = [Buf(after_B) for _ in range(9)]
            vone_b = Buf(after_B)
            msk_b = Buf(after_B)
            pt_b = [Buf(after_B) for _ in range(8)]
            ssb_b = [Buf(after_B) for _ in range(2)]
            tb = new_T(4 + 4 + 4 + 4 + 4 + 6 + 2 * NJ + 9 + 3 + NJ)
            r0_b, r1_b, t0_b, osb_b, on_b = tb[0:4], tb[4:8], tb[8:12], tb[12:16], tb[16:20]
            ssq_b, lnv_b, rstd3_b, junk_b = tb[20], tb[21], tb[22], tb[23]
            oout_b = tb[24:26]
            RS = TB
            T0S = TB + 128
            OSBS = T0S + 2048
            JUNK = OSBS + 2048
            ONS = JUNK + 512
            OOUT = [ONS + 1024, ONS + 2048]
            KA1 = TB + 8192
            KB1 = KA1 + 8224
            VV1 = KB1 + 8224
            assert VV1 + 8544 <= TB + 44 * 1024
            KAs, KBs, VVs = [KA, KA1], [KB, KB1], [VV, VV1]
            kA_bs = [kA_b, tb[26:26 + NJ]]
            kB_bs = [kB_b, tb[26 + NJ:26 + 2 * NJ]]
            v_bs = [v_b, tb[26 + 2 * NJ:26 + 2 * NJ + 9]]
            kz1_b, vone1_b, kaug1_b = tb[26 + 2 * NJ + 9], tb[26 + 2 * NJ + 10], tb[26 + 2 * NJ + 11]
            qaug_bt = [Buf(after_B) for _ in range(NJ)]
            kaug_bs = [qaug_b, kaug1_b]
            vone_bs = [vone_b, vone1_b]
            os_b = [[Buf() for _ in range(NJ)] for _ in range(8)]
            oT_b = [[Buf() for _ in range(NJ)] for _ in range(8)]

            for base in (QA, KA):
                P.op("pool", lambda e, base=base: e.memset(V(base, BF16, NT, 64, 128), 0.0), writes=[qz_b])
            for base in (QB, KB):
                P.op("pool", lambda e, base=base: e.memset(V(base, BF16, NT, 0, 64), 0.0), writes=[qz_b])
            mskv = V(MSK, F32, 4 * 512)
            P.dma("sp", mskv, masks_d, writes=[msk_b])
            P.op("pool", lambda e: e.memset(V(VV, BF16, 33 * 129).rearrange("p (b c) -> p b c", c=129)[:, :, 128:129], 1.0), writes=[vone_b])
            P.op("pool", lambda e: e.memset(V(KA1, BF16, NT, 64, 128), 0.0), writes=[kz1_b])
            P.op("pool", lambda e: e.memset(V(KB1, BF16, NT, 0, 64), 0.0), writes=[kz1_b])
            P.op("pool", lambda e: e.memset(V(VV1, BF16, 33 * 129).rearrange("p (b c) -> p b c", c=129)[:, :, 128:129], 1.0), writes=[vone1_b])

            tasks = []

            def q_task(hh, j):
                def f():
                    t0, w = trng(j)
                    wq_, wq_b_ = wviews[U_P3[hh]][0]
                    bk = sctr[0] % 4
                    sctr[0] += 1
                    P.mm(PS(bk, w), [(wk(wq_, k), bigv(BIGA, k, t0, w)) for k in range(8)], reads=[aT_b[k][j] for k in range(8)] + [wq_b_], writes=[pbuf[bk]])
                    P.op("dve", lambda e: e.tensor_scalar(out=V(QA + t0 * 2, BF16, w, 0, 64), in0=PS(bk, w, 0, 64), scalar1=0.125, scalar2=None, op0=ALU.mult),
                         reads=[pbuf[bk]], writes=[qA_b[j]])
                    P.op("dve", lambda e: e.tensor_scalar(out=V(QB + t0 * 2, BF16, w, 64, 128), in0=PS(bk, w, 64, 128), scalar1=0.125, scalar2=None, op0=ALU.mult),
                         reads=[pbuf[bk]], writes=[qB_b[j]])
                return f

            def k_task(hh, j):
                def f():
                    p = (hh + PSWAP) % 2
                    t0, w = trng(j)
                    wk_, wk_b_ = wviews[U_P3[hh]][1]
                    bk = sctr[0] % 4
                    sctr[0] += 1
                    P.mm(PS(bk, w), [(wk(wk_, k), bigv(BIGA, k, t0, w)) for k in range(8)], reads=[aT_b[k][j] for k in range(8)] + [wk_b_], writes=[pbuf[bk]])
                    P.op("dve", lambda e: e.tensor_copy(out=V(KAs[p] + t0 * 2, BF16, w, 0, 64), in_=PS(bk, w, 0, 64)), reads=[pbuf[bk]], writes=[kA_bs[p][j]])
                    P.op("dve", lambda e: e.tensor_copy(out=V(KBs[p] + t0 * 2, BF16, w, 64, 128), in_=PS(bk, w, 64, 128)), reads=[pbuf[bk]], writes=[kB_bs[p][j]])
                return f

            def v_task(hh, g):
                def f():
                    p = (hh + PSWAP) % 2
                    wv_, wv_b_ = wviews[U_P3[hh]][2]
                    bk = sctr[0] % 4
                    sctr[0] += 1
                    nblk = 4 if g < 8 else 1
                    for bi in range(nblk):
                        tbk = g * 4 + bi
                        tw = 128 if tbk < 32 else 16
                        jj = tbk // 4
                        P.mm(psb[bk][0:tw, bi * 128:(bi + 1) * 128], [(bigv(BIGA, k, tbk * 128, tw), wk(wv_, k)) for k in range(8)],
                             reads=[aT_b[k][jj] for k in range(8)] + [wv_b_], writes=[pbuf[bk]])
                    tw = 128 if g < 8 else 16
                    vdst = V(VVs[p] + g * 4 * 129 * 2, BF16, nblk * 129, 0, tw).rearrange("p (b c) -> p b c", c=129)[:, :, 0:128]
                    vsrc = psb[bk][0:tw, 0:nblk * 128].rearrange("p (b c) -> p b c", c=128)
                    P.op("dve", lambda e: e.tensor_copy(out=vdst, in_=vsrc), reads=[pbuf[bk]], writes=[v_bs[p][g]])
                return f

            def kaug_task(hh):
                def f():
                    p = (hh + PSWAP) % 2
                    zb_ = qz_b if p == 0 else kz1_b
                    P.dma("sp", V(KAs[p], BF16, NT, 64, 68), augk_d[hh], reads=[zb_], writes=[kaug_bs[p]])
                    P.dma("sp", V(KBs[p], BF16, NT, 0, 4), augk_d[hh], reads=[zb_], writes=[kaug_bs[p]])
                return f
            sctr = [0]

            cast_eng[0] = "pool"
            for h in range(8):
                prefetch(U_P3[h] + 2)
                (wq, wq_b), (wkk, wkk_b), (wv, wv_b) = wviews[U_P3[h]]
                pset = (h + PSWAP) % 2
                P.dma("sp", V(QA, BF16, NT, 64, 68), augq_d[h], reads=[qz_b], writes=[qaug_bt[0]])
                P.dma("sp", V(QB, BF16, NT, 0, 4), augq_d[h], reads=[qz_b], writes=[qaug_bt[0]])
                if h == 0:
                    kaug_task(0)()
                    for j in range(NJ):
                        q_task(0, j)()
                        k_task(0, j)()
                    for g in range(9):
                        v_task(0, g)()
                if h < 7:
                    tasks.append(kaug_task(h + 1))
                    for j in range(NJ):
                        tasks.append(k_task(h + 1, j))
                    for g in range(9):
                        tasks.append(v_task(h + 1, g))

                sweep = []
                for j in range(NJ):
                    nkb = 4 * j + 4 if j < 8 else NKB
                    for m in range(2):
                        if j < 8:
                            for kb in range(nkb):
                                sweep.append((j, m, kb, kb == nkb - 1, 1))
                        else:
                            for g_ in range(8):
                                sweep.append((j, m, 4 * g_, False, 4))
                            sweep.append((j, m, 32, True, 1))
                pending = []

                def emit_S(i):
                    j, m, kb, last, nblk = sweep[i]
                    t0, w = trng(j)
                    kw = 128 if kb < 32 else 16
                    bk = sctr[0] % 4
                    sctr[0] += 1
                    Qb, Kb = (QA, KAs[pset]) if m == 0 else (QB, KBs[pset])
                    qb_, kb_ = (qA_b, kA_bs[pset]) if m == 0 else (qB_b, kB_bs[pset])
                    xrd = [qaug_bt[0], kaug_bs[pset], qz_b] + ([kz1_b] if pset == 1 else [])
                    if nblk == 4:
                        rd = [qb_[j], kb_[kb // 4]] + xrd
                        q = P.q["pe"]
                        waits = P._collect("pe", rd, [pbuf[bk]])
                        q.n += 1
                        tok = (q.sem, q.n)
                        for b_ in range(4):
                            def fn(e, b_=b_):
                                return e.matmul(PS(bk, w, 0, 128, b_ * w), lhsT=V(Kb + (kb + b_) * 128 * 2, BF16, 128), rhs=V(Qb + t0 * 2, BF16, w),
                                                start=True, stop=True, skip_group_check=True)
                            q.ops.append((waits if b_ == 0 else [], fn, tok if b_ == 3 else None, 1))
                        P._commit(tok, rd, [pbuf[bk]])
                        ptv4 = V(PT + (i % 8) * 1024, BF16, 4 * w)
                        P.op("act", lambda e: e.activation(out=ptv4, in_=PS(bk, 4 * w), func=AF.Exp), reads=[pbuf[bk]], writes=[pt_b[i % 8]])
                        return
                    diag = (kb >= 4 * j)
                    r = kb - 4 * j if (diag and j < 8) else 0
                    c0 = 128 * r
                    wc = w - c0
                    P.mm(PS(bk, wc, 0, kw, c0), [(V(Kb + kb * 128 * 2, BF16, kw), V(Qb + (t0 + c0) * 2, BF16, wc))],
                         reads=[qb_[j], kb_[kb // 4]] + xrd, writes=[pbuf[bk]])
                    ptv = V(PT + (i % 8) * 1024 + c0 * 2, BF16, wc, 0, kw)
                    if diag:
                        ssv = V(SSB + (i % 2) * 2048 + c0 * 4, F32, wc, 0, kw)
                        P.op("dve", lambda e: e.tensor_tensor(out=ssv, in0=PS(bk, wc, 0, kw, c0), in1=V(MSK + r * 2048 + c0 * 4, F32, wc, 0, kw), op=ALU.add),
                             reads=[pbuf[bk], msk_b], writes=[ssb_b[i % 2]])
                        P.op("act", lambda e: e.activation(out=ptv, in_=ssv, func=AF.Exp), reads=[ssb_b[i % 2]], writes=[pt_b[i % 8]])
                    else:
                        P.op("act", lambda e: e.activation(out=ptv, in_=PS(bk, wc, 0, kw, c0), func=AF.Exp), reads=[pbuf[bk]], writes=[pt_b[i % 8]])

                def emit_PV(i):
                    j, m, kb, last, nblk = sweep[i]
                    t0, w = trng(j)
                    kw = 128 if kb < 32 else 16
                    r = kb - 4 * j if (kb >= 4 * j and j < 8) else 0
                    first = (kb == 0)
                    g = kb // 4
                    nqs = 4 if j < 8 else 1
                    qw = 128 if j < 8 else 16
                    if nblk == 4:
                        bank_ = pbuf[4 + 2 * m]
                        rd = [pt_b[i % 8], v_bs[pset][g], vone_bs[pset]]
                        q = P.q["pe"]
                        waits = P._collect("pe", rd, [bank_] if first else [])
                        q.n += 1
                        tok = (q.sem, q.n)
                        vbase = VVs[pset]
                        for b_ in range(4):
                            def fn(e, b_=b_, vbase=vbase):
                                return e.matmul(psb[4 + 2 * m][0:w, 0:129], lhsT=V(PT + (i % 8) * 1024 + b_ * w * 2, BF16, w),
                                                rhs=V(vbase + (kb + b_) * 129 * 2, BF16, 129), start=(first and b_ == 0), stop=False, skip_group_check=True)
                            q.ops.append((waits if b_ == 0 else [], fn, tok if b_ == 3 else None, 1))
                        P._commit(tok, rd, [bank_])
                        return
                    vblk = V(VVs[pset] + kb * 129 * 2, BF16, 129, 0, kw)
                    banks = [pbuf[4 + 2 * m], pbuf[5 + 2 * m]] if j < 8 else [pbuf[4 + 2 * m]]
                    reads = [pt_b[i % 8], v_bs[pset][g], vone_bs[pset]]
                    q = P.q["pe"]
                    waits = P._collect("pe", reads, banks if first else [])
                    fns = []
                    for qs in range(r, nqs):
                        bank = 4 + 2 * m + qs // 2
                        col = (qs % 2) * 129
                        out_ap = psb[bank][0:qw, col:col + 129]
                        lhsT = V(PT + (i % 8) * 1024 + qs * 128 * 2, BF16, qw, 0, kw)
                        st = first and (qs % 2 == 0)
                        lastq = (kb == (4 * j + qs if j < 8 else 32))

                        def fn(e, out_ap=out_ap, lhsT=lhsT, st=st, lastq=lastq):
                            return e.matmul(out_ap, lhsT=lhsT, rhs=vblk, start=st, stop=lastq, skip_group_check=True)
                        fns.append(fn)
                    q.n += 1
                    tok = (q.sem, q.n)
                    for k_, fn in enumerate(fns):
                        q.ops.append((waits if k_ == 0 else [], fn, tok if k_ == len(fns) - 1 else None, 1))
                    P._commit(tok, reads, banks)
                    if last:
                        emit_norm(i, j, m)
                        if m == 1 and h < 7:
                            tasks.append(q_task(h + 1, j))

                def emit_norm(i, j, m):
                    t0, w = trng(j)
                    nqs = 4 if j < 8 else 1
                    qw = 128 if j < 8 else 16

                    def oreg(qs, c0, c1):
                        bank = 4 + 2 * m + qs // 2
                        col = (qs % 2) * 129
                        return psb[bank][0:qw, col + c0:col + c1], pbuf[bank]

                    def rsc(c):
                        return V(RS + c * 4, F32, 1, 0, qw)

                    def T0v(qs):
                        return V(T0S + qs * 512, F32, 128, 0, qw)

                    def OSBv(qs):
                        return V(OSBS + qs * 512, F32, 128, 0, qw)

                    def ONv(qs):
                        return V(ONS + qs * 256, BF16, 128, 0, qw)
                    steps = []
                    for qs in range(nqs):
                        def f(qs=qs):
                            oap, ob_ = oreg(qs, 0, 128)
                            sap, _ = oreg(qs, 128, 129)
                            if m == 0:
                                P.op("dve", lambda e: e.reciprocal(out=rsc(qs), in_=sap), reads=[ob_], writes=[r0_b[qs]])
                                P.op("dve", lambda e: e.tensor_scalar(out=T0v(qs), in0=oap, scalar1=rsc(qs), scalar2=None, op0=ALU.mult),
                                     reads=[ob_, r0_b[qs]], writes=[t0_b[qs]])
                            else:
                                P.op("dve", lambda e: e.reciprocal(out=rsc(4 + qs), in_=sap), reads=[ob_], writes=[r1_b[qs]])
                                P.op("dve", lambda e: e.tensor_tensor(out=rsc(8 + qs), in0=rsc(4 + qs), in1=misc[0:qw, 3:4], op=ALU.mult),
                                     reads=[r1_b[qs], misc_b], writes=[r1_b[qs]])
                                P.op("dve", lambda e: e.scalar_tensor_tensor(out=OSBv(qs), in0=oap, scalar=rsc(8 + qs), in1=T0v(qs), op0=ALU.mult, op1=ALU.add),
                                     reads=[ob_, r1_b[qs], t0_b[qs]], writes=[osb_b[qs]])
                        steps.append((qs, f))
                    if m == 1:
                        def s_sumsq():
                            for qs in range(nqs):
                                P.op("dve", lambda e, qs=qs: e.scalar_tensor_tensor(out=V(JUNK, F32, 128, 0, qw), in0=OSBv(qs), scalar=1.0, in1=OSBv(qs),
                                                                                    op0=ALU.mult, op1=ALU.mult, accum_out=rsc(12 + qs)),
                                     reads=[osb_b[qs]], writes=[junk_b, ssq_b])
                        steps.append((5, s_sumsq))
                        steps.append((7, lambda: P.op("act", lambda e: e.activation(out=V(RS + 16 * 4, F32, nqs, 0, qw), in_=V(RS + 12 * 4, F32, nqs, 0, qw),
                                                                                       func=AF.Ln, scale=1.0 / 128, bias=misc[0:qw, 5:6]),
                                                      reads=[ssq_b, misc_b], writes=[lnv_b])))
                        steps.append((8, lambda: P.op("act", lambda e: e.activation(out=V(RS + 20 * 4, F32, nqs, 0, qw), in_=V(RS + 16 * 4, F32, nqs, 0, qw),
                                                                                       func=AF.Exp, scale=-0.5),
                                                      reads=[lnv_b], writes=[rstd3_b])))

                        def s_scale():
                            for qs in range(nqs):
                                P.op("dve", lambda e, qs=qs: e.tensor_scalar(out=ONv(qs), in0=OSBv(qs), scalar1=rsc(20 + qs), scalar2=None, op0=ALU.mult),
                                     reads=[osb_b[qs], rstd3_b], writes=[on_b[qs]])
                        steps.append((9, s_scale))

                        def s_out():
                            sb_ = sctr[0] % 4
                            sctr[0] += 1
                            psbf = psb[sb_][:, :].bitcast(BF16)
                            q = P.q["pe"]
                            rd = [on_b[qs] for qs in range(nqs)] + [identb_b]
                            waits = P._collect("pe", rd, [pbuf[sb_]])
                            q.n += 1
                            tok = (q.sem, q.n)
                            for qs in range(nqs):
                                def fn(e, qs=qs):
                                    return e.transpose(psbf[:, qs * 128:qs * 128 + qw], ONv(qs), identb[0:qw, 0:qw])
                                q.ops.append((waits if qs == 0 else [], fn, tok if qs == nqs - 1 else None, 1))
                            P._commit(tok, rd, [pbuf[sb_]])
                            if h < 7:
                                oo = V(OOUT[j % 2], BF16, w)
                                P.op("dve", lambda e: e.tensor_scalar(out=oo, in0=psbf[:, 0:w], scalar1=gsub, scalar2=None, op0=ALU.mult),
                                     reads=[pbuf[sb_], misc_b], writes=[oout_b[j % 2]])
                                P.dma("pool", o_scr[h, :, t0:t0 + w], oo, reads=[oout_b[j % 2]], writes=[os_b[h][j]])
                            else:
                                P.op("dve", lambda e: e.tensor_scalar(out=bigv(BIGA, 7, t0, w), in0=psbf[:, 0:w], scalar1=gsub, scalar2=None, op0=ALU.mult),
                                     reads=[pbuf[sb_], misc_b], writes=[oT_b[7][j], aT_b[7][j]])
                        steps.append((11, s_out))
                    for off_, f_ in steps:
                        pending.append((i + 3 + off_, f_))
                    pending.sort(key=lambda x: x[0])

                n = len(sweep)
                LA = 3
                for i in range(n + LA):
                    if i < n:
                        emit_S(i)
                    if i >= LA:
                        emit_PV(i - LA)
                    while pending and pending[0][0] <= i:
                        pending.pop(0)[1]()
                    if TASK_INTERLEAVE and i % 8 == 5 and tasks:
                        tasks.pop(0)()
                while pending:
                    pending.pop(0)[1]()
                while tasks:
                    tasks.pop(0)()

                if h == 7:
                    pass
                if h == 6:
                    pass

            cast_eng[0] = "act"
            for hh in range(7):
                P.dma("sp", bigv(BIGA, hh, 0, NT), o_scr[hh, :, :], reads=os_b[hh], writes=oT_b[hh] + aT_b[hh])

            if DEBUG:
                P.dma("sp", dbgC, V(BIGA, BF16, 8 * NT), reads=[b for r in oT_b for b in r])

            check_stop(4)
            after_B = merged([qz_b, qaug_b, msk_b, vone_b] + qaug_bt + qA_b + qB_b + kA_b + kB_b + v_b + pt_b + ssb_b)
            m2_b = [[Buf(after_B) for _ in range(NJ)] for _ in range(8)]
            NS4 = 6
            tb = new_T(2 * NS4 + 4)
            m1i_b, sgi_b, tmp_b = tb[0:NS4], tb[NS4:2 * NS4], tb[2 * NS4:2 * NS4 + 4]
            M1I = [TB + 2048 * i for i in range(NS4)]
            SGI = [TB + 2048 * (NS4 + i) for i in range(NS4)]
            TMP = [TB + 2048 * (2 * NS4 + i) for i in range(4)]
            idx = 0
            for f in range(8):
                prefetch(min(U_P4[f] + 3, U_P5))
                (wao, wao_b), = wviews[U_P4[f]]
                for j in range(NJ):
                    t0, w = trng(j)
                    bk = idx % 4
                    i3 = idx % NS4
                    m1i = V(M1I[i3], F32, w)
                    sgi = V(SGI[i3], F32, w)
                    P.dma("sp", m1i, m1_scr[f, :, t0:t0 + w], reads=[m1s_b[f][j]], writes=[m1i_b[i3]])
                    P.dma("sp", sgi, sgb_scr[f, :, t0:t0 + w], reads=[sgbs_b[f][j]], writes=[sgi_b[i3]])
                    P.mm(PS(bk, w), [(wk(wao, k), bigv(BIGA, k, t0, w)) for k in range(8)], reads=[oT_b[k][j] for k in range(8)] + [wao_b], writes=[pbuf[bk]])
                    tmp = V(TMP[idx % 4], F32, w)
                    P.op("dve", lambda e, tmp=tmp, bk=bk, w=w, sgi=sgi: e.tensor_tensor(out=tmp, in0=PS(bk, w), in1=sgi, op=ALU.mult),
                         reads=[pbuf[bk], sgi_b[i3]], writes=[tmp_b[idx % 4]])
                    P.op("pool", lambda e, tmp=tmp, m1i=m1i, f=f, t0=t0, w=w: e.tensor_tensor(out=bigv(BIGB, f, t0, w), in0=tmp, in1=m1i, op=ALU.add),
                         reads=[tmp_b[idx % 4], m1i_b[i3]], writes=[m2_b[f][j]])
                    idx += 1

            if EXTRA:
                xb = Buf(merged(Tbufs))
                for _ in range(EXTRA):
                    P.dma("sp", V(TB, F32, 4096).rearrange("p (c t) -> p c t", c=8), hT0_v[:, :, 0:512], writes=[xb])
            check_stop(5)
            prefetch(U_P5 + 1)
            wmix = wviews[U_P5]
            tb = new_T(16 + 8 + 2)
            mix_b = [tb[0:8], tb[8:16]]
            msq0_b = tb[16:24]
            sqv5_b, rstd5_b = tb[24], tb[25]
            after_wst = merged(wst_b)
            h0_b = [Buf(after_wst) for _ in range(8)]
            MIX = [TB, TB + 16384]
            MSQ0, SQV5, RSTD5 = TB + 32768, TB + 40960, TB + 43008
            H0 = WST
            h1s_b = [Buf() for _ in range(NJ)]
            fT_b = [[Buf() for _ in range(NJ)] for _ in range(8)]

            def p5_A(j, part):
                t0, w = trng(j)
                p = j % 2
                mixv = V(MIX[p], F32, 8 * w)
                for fo in range(4 * part, 4 * part + 4):
                    bk = fo % 4
                    wm, wm_b = wmix[fo]
                    P.mm(PS(bk, w), [(wk(wm, k), bigv(BIGB, k, t0, w)) for k in range(8)], reads=[m2_b[k][j] for k in range(8)] + [wm_b], writes=[pbuf[bk]])
                    P.op("dve", lambda e, fo=fo, bk=bk: e.tensor_copy(out=mixv[:, fo * w:(fo + 1) * w], in_=PS(bk, w)),
                         reads=[pbuf[bk]], writes=[mix_b[p][fo]])
                    if j == 0:
                        dst, dstb = V(MSQ0 + fo * w * 2, BF16, w), msq0_b[fo]
                    else:
                        pt0, _ = trng(j - 1)
                        dst, dstb = bigv(BIGB, fo, pt0, w), m2_b[fo][j - 1]
                    P.op("act", lambda e, fo=fo, dst=dst: e.activation(out=dst, in_=mixv[:, fo * w:(fo + 1) * w], func=AF.Square),
                         reads=[mix_b[p][fo]], writes=[dstb])

            def p5_B(j, half):
                t0, w = trng(j)
                p = j % 2
                mixv = V(MIX[p], F32, 8 * w)
                h0 = V(H0, F32, 8 * w)
                sqv = V(SQV5, F32, w)
                rstd = V(RSTD5, F32, w)
                sq2 = [(V(MSQ0 + c * w * 2, BF16, w), msq0_b[c]) for c in range(8)]
                if half == 1:
                    P.mm(PS(7, w), [(ones, v_) for v_, _ in sq2], reads=[ones_b] + [b_ for _, b_ in sq2], writes=[pbuf[7]])
                    P.op("act", lambda e: e.activation(out=sqv, in_=PS(7, w), func=AF.Ln, scale=1.0 / 1024, bias=epsb), reads=[pbuf[7], misc_b], writes=[sqv5_b])
                    P.op("act", lambda e: e.activation(out=rstd, in_=sqv, func=AF.Exp, scale=-0.5), reads=[sqv5_b], writes=[rstd5_b])
                    for c in range(8):
                        P.op("dve", lambda e, c=c: e.scalar_tensor_tensor(
                            out=bigv(BIGA, c, t0, w), in0=h0[:, c * w:(c + 1) * w], scalar=sc(C_GFPRE + c), in1=rstd, op0=ALU.mult, op1=ALU.mult),
                            reads=[h0_b[c], rstd5_b, small_b], writes=[fT_b[c][j], oT_b[c][j]])
                    return
                if j == 0:
                    sqs = [(V(MSQ0 + c * w * 2, BF16, w), msq0_b[c]) for c in range(8)]
                else:
                    pt0, _ = trng(j - 1)
                    sqs = [(bigv(BIGB, c, pt0, w), m2_b[c][j - 1]) for c in range(8)]
                P.mm(PS(6, w), [(ones, v_) for v_, _ in sqs], reads=[ones_b] + [b_ for _, b_ in sqs], writes=[pbuf[6]])
                P.op("act", lambda e: e.activation(out=sqv, in_=PS(6, w), func=AF.Ln, scale=1.0 / 1024, bias=epsb), reads=[pbuf[6], misc_b], writes=[sqv5_b])
                P.op("act", lambda e: e.activation(out=rstd, in_=sqv, func=AF.Exp, scale=-0.5), reads=[sqv5_b], writes=[rstd5_b])
                for fo in range(8):
                    P.op("dve", lambda e, fo=fo: e.scalar_tensor_tensor(
                        out=mixv[:, fo * w:(fo + 1) * w], in0=mixv[:, fo * w:(fo + 1) * w], scalar=sc(C_GPOST + fo), in1=rstd, op0=ALU.mult, op1=ALU.mult),
                        reads=[mix_b[p][fo], rstd5_b, small_b], writes=[mix_b[p][fo]])
                for fo in range(8):
                    eng = "pool" if fo % 2 == 0 else "dve"
                    P.op(eng, lambda e, fo=fo: e.tensor_tensor(out=h0[:, fo * w:(fo + 1) * w], in0=h0[:, fo * w:(fo + 1) * w], in1=mixv[:, fo * w:(fo + 1) * w], op=ALU.add),
                         reads=[h0_b[fo], mix_b[p][fo]], writes=[h0_b[fo]])
                    P.op("act", lambda e, fo=fo: e.activation(out=sq2[fo][0], in_=h0[:, fo * w:(fo + 1) * w], func=AF.Square),
                         reads=[h0_b[fo]], writes=[sq2[fo][1]])
                P.dma("pool", h1_scr[j, :, 0:8 * w], h0, reads=h0_b, writes=[h1s_b[j]])

            for j in range(NJ + 1):
                if j < NJ:
                    p5_A(j, 0)
                if j >= 1:
                    p5_B(j - 1, 0)
                if j < NJ:
                    p5_A(j, 1)
                if j >= 1:
                    p5_B(j - 1, 1)
                if j < NJ:
                    t0, w = trng(j)
                    P.dma("sp", V(H0, F32, 8 * w).rearrange("p (c t) -> p c t", c=8), hT0_v[:, :, t0:t0 + w], writes=h0_b)
            after_h0 = merged(h0_b)
            for b in wst_b:
                for k_, v_ in after_h0.items():
                    if v_ > b.r.get(k_, 0):
                        b.r[k_] = v_

            check_stop(6)
            tb = new_T(2 + 2 + 2 + 2 + 1 + 4 * NJ)
            dg6_b, sg6_b, hid_b, yb_b, ubz6_b = tb[0:2], tb[2:4], tb[4:6], tb[6:8], tb[8]
            ug_b = [tb[9:9 + NJ], tb[9 + NJ:9 + 2 * NJ]]
            uu_b = [tb[9 + 2 * NJ:9 + 3 * NJ], tb[9 + 3 * NJ:9 + 4 * NJ]]
            UG = [TB, TB + 8256]
            UU = [TB + 2 * 8256, TB + 3 * 8256]
            DG6 = [TB + 4 * 8256, TB + 4 * 8256 + 768]
            SG6 = [TB + 4 * 8256 + 1536, TB + 4 * 8256 + 1536 + 2048]
            YB = [TB + 4 * 8256 + 1536 + 4096, TB + 4 * 8256 + 1536 + 6144]
            HID = [TB + 4 * 8256 + 1536 + 8192 + i * 1024 for i in range(2)]
            assert HID[1] + 1024 <= TB + 44 * 1024
            hids_b = [[Buf() for _ in range(NJ)] for _ in range(22)]
            for base in UG + UU:
                P.op("pool", lambda e, base=base: e.memset(V(base, BF16, 2), 0.0), writes=[ubz6_b])
            items6 = [(i, j) for i in range(22) for j in range(NJ)]
            after_m2 = merged([b for r in m2_b for b in r])
            for b in wdn_b:
                b.r = dict(after_m2)
            UBK = [2, 3, 6]

            wd_left = list(wplan[U_P7])
            cast_eng[0] = "act_later"

            def p6_A(idx):
                i, j = items6[idx]
                t0, w = trng(j)
                if j == 0:
                    prefetch(min(U_P6[i] + 2, U_P7))
                    if i == 0:
                        while deferred_casts:
                            deferred_casts.pop(0)()
                    for s in range(3):
                        P.op("dve", lambda e, i=i, s=s: e.tensor_scalar(out=dgv(DG6[i % 2], s), in0=ident, scalar1=sc(C_FCW + s * 44 + i), scalar2=None, op0=ALU.mult),
                             reads=[ident_b, small_b], writes=[dg6_b[i % 2]])
                if j >= 1 and deferred_casts:
                    deferred_casts.pop(0)()
                if idx >= 12 * NJ and wd_left:
                    src_, kc_, dst_, dstb_ = wd_left.pop(0)
                    load_w(src_, kc_, dst_, dstb_, eng="act")
                (wg, wg_b), (wu, wu_b) = wviews[U_P6[i]]
                bg = idx % 2
                bu = UBK[idx % 3]
                rd = [fT_b[k][j] for k in range(8)]
                P.mm(PS(bg, w), [(wk(wg, k), bigv(BIGA, k, t0, w)) for k in range(8)], reads=rd + [wg_b], writes=[pbuf[bg]])
                P.mm(PS(bu, w), [(wk(wu, k), bigv(BIGA, k, t0, w)) for k in range(8)], reads=rd + [wu_b], writes=[pbuf[bu]])
                P.op("act", lambda e: e.activation(out=V(UG[i % 2] + (2 + t0) * 2, BF16, w), in_=PS(bg, w), func=AF.Copy), reads=[pbuf[bg]], writes=[ug_b[i % 2][j]])
                P.op("act", lambda e: e.activation(out=V(UU[i % 2] + (2 + t0) * 2, BF16, w), in_=PS(bu, w), func=AF.Copy), reads=[pbuf[bu]], writes=[uu_b[i % 2][j]])

            def p6_B(idx):
                i, j = items6[idx]
                t0, w = trng(j)
                bk = 4 + idx % 2
                bu = UBK[idx % 3]
                rdg = [ug_b[i % 2][j], dg6_b[i % 2], ubz6_b] + ([ug_b[i % 2][j - 1]] if j > 0 else [])
                rdu = [uu_b[i % 2][j], ubz6_b] + ([uu_b[i % 2][j - 1]] if j > 0 else [])
                P.mm(PS(bk, w), [(dgv(DG6[i % 2], s), V(UG[i % 2] + (t0 + s) * 2, BF16, w)) for s in range(3)], reads=rdg, writes=[pbuf[bk]])
                yb = V(YB[idx % 2], F32, w)
                ybb = yb_b[idx % 2]
                P.op("act", lambda e: e.activation(out=yb, in_=V(UU[i % 2] + t0 * 2, BF16, w), func=AF.Identity, scale=sc(C_FCW + 22 + i), bias=sc(C_FCB + 22 + i)),
                     reads=rdu + [small_b], writes=[ybb])
                P.op("dve", lambda e: e.scalar_tensor_tensor(out=yb, in0=V(UU[i % 2] + (t0 + 1) * 2, BF16, w), scalar=sc(C_FCW + 44 + 22 + i), in1=yb, op0=ALU.mult, op1=ALU.add),
                     reads=rdu + [ybb, small_b], writes=[ybb])
                sg = V(SG6[idx % 2], F32, w)
                P.op("act", lambda e: e.activation(out=sg, in_=PS(bk, w), func=AF.Silu, bias=sc(C_FCB + i)), reads=[pbuf[bk], small_b], writes=[sg6_b[idx % 2]])
                P.op("dve", lambda e: e.scalar_tensor_tensor(out=yb, in0=PS(bu, w), scalar=sc(C_FCW + 88 + 22 + i), in1=yb, op0=ALU.mult, op1=ALU.add),
                     reads=[pbuf[bu], ybb, small_b], writes=[ybb])
                hv = V(HID[idx % 2], BF16, w)
                P.op("dve", lambda e: e.tensor_tensor(out=hv, in0=yb, in1=sg, op=ALU.mult),
                     reads=[ybb, sg6_b[idx % 2]], writes=[hid_b[idx % 2]])
                P.dma("pool", hid_scr[j, :, i * w:(i + 1) * w], hv, reads=[hid_b[idx % 2]], writes=[hids_b[i][j]])

            for idx in range(len(items6) + 1):
                if idx < len(items6):
                    p6_A(idx)
                if idx >= 1:
                    p6_B(idx - 1)

            check_stop(7)
            assert not wd_left and wnext[0] == U_P7
            wnext[0] = U_P7 + 1
            tb = new_T(2)
            hd_b = tb[0:2]
            HD = [TB, TB + 22528]
            after_A = merged([b for r in fT_b for b in r])
            after_W = merged(wbf_b)
            ysb_b = [Buf(after_A), Buf(after_A)]
            h1b_b = [Buf(after_A), Buf(after_A)]
            ysq_b = [Buf(after_m2), Buf(after_m2)]
            sqv7_b = [Buf(after_W), Buf(after_W)]
            rstd7_b = [Buf(after_W), Buf(after_W)]
            YSB = [BIGA, BIGA + 16384]
            H1B = [BIGA + 32768, BIGA + 49152]
            YSQ = [BIGB + 45056, BIGB + 45056 + 8192]
            SQV7 = [WBF, WBF + 2048]
            RSTD7 = [WBF + 4096, WBF + 6144]
            out_toks = {}

            def p7_A(j, part):
                t0, w = trng(j)
                p = j % 2
                hd = V(HD[p], BF16, 22 * w)
                h1b = V(H1B[p], F32, 8 * w)
                ysb = V(YSB[p], F32, 8 * w)
                ysq = V(YSQ[p], BF16, 8 * w)
                if part == 0:
                    P.dma("sp", hd, hid_scr[j, :, 0:22 * w], reads=[hids_b[i][j] for i in range(22)], writes=[hd_b[p]])
                    P.dma("sp", h1b, h1_scr[j, :, 0:8 * w], reads=[h1s_b[j]], writes=[h1b_b[p]])
                for fo in range(4 * part, 4 * part + 4):
                    bk = fo % 4
                    P.mm(PS(bk, w), [(wdn(fo, k), hd[:, k * w:(k + 1) * w]) for k in range(22)], reads=[hd_b[p], wdn_b[fo]], writes=[pbuf[bk]])
                    P.op("dve", lambda e, fo=fo, bk=bk: e.tensor_copy(out=ysb[:, fo * w:(fo + 1) * w], in_=PS(bk, w)),
                         reads=[pbuf[bk]], writes=[ysb_b[p]])
                    P.op("act", lambda e, fo=fo: e.activation(out=ysq[:, fo * w:(fo + 1) * w], in_=ysb[:, fo * w:(fo + 1) * w], func=AF.Square),
                         reads=[ysb_b[p]], writes=[ysq_b[p]])

            def p7_B(j):
                t0, w = trng(j)
                p = j % 2
                h1b = V(H1B[p], F32, 8 * w)
                ysb = V(YSB[p], F32, 8 * w)
                ysq = V(YSQ[p], BF16, 8 * w)
                sqv = V(SQV7[p], F32, w)
                rstd = V(RSTD7[p], F32, w)
                norm_stats(lambda c: ysq[:, c * w:(c + 1) * w], 8, w, 6 + p, ysq_b[p], 1.0 / 1024, sqv, sqv7_b[p], rstd, rstd7_b[p])
                for fo in range(8):
                    P.op("dve", lambda e, fo=fo: e.scalar_tensor_tensor(
                        out=ysb[:, fo * w:(fo + 1) * w], in0=ysb[:, fo * w:(fo + 1) * w], scalar=sc(C_GFPOST + fo), in1=rstd, op0=ALU.mult, op1=ALU.mult),
                        reads=[ysb_b[p], rstd7_b[p], small_b], writes=[ysb_b[p]])
                for fo in range(8):
                    P.op("pool", lambda e, fo=fo: e.tensor_tensor(out=h1b[:, fo * w:(fo + 1) * w], in0=h1b[:, fo * w:(fo + 1) * w], in1=ysb[:, fo * w:(fo + 1) * w], op=ALU.add),
                         reads=[h1b_b[p], ysb_b[p]], writes=[h1b_b[p]])
                h1b3 = h1b.rearrange("p (c t) -> p c t", c=8)
                if j == 0:
                    tok = P.dma("pool", outT_v[:, :, 0:496], h1b3[:, :, 16:512], reads=[h1b_b[p]])
                else:
                    tok = P.dma("pool", outT_v[:, :, t0 - 16:t0 - 16 + w], h1b3, reads=[h1b_b[p]])
                out_toks[tok[0]] = max(out_toks.get(tok[0], 0), tok[1])

            for j in range(NJ + 1):
                if j < NJ:
                    p7_A(j, 0)
                if j >= 1:
                    p7_B(j - 1)
                if j < NJ:
                    p7_A(j, 1)
            P.final_wait("sp", out_toks)

        try:
            body()
        except StopBuild:
            P.final_all()

        @block.sync
        def _(e):
            P.replay("sp", e)

        @block.gpsimd
        def _(e):
            P.replay("pool", e)

        @block.tensor
        def _(e):
            P.replay("pe", e)

        @block.scalar
        def _(e):
            P.replay("act", e)

        @block.vector
        def _(e):
            P.replay("dve", e)
    return nc


def host_consts():
    ident = np.eye(128, dtype=np.float32)
    pos = np.arange(NT)
    augq = np.zeros((8, 4, NT), np.float32)
    augk = np.zeros((8, 4, NT), np.float32)
    for h in range(8):
        slope = 2.0 ** (-(h + 1))
        augq[h, 0] = -slope * ((pos >> 8) << 8)
        augq[h, 1] = -slope * (pos & 255)
        augq[h, 2] = 1.0
        augq[h, 3] = 1.0
        augk[h, 0] = 1.0
        augk[h, 1] = 1.0
        augk[h, 2] = slope * ((pos >> 7) << 7)
        augk[h, 3] = slope * (pos & 127)
    ki = np.arange(128)[:, None]
    qi = np.arange(512)[None, :]
    masks = np.zeros((128, 4, 512), np.float32)
    for r in range(4):
        masks[:, r, :] = np.where(128 * r + ki <= qi, 0.0, -30000.0)
    return ident, augq.astype(ml_dtypes.bfloat16), augk.astype(ml_dtypes.bfloat16), masks.reshape(128, 2048)


def pack_small(inp):
    s = np.zeros((128, NSMALL), np.float32)

    def g(v):
        return np.asarray(v, np.float32).reshape(8, 128).T

    s[:, C_G1:C_G1 + 8] = g(inp["norm_mix_pre"][0])
    s[:, C_GPOST:C_GPOST + 8] = g(inp["norm_mix_post"][0])
    s[:, C_GFPRE:C_GFPRE + 8] = g(inp["norm_ffn_pre"][0])
    s[:, C_GFPOST:C_GFPOST + 8] = g(inp["norm_ffn_post"][0])
    s[:, C_CW:C_CW + 24] = np.asarray(inp["conv_w"][0], np.float32).reshape(3, 8, 128).transpose(2, 0, 1).reshape(128, 24)
    s[:, C_FCW:C_FCW + 132] = np.asarray(inp["ffn_conv_w"][0], np.float32).reshape(3, 44, 128).transpose(2, 0, 1).reshape(128, 132)
    s[:, C_FCB:C_FCB + 44] = np.asarray(inp["ffn_conv_b"][0], np.float32).reshape(44, 128).T
    s[:, C_SUBG] = np.asarray(inp["subln_g"][0], np.float32)
    for col, k in ((C_LQ1, "lambda_q1"), (C_LK1, "lambda_k1"), (C_LQ2, "lambda_q2"), (C_LK2, "lambda_k2")):
        s[:, col:col + 64] = np.asarray(inp[k][0], np.float32)[None, :]
    return s


_NC = None


def make_in_maps(inputs):
    x = np.asarray(inputs["x"], np.float32)
    meta = np.asarray(inputs["meta_tokens"], np.float32)
    ident, augq, augk, masks = host_consts()
    small = pack_small(inputs)
    shared = {
        "w_in": np.ascontiguousarray(np.asarray(inputs["w_in"], np.float32)[0]),
        "w_conv_out": np.ascontiguousarray(np.asarray(inputs["w_conv_out"], np.float32)[0]),
        "w_attn_out": np.ascontiguousarray(np.asarray(inputs["w_attn_out"], np.float32)[0]),
        "w_mix_out": np.ascontiguousarray(np.asarray(inputs["w_mix_out"], np.float32)[0]),
        "w_ffn_up": np.ascontiguousarray(np.asarray(inputs["w_ffn_up"], np.float32)[0]),
        "w_ffn_down": np.ascontiguousarray(np.asarray(inputs["w_ffn_down"], np.float32)[0]),
        "small": small, "ident": ident, "augq": augq, "augk": augk, "masks": masks,
    }
    in_maps = []
    for b in range(x.shape[0]):
        h0 = np.concatenate([meta, x[b]], axis=0)
        d = dict(shared)
        d["hT0"] = np.ascontiguousarray(h0.T)
        in_maps.append(d)
    return in_maps


def kernel(**inputs):
    global _NC
    in_maps = make_in_maps(inputs)
    if _NC is None:
        _NC = build_nc()
    res = run_bass_kernel_spmd(_NC, in_maps, core_ids=list(range(len(in_maps))))
    out = np.stack([np.ascontiguousarray(np.asarray(r["outT"], np.float32).T) for r in res.results])
    return out.astype(np.float32)
```
